# Optimizing a Trainium2 kernel written in Bass

```python
import math
import jax
import jax.numpy as jnp
from jax import lax
import numpy as np

D_MODEL = 1024
BATCH = 4
SEQ = 8192
DEPTH = 2
DEC_BATCH = 8
DEC_SEQ = 8192
PAST_LEN = 128

MIX_WIDTH = D_MODEL
A_WIDTH = MIX_WIDTH // 2
A_HEAD_DIM = 64
A_HEADS = A_WIDTH // A_HEAD_DIM
DECAY_LORA = 32
ICLR_LORA = 32
VRES_LORA = 32
GATE_LORA = 96
B_WIDTH = MIX_WIDTH - A_WIDTH
B_HEAD_DIM = 64
B_HEADS = B_WIDTH // (2 * B_HEAD_DIM)
Q_BLOCK = 128
N_BUCKETS = 32
MAX_DISTANCE = 128
D_FF = 2816
LN_EPS = 1e-5
GN_EPS = 64e-5
ALPHA = (2 * DEPTH) ** 0.25
BETA = (8 * DEPTH) ** -0.25

OFF_R = 0
OFF_K = OFF_R + A_WIDTH
OFF_V = OFF_K + A_WIDTH
OFF_WD = OFF_V + A_WIDTH
OFF_AD = OFF_WD + 2 * DECAY_LORA
OFF_GD = OFF_AD + 2 * ICLR_LORA
A_COLS = OFF_GD + GATE_LORA
B_QK = B_HEADS * 2 * B_HEAD_DIM
B_COLS = 3 * B_QK
P_COLS = A_COLS + B_COLS

kernel_name = 'hybrid_rwkv7_diffattn_encoder'


def _layer_norm(x, g, b):
    xf = x.astype(jnp.float32)
    mu = jnp.mean(xf, -1, keepdims=True)
    var = jnp.mean(jnp.square(xf - mu), -1, keepdims=True)
    return ((xf - mu) * lax.rsqrt(var + LN_EPS) * g + b).astype(x.dtype)


def _shift_prev(x):
    return jnp.pad(x[:, :-1], ((0, 0), (1, 0), (0, 0)))


def _shift_next(x):
    return jnp.pad(x[:, 1:], ((0, 0), (0, 1), (0, 0)))


def _heads(t):
    return t.reshape(t.shape[:-1] + (A_HEADS, A_HEAD_DIM))


def _t5_bucket(rel):
    nb = N_BUCKETS // 2
    max_exact = nb // 2
    ret = jnp.where(rel > 0, nb, 0)
    n = jnp.abs(rel)
    nf = jnp.maximum(n, 1).astype(jnp.float32)
    large = max_exact + (jnp.log(nf / max_exact) / math.log(MAX_DISTANCE / max_exact)
                         * (nb - max_exact)).astype(jnp.int32)
    large = jnp.minimum(large, nb - 1)
    return ret + jnp.where(n < max_exact, n, large)


def _rwkv7_scan(r, w, k, v, a_, b, reverse):
    bsz, _, h, n = r.shape

    def step(S, inp):
        r_t, w_t, k_t, v_t, a_t, b_t = inp
        sa = jnp.einsum('bhij,bhj->bhi', S, a_t)
        S = S * w_t[:, :, None, :] + sa[..., None] * b_t[:, :, None, :] + v_t[..., None] * k_t[:, :, None, :]
        return S, jnp.einsum('bhij,bhj->bhi', S, r_t)

    S0 = jnp.zeros((bsz, h, n, n), jnp.float32)
    xs = tuple(jnp.moveaxis(t, 1, 0) for t in (r, w, k, v, a_, b))
    _, ys = lax.scan(step, S0, xs, reverse=reverse)
    return jnp.moveaxis(ys, 0, 1)


def _rwkv7_group(pa, l, v_first, prm):
    bsz, T, _ = pa.shape
    mu = prm['shift_mu'][l].astype(jnp.float32)
    pa = pa + mu[0] * (_shift_prev(pa) - pa) + mu[1] * (_shift_next(pa) - pa)
    r = pa[..., OFF_R:OFF_K]
    k = pa[..., OFF_K:OFF_V]
    v = pa[..., OFF_V:OFF_WD]
    wd = pa[..., OFF_WD:OFF_AD].reshape(bsz, T, 2, DECAY_LORA)
    ad = pa[..., OFF_AD:OFF_GD].reshape(bsz, T, 2, ICLR_LORA)
    gd = pa[..., OFF_GD:A_COLS]
    w_log = -jax.nn.softplus(-(prm['w0'][l] + jnp.einsum('btdr,drc->btdc', jnp.tanh(wd), prm['decay_up'][l]))) - 0.5
    decay = jnp.exp(-jnp.exp(w_log))
    a = jax.nn.sigmoid(prm['a0'][l] + jnp.einsum('btdr,drc->btdc', ad, prm['iclr_up'][l]))
    if v_first is None:
        v_first = v
    else:
        i = l - 1
        gate_v = jax.nn.sigmoid(prm['vres_0'][i] + (v @ prm['vres_down'][i]) @ prm['vres_up'][i])
        v = v + (v_first - v) * gate_v
    kk = _heads(k * prm['k_k'][l])
    kk = kk * lax.rsqrt(jnp.sum(kk * kk, -1, keepdims=True) + 1e-12)
    k_dir = k[:, :, None, :] * (1.0 + (a - 1.0) * prm['k_a'][l])
    rh, vh = _heads(r), _heads(v)
    kf, kb = _heads(k_dir[:, :, 0]), _heads(k_dir[:, :, 1])
    af, ab = _heads(a[:, :, 0]), _heads(a[:, :, 1])
    y = (_rwkv7_scan(rh, _heads(decay[:, :, 0]), kf, vh, -kk, kk * af, False)
         + _rwkv7_scan(rh, _heads(decay[:, :, 1]), kb, vh, -kk, kk * ab, True))
    ym = jnp.mean(y, -1, keepdims=True)
    yv = jnp.mean(jnp.square(y - ym), -1, keepdims=True)
    y = ((y - ym) * lax.rsqrt(yv + GN_EPS) * prm['gn_g'][l].reshape(A_HEADS, A_HEAD_DIM)
         + prm['gn_b'][l].reshape(A_HEADS, A_HEAD_DIM))
    bonus = jnp.sum(rh * (kf + kb) * prm['r_k'][l], -1, keepdims=True) * vh
    g = jax.nn.sigmoid(gd) @ prm['g_up'][l]
    return (y + bonus).reshape(bsz, T, A_WIDTH) * g, v_first


def _diff_group(pb, l, prm):
    bsz, T, _ = pb.shape
    q = pb[..., :B_QK].reshape(bsz, T, B_HEADS, 2, B_HEAD_DIM)
    k = pb[..., B_QK:2 * B_QK].reshape(bsz, T, B_HEADS, 2, B_HEAD_DIM)
    v = pb[..., 2 * B_QK:].reshape(bsz, T, B_HEADS, 2 * B_HEAD_DIM)
    lam_init = 0.8 - 0.6 * math.exp(-0.3 * l)
    lp = prm['lam'][l].astype(jnp.float32)
    lam = jnp.exp(jnp.sum(lp[0] * lp[1])) - jnp.exp(jnp.sum(lp[2] * lp[3])) + lam_init
    rel_bias = prm['rel_bias'].astype(jnp.float32)
    nblk = T // Q_BLOCK
    qb = q.reshape(bsz, nblk, Q_BLOCK, B_HEADS, 2, B_HEAD_DIM).transpose(1, 0, 2, 3, 4, 5)
    kpos = jnp.arange(T)
    scale = B_HEAD_DIM ** -0.5

    def block(args):
        i, q_blk = args
        qpos = i * Q_BLOCK + jnp.arange(Q_BLOCK)
        bias = rel_bias[_t5_bucket(kpos[None, :] - qpos[:, None])].transpose(2, 0, 1)
        s = jnp.einsum('bqhcd,bkhcd->bhcqk', q_blk, k).astype(jnp.float32) * scale + bias[None, :, None]
        p = jax.nn.softmax(s, axis=-1)
        attn = p[:, :, 0] - lam * p[:, :, 1]
        return jnp.einsum('bhqk,bkhe->bqhe', attn.astype(v.dtype), v)

    o = lax.map(block, (jnp.arange(nblk), qb))
    o = o.transpose(1, 0, 2, 3, 4).reshape(bsz, T, B_HEADS, 2 * B_HEAD_DIM).astype(jnp.float32)
    o = o * lax.rsqrt(jnp.mean(o * o, -1, keepdims=True) + LN_EPS) * prm['subln_g'][l] * (1.0 - lam_init)
    return o.reshape(bsz, T, B_WIDTH)


def _conv_ffn(x, l, prm):
    h = x @ prm['ffn_up'][l]
    cw = prm['conv_w'][l]
    h = cw[0] * _shift_prev(h) + cw[1] * h + cw[2] * _shift_next(h) + prm['conv_b'][l]
    gate, up = jnp.split(h, 2, axis=-1)
    return (jax.nn.gelu(gate, approximate=False) * up) @ prm['ffn_down'][l]


def _trunk(x, prm):
    x = _layer_norm(x, prm['ln_in_g'], prm['ln_in_b'])
    v_first = None
    for l in range(DEPTH):
        proj = x @ prm['w_in'][l]
        out_a, v_first = _rwkv7_group(proj[..., :A_COLS].astype(jnp.float32), l, v_first, prm)
        out_b = _diff_group(proj[..., A_COLS:], l, prm)
        mixed = jnp.concatenate([out_a.astype(x.dtype), out_b.astype(x.dtype)], -1) @ prm['w_out'][l]
        x = _layer_norm(ALPHA * x + mixed, prm['ln1_g'][l], prm['ln1_b'][l])
        x = _layer_norm(ALPHA * x + _conv_ffn(x, l, prm), prm['ln2_g'][l], prm['ln2_b'][l])
    return x


def setup_inputs(seed: int = 0) -> dict:
    key = jax.random.key(seed)
    ks = iter(jax.random.split(key, 40))
    f32 = jnp.float32
    L = DEPTH

    def nrm(shape, s):
        return jax.random.normal(next(ks), shape, f32) * s

    def uni(shape, lo, hi):
        return jax.random.uniform(next(ks), shape, f32, lo, hi)

    return {
        'x_prompt': nrm((BATCH, SEQ, D_MODEL), 1.0),
        'x_sample': nrm((DEC_BATCH, DEC_SEQ, D_MODEL), 1.0),
        'ln_in_g': 1.0 + nrm((D_MODEL,), 0.05),
        'ln_in_b': nrm((D_MODEL,), 0.02),
        'w_in': nrm((L, D_MODEL, P_COLS), D_MODEL ** -0.5),
        'shift_mu': uni((L, 2, A_COLS), 0.0, 0.5),
        'w0': uni((L, 2, A_WIDTH), -7.0, 0.0),
        'decay_up': nrm((L, 2, DECAY_LORA, A_WIDTH), 0.1),
        'a0': nrm((L, 2, A_WIDTH), 0.5),
        'iclr_up': nrm((L, 2, ICLR_LORA, A_WIDTH), 0.1),
        'g_up': nrm((L, GATE_LORA, A_WIDTH), GATE_LORA ** -0.5),
        'k_k': 0.85 + nrm((L, A_WIDTH), 0.05),
        'k_a': 1.0 + nrm((L, A_WIDTH), 0.05),
        'r_k': nrm((L, A_HEADS, A_HEAD_DIM), 0.1),
        'gn_g': 1.0 + nrm((L, A_WIDTH), 0.05),
        'gn_b': nrm((L, A_WIDTH), 0.02),
        'vres_down': nrm((L - 1, A_WIDTH, VRES_LORA), A_WIDTH ** -0.5),
        'vres_up': nrm((L - 1, VRES_LORA, A_WIDTH), 0.2),
        'vres_0': 1.0 + nrm((L - 1, A_WIDTH), 0.1),
        'lam': nrm((L, 4, B_HEAD_DIM), 0.1),
        'subln_g': 1.0 + nrm((L, 2 * B_HEAD_DIM), 0.05),
        'rel_bias': nrm((N_BUCKETS, B_HEADS), 0.5),
        'w_out': nrm((L, MIX_WIDTH, D_MODEL), MIX_WIDTH ** -0.5 * BETA),
        'ln1_g': 1.0 + nrm((L, D_MODEL), 0.05),
        'ln1_b': nrm((L, D_MODEL), 0.02),
        'ffn_up': nrm((L, D_MODEL, 2 * D_FF), D_MODEL ** -0.5),
        'conv_w': jnp.array([0.0, 1.0, 0.0], f32)[None, :, None] + nrm((L, 3, 2 * D_FF), 0.3),
        'conv_b': nrm((L, 2 * D_FF), 0.02),
        'ffn_down': nrm((L, D_FF, D_MODEL), D_FF ** -0.5 * BETA),
        'ln2_g': 1.0 + nrm((L, D_MODEL), 0.05),
        'ln2_b': nrm((L, D_MODEL), 0.02),
    }


def reference(x_prompt, x_sample, ln_in_g, ln_in_b, w_in, shift_mu, w0, decay_up, a0, iclr_up, g_up,
              k_k, k_a, r_k, gn_g, gn_b, vres_down, vres_up, vres_0, lam, subln_g, rel_bias, w_out,
              ln1_g, ln1_b, ffn_up, conv_w, conv_b, ffn_down, ln2_g, ln2_b):
    prm = dict(ln_in_g=ln_in_g, ln_in_b=ln_in_b, w_in=w_in, shift_mu=shift_mu, w0=w0, decay_up=decay_up,
               a0=a0, iclr_up=iclr_up, g_up=g_up, k_k=k_k, k_a=k_a, r_k=r_k, gn_g=gn_g, gn_b=gn_b,
               vres_down=vres_down, vres_up=vres_up, vres_0=vres_0, lam=lam, subln_g=subln_g,
               rel_bias=rel_bias, w_out=w_out, ln1_g=ln1_g, ln1_b=ln1_b, ffn_up=ffn_up, conv_w=conv_w,
               conv_b=conv_b, ffn_down=ffn_down, ln2_g=ln2_g, ln2_b=ln2_b)
    y_prompt = _trunk(x_prompt, prm)
    y_sample = _trunk(x_sample, prm)
    return (y_prompt, y_sample)
```

```python
import contextlib
import os
import math
import numpy as np
import concourse.bass as bass
import concourse.mybir as mybir
from concourse.bass_utils import run_bass_kernel_spmd

F32 = mybir.dt.float32
BF16 = mybir.dt.bfloat16
AF = mybir.ActivationFunctionType
ALU = mybir.AluOpType

D = 1024
DEPTH = 2
AW = 512
A_COLS = 1760
P_COLS = 3296
D_FF = 2816
LN_EPS = 1e-5
GN_EPS = 64e-5
ALPHA = (2 * DEPTH) ** 0.25
NB_LEN = 1279
CEXP = math.exp(-0.5)


class StopBuild(Exception):
    pass


def _stage(n):
    if int(os.environ.get('RWKV_STOP', '99')) == n:
        raise StopBuild()


class Dep:
    __slots__ = ("w", "r")

    def __init__(self):
        self.w = {}
        self.r = {}


class V:
    __slots__ = ("ap", "d")

    def __init__(self, ap, d):
        self.ap = ap
        self.d = d


class Tile:
    def __init__(self, t, d=None):
        self.t = t
        self.d = d if d is not None else Dep()

    def __getitem__(self, idx):
        return V(self.t[idx], self.d)


class Sched:
    ENG = ("pe", "act", "dve", "pool", "sp")

    def __init__(self, nc, n_dma=28, n_sw=10):
        self.nc = nc
        self.eng = {"pe": nc.tensor, "act": nc.scalar, "dve": nc.vector, "pool": nc.gpsimd, "sp": nc.sync}
        self.sem = {e: nc.semaphore("s_" + e).__enter__() for e in self.ENG}
        self.cnt = {e: 0 for e in self.ENG}
        self.dsem = [nc.semaphore("d%d" % i).__enter__() for i in range(n_dma)]
        self.dcnt = [0] * n_dma
        self.dnext = 0
        self.n_sw = n_sw
        self.swnext = 0
        self.waited = {e: {} for e in self.ENG}
        self.uid = 0

    def _s(self, k):
        return self.sem[k] if isinstance(k, str) else self.dsem[k]

    def op(self, eng, fn, reads=(), writes=(), dma=False):
        need = {}
        for d in reads:
            if d is None:
                continue
            for k, v in d.w.items():
                if need.get(k, 0) < v:
                    need[k] = v
        for d in writes:
            if d is None:
                continue
            for k, v in d.w.items():
                if k == eng and not dma:
                    continue
                if need.get(k, 0) < v:
                    need[k] = v
            for k, v in d.r.items():
                if k == eng and not dma:
                    continue
                if need.get(k, 0) < v:
                    need[k] = v
        if dma:
            if eng == "pool":
                j = self.swnext
                self.swnext = (j + 1) % self.n_sw
            else:
                j = self.n_sw + self.dnext
                self.dnext = (self.dnext + 1) % (len(self.dsem) - self.n_sw)
            if self.dcnt[j] > 0 and need.get(j, 0) < self.dcnt[j]:
                need[j] = self.dcnt[j]
            self.dcnt[j] += 16
            tk = (j, self.dcnt[j])
        else:
            self.cnt[eng] += 1
            tk = (eng, self.cnt[eng])
        E = self.eng[eng]
        wd = self.waited[eng]
        for k, v in need.items():
            if k == "pe" and eng == "pe" and not dma:
                continue
            if wd.get(k, 0) >= v:
                continue
            wd[k] = v
            E.wait_ge(self._s(k), v)
        ins = fn(E)
        ins.then_inc(self._s(tk[0]), 16 if dma else 1)
        for d in reads:
            if d is not None:
                d.r[tk[0]] = tk[1]
        for d in writes:
            if d is not None:
                d.w[tk[0]] = tk[1]
                d.r = {}
        return tk

    def barrier(self):
        for e in self.ENG:
            E = self.eng[e]
            wd = self.waited[e]
            for k in self.ENG:
                if k != e and self.cnt[k] > wd.get(k, 0):
                    wd[k] = self.cnt[k]
                    E.wait_ge(self.sem[k], self.cnt[k])
            for j in range(len(self.dsem)):
                if self.dcnt[j] > wd.get(j, 0):
                    wd[j] = self.dcnt[j]
                    E.wait_ge(self.dsem[j], self.dcnt[j])

    def dma(self, out, in_, q="sp", **kw):
        return self.op(q, lambda e: e.dma_start(out=out.ap, in_=in_.ap, **kw), [in_.d], [out.d], dma=True)

    def mm(self, out, lhsT, rhs, start=True, stop=True, extra_reads=()):
        return self.op("pe", lambda e: e.matmul(out.ap, lhsT=lhsT.ap, rhs=rhs.ap, start=start, stop=stop,
                                                skip_group_check=True),
                       [lhsT.d, rhs.d] + list(extra_reads), [out.d])

    def tr(self, out, in_, ident):
        return self.op("pe", lambda e: e.transpose(out.ap, in_.ap, ident.ap), [in_.d, ident.d], [out.d])

    def act(self, out, in_, func, bias=None, scale=None, accum=None, eng="act"):
        kw = {}
        rd = [in_.d]
        wr = [out.d]
        if bias is not None:
            if isinstance(bias, V):
                kw["bias"] = bias.ap
                rd.append(bias.d)
            else:
                kw["bias"] = bias
        if scale is not None:
            if isinstance(scale, V):
                kw["scale"] = scale.ap
                rd.append(scale.d)
            else:
                kw["scale"] = scale
        if accum is not None:
            kw["accum_out"] = accum.ap
            wr.append(accum.d)
        return self.op("act", lambda e: e.activation(out=out.ap, in_=in_.ap, func=func, **kw), rd, wr)

    def tt(self, out, a, b, op, eng="dve"):
        return self.op(eng, lambda e: e.tensor_tensor(out=out.ap, in0=a.ap, in1=b.ap, op=op), [a.d, b.d], [out.d])

    def ts(self, out, a, s1, s2, op0, op1=None, eng="dve"):
        rd = [a.d]
        x1, x2 = s1, s2
        if isinstance(s1, V):
            rd.append(s1.d)
            x1 = s1.ap
        if isinstance(s2, V):
            rd.append(s2.d)
            x2 = s2.ap
        if op1 is None:
            return self.op(eng, lambda e: e.tensor_scalar(out=out.ap, in0=a.ap, scalar1=x1, scalar2=None, op0=op0),
                           rd, [out.d])
        return self.op(eng, lambda e: e.tensor_scalar(out=out.ap, in0=a.ap, scalar1=x1, scalar2=x2, op0=op0, op1=op1),
                       rd, [out.d])

    def stt(self, out, a, s, b, op0, op1):
        rd = [a.d, b.d]
        x = s
        if isinstance(s, V):
            rd.append(s.d)
            x = s.ap
        return self.op("dve", lambda e: e.scalar_tensor_tensor(out=out.ap, in0=a.ap, scalar=x, in1=b.ap, op0=op0, op1=op1),
                       rd, [out.d])

    def copy(self, out, in_, eng="dve"):
        if eng == "act":
            return self.act(out, in_, AF.Copy)
        return self.op(eng, lambda e: e.tensor_copy(out=out.ap, in_=in_.ap), [in_.d], [out.d])

    def memset(self, out, val, eng="dve"):
        return self.op(eng, lambda e: e.memset(out.ap, val), [], [out.d])


class Ctx:
    def __init__(self, nc, S):
        self.nc = nc
        self.S = S
        self.es = contextlib.ExitStack()
        self.n = 0

    def sb(self, shape, dt, name="t"):
        self.S.uid += 1
        return Tile(self.es.enter_context(self.nc.sbuf_tensor("%s_%d" % (name, self.S.uid), list(shape), dt)))

    def ps(self, shape, dt=F32, name="p"):
        self.S.uid += 1
        return Tile(self.es.enter_context(self.nc.psum_tensor("%s_%d" % (name, self.S.uid), list(shape), dt)))

    def close(self):
        self.S.barrier()
        self.es.close()


class Ring:
    def __init__(self, tiles):
        self.tiles = tiles
        self.i = 0

    def next(self):
        t = self.tiles[self.i % len(self.tiles)]
        self.i += 1
        return t


def bcast_rows(ap1d, n, parts=128):
    return bass.AP(tensor=ap1d.tensor, offset=ap1d.offset, ap=[[0, parts], [1, n]])


def build(T, NSEQ, dbg=False, plan=None):
    nc = bass.Bass("TRN2", target_bir_lowering=False)
    S = Sched(nc)
    NT5 = T // 512
    NCH = T // 128

    def din(name, shape, dt=F32):
        return nc.dram_tensor(name, list(shape), dt, kind="ExternalInput").ap()

    def dscr(name, shape, dt=F32):
        kind = "ExternalOutput" if dbg else "Internal"
        return Tile(nc.dram_tensor(name, list(shape), dt, kind=kind).ap())

    x_in = din("x", [NSEQ, T, D])
    y_out = Tile(nc.dram_tensor("y", [NSEQ, T, D], F32, kind="ExternalOutput").ap())
    P = {}
    for name, shape in [("ln_in_g", [D]), ("ln_in_b", [D]), ("w_in", [2, D, P_COLS]), ("shift_mu", [2, 2, A_COLS]),
                        ("w0", [2, 2, AW]), ("decay_up", [2, 2, 32, AW]), ("a0", [2, 2, AW]), ("iclr_up", [2, 2, 32, AW]),
                        ("g_up", [2, 96, AW]), ("k_k", [2, AW]), ("k_a", [2, AW]), ("r_k", [2, AW]), ("gn_g", [2, AW]),
                        ("gn_b", [2, AW]), ("vres_down", [1, AW, 32]), ("vres_up", [1, 32, AW]), ("vres_0", [1, AW]),
                        ("lam", [2, 4, 64]), ("subln_g", [2, 128]), ("rel_bias", [32, 4]), ("w_out", [2, D, D]),
                        ("ln1_g", [2, D]), ("ln1_b", [2, D]), ("ffn_up", [2, D, 2 * D_FF]), ("conv_w", [2, 3, 2 * D_FF]),
                        ("conv_b", [2, 2 * D_FF]), ("ffn_down", [2, D_FF, D]), ("ln2_g", [2, D]), ("ln2_b", [2, D]),
                        ("c_ident", [128, 128]), ("c_masks", [128, 4, 128]), ("c_blk", [128, 128]),
                        ("c_onehot", [32, NB_LEN + 1])]:
        P[name] = din(name, shape)

    xres = dscr("xres", [T, D])
    pa = dscr("pa", [1792, T])
    qk = dscr("qk", [1024, T], BF16)
    vb = dscr("vb", [T, 512], BF16)
    mix = dscr("mix", [1024, T], BF16)
    x1s = dscr("x1s", [T, D])
    yfs = dscr("yfs", [512, T])
    vfs = dscr("vfs", [512, T])
    gsc = dscr("gsc", [4, NB_LEN + 1])
    hsc = dscr("hsc", [4, 128, 1536])
    upbf = dscr("upbf", [2, 44, 128, 8 * 128], BF16)

    G = Ctx(nc, S)
    ident = G.sb([128, 128], F32, "ident")
    identb = G.sb([128, 128], BF16, "identb")
    masks = G.sb([128, 4, 128], F32, "masks")
    blk = G.sb([128, 128], BF16, "blk")
    epsln = G.sb([128, 1], F32, "epsln")
    S.dma(ident[:, :], V(P["c_ident"], None))
    S.dma(identb[:, :], V(P["c_ident"], None), q="pool")
    S.dma(masks[:, :, :], V(P["c_masks"], None))
    S.dma(blk[:, :], V(P["c_blk"], None), q="pool")
    S.memset(epsln[:, :], LN_EPS)

    def layer_norm_rows(C, xt, g_bc, b_bc, out):
        st = C["st"].next()
        for c in range(2):
            S.op("dve", lambda e, c=c: e.bn_stats(out=st.t[:, c * 6:(c + 1) * 6], in_=xt.ap[:, c * 512:(c + 1) * 512]),
                 [xt.d], [st.d])
        S.op("dve", lambda e: e.bn_aggr(out=st.t[:, 12:14], in_=st.t[:, 0:12]), [st.d], [st.d])
        S.act(st[:, 14:15], st[:, 13:14], AF.Sqrt, bias=epsln[:, 0:1], scale=1.0)
        S.op("dve", lambda e: e.reciprocal(out=st.t[:, 15:16], in_=st.t[:, 14:15]), [st.d], [st.d])
        S.ts(out, xt, st[:, 12:13], st[:, 15:16], ALU.subtract, ALU.mult)
        S.tt(out, out, g_bc, ALU.mult)
        S.tt(out, out, b_bc, ALU.add)

    def load_w_bf16(dst, src_ap, KT, ncols, col0=0):
        v = src_ap.rearrange("(k p) c -> p k c", p=128)
        for k in range(KT):
            S.dma(dst[:, k, :], V(v[:, k, col0:col0 + ncols], None), q="pool")

    def phase_proj(l, s):
        C = Ctx(nc, S)
        w = C.sb([128, 8, P_COLS], BF16, "win")
        load_w_bf16(w, P["w_in"][l], 8, P_COLS)
        g_bc = C.sb([128, D], F32, "gbc")
        b_bc = C.sb([128, D], F32, "bbc")
        if l == 0:
            S.dma(g_bc[:, :], V(bcast_rows(P["ln_in_g"], D), None))
            S.dma(b_bc[:, :], V(bcast_rows(P["ln_in_b"], D), None))
        CC = {"st": Ring([C.sb([128, 16], F32, "st") for _ in range(4)])}
        xtk = Ring([C.sb([128, D], F32, "xtk") for _ in range(6)])
        xT = Ring([C.sb([128, 8, 512], BF16, "xT") for _ in range(2)])
        pst = Ring([C.ps([128, 512], F32, "pst") for _ in range(3)])
        psm = Ring([C.ps([128, 512], F32, "psm") for _ in range(4)])
        stA = Ring([C.sb([128, 512], F32, "stA") for _ in range(4)])
        stB = Ring([C.sb([128, 512], BF16, "stB") for _ in range(4)])
        ev = 0
        for t5 in range(NT5):
            t0 = t5 * 512
            xt_t = xT.next()
            blocks = []
            for j in range(4):
                xb = xtk.next()
                r0 = t0 + j * 128
                if l == 0:
                    S.dma(xb[:, :], V(x_in[s, r0:r0 + 128, :], None))
                    layer_norm_rows(CC, xb[:, :], g_bc[:, :], b_bc[:, :], xb[:, :])
                    S.dma(xres[r0:r0 + 128, :], xb[:, :], q="sp")
                else:
                    S.dma(xb[:, :], xres[r0:r0 + 128, :])
                blocks.append(xb)
            for k in range(8):
                p = pst.next()
                for j in range(4):
                    S.tr(p[:, j * 128:(j + 1) * 128], blocks[j][:, k * 128:(k + 1) * 128], ident[:, :])
                if k % 2 == 0:
                    S.copy(xt_t[:, k, :], p[:, :], eng="act")
                else:
                    S.copy(xt_t[:, k, :], p[:, :], eng="dve")
            for c in range(22):
                if c < 14:
                    c0 = c * 128
                    m = min(128, A_COLS - c0)
                else:
                    c0 = A_COLS + (c - 14) * 128
                    m = 128
                p = psm.next()
                for k in range(8):
                    S.mm(p[0:m, :], w[:, k, c0:c0 + m], xt_t[:, k, :], start=(k == 0), stop=(k == 7))
                if c < 14:
                    o = stA.next()
                    S.copy(o[0:m, :], p[0:m, :], eng=("act" if ev % 2 == 0 else "dve"))
                    S.dma(pa[c0:c0 + m, t0:t0 + 512], o[0:m, :])
                else:
                    o = stB.next()
                    S.copy(o[:, :], p[:, :], eng=("act" if ev % 2 == 0 else "dve"))
                    S.dma(qk[(c - 14) * 128:(c - 13) * 128, t0:t0 + 512], o[:, :])
                ev += 1
            for j in range(4):
                p = psm.next()
                for k in range(8):
                    S.mm(p[:, :], xt_t[:, k, j * 128:(j + 1) * 128], w[:, k, A_COLS + 1024:A_COLS + 1536],
                         start=(k == 0), stop=(k == 7))
                o = stB.next()
                S.copy(o[:, :], p[:, :], eng=("act" if ev % 2 == 0 else "dve"))
                ev += 1
                S.dma(vb[t0 + j * 128:t0 + (j + 1) * 128, :], o[:, :])
        C.close()

    def load_cols(C, dst, src1d, n, pst, stg):
        S.dma(stg[0:n, :], V(src1d.rearrange("(c p) -> c p", p=128), None))
        S.tr(pst[:, 0:n], stg[0:n, :], ident[0:n, 0:n])
        S.copy(dst, pst[:, 0:n])

    def phase_setup(l_, s_):
        C = Ctx(nc, S)
        for l in range(DEPTH):
            v = P["ffn_up"][l].rearrange("(k p) c -> p k c", p=128)
            for c in range(44):
                stg = None
                S.dma(V(upbf.t[l, c].rearrange("p (k c) -> p k c", k=8), upbf.d), V(v[:, :, c * 128:(c + 1) * 128], None),
                      q="pool")
        rb = C.sb([32, 4], F32, "rb")
        oh = C.sb([32, NB_LEN + 1], F32, "oh")
        gsb = C.sb([4, NB_LEN + 1], F32, "gsb")
        S.dma(rb[:, :], V(P["rel_bias"], None))
        S.dma(oh[:, :], V(P["c_onehot"], None))
        pp = C.ps([128, 512], F32, "pp")
        for c0 in (0, 512, 1024):
            n = min(512, NB_LEN + 1 - c0)
            S.mm(pp[0:4, 0:n], rb[:, :], oh[:, c0:c0 + n])
            S.copy(gsb[:, c0:c0 + n], pp[0:4, 0:n])
        S.dma(gsc[:, :], gsb[:, :])
        for h in range(4):
            dst = bass.AP(tensor=hsc.t.tensor, offset=hsc.t.offset + h * 128 * 1536, ap=[[1537, 128], [1, NB_LEN + 1]])
            src = bass.AP(tensor=gsc.t.tensor, offset=gsc.t.offset + h * (NB_LEN + 1), ap=[[0, 128], [1, NB_LEN + 1]])
            S.dma(V(dst, hsc.d), V(src, gsc.d))
        C.close()

    def phase_attn(l, s):
        C = Ctx(nc, S)
        lam_init = 0.8 - 0.6 * math.exp(-0.3 * l)
        lt = C.sb([1, 4, 64], F32, "lt")
        l2 = C.sb([1, 8], F32, "l2")
        ones1 = C.sb([1, 128], F32, "ones1")
        nlam = C.sb([128, 1], F32, "nlam")
        subg = C.sb([128, 128], F32, "subg")
        S.dma(lt[:, :, :], V(P["lam"][l:l + 1], None))
        S.memset(ones1[:, :], 1.0)
        S.tt(lt[:, 0, :], lt[:, 0, :], lt[:, 1, :], ALU.mult)
        S.tt(lt[:, 2, :], lt[:, 2, :], lt[:, 3, :], ALU.mult)
        S.op("dve", lambda e: e.reduce_sum(out=l2.t[:, 0:1], in_=lt.t[:, 0, :], axis=mybir.AxisListType.X), [lt.d], [l2.d])
        S.op("dve", lambda e: e.reduce_sum(out=l2.t[:, 1:2], in_=lt.t[:, 2, :], axis=mybir.AxisListType.X), [lt.d], [l2.d])
        S.act(l2[:, 2:4], l2[:, 0:2], AF.Exp)
        S.tt(l2[:, 4:5], l2[:, 3:4], l2[:, 2:3], ALU.subtract)
        S.ts(l2[:, 5:6], l2[:, 4:5], -lam_init, None, ALU.add)
        accA = [C.ps([128, 512], F32, "accA") for _ in range(2)]
        accB = [C.ps([128, 512], F32, "accB") for _ in range(2)]
        pl = accB[0]
        S.mm(pl[:, 0:1], ones1[:, :], l2[:, 5:6])
        S.copy(nlam[:, :], pl[:, 0:1])
        S.dma(subg[:, :], V(bcast_rows(P["subln_g"][l], 128), None))
        S.ts(subg[:, :], subg[:, :], 1.0 - lam_init, None, ALU.mult)

        KT = Ring([C.sb([128, T], BF16, "KT") for _ in range(2)])
        QT = Ring([C.sb([128, T], BF16, "QT") for _ in range(2)])
        VA = Ring([C.sb([128, NCH, 130], BF16, "VA") for _ in range(2)])
        BT = Ring([C.sb([128, 6, 512], F32, "BT") for _ in range(2)])
        CB = Ring([C.sb([128, 2], F32, "CB") for _ in range(2)])
        stR = Ring([C.ps([128, 512], F32, "st") for _ in range(3)])
        ptr = C.ps([128, 512], BF16, "ptr")
        ER = Ring([C.sb([128, 512], BF16, "E") for _ in range(4)])
        TMP = Ring([C.sb([128, 512], F32, "tmp") for _ in range(3)])
        ON = [C.sb([128, 4, 128], F32, "on") for _ in range(2)]
        SM = Ring([C.sb([128, 8], F32, "sm") for _ in range(8)])
        OB = Ring([C.sb([128, 128], BF16, "ob") for _ in range(3)])
        OF = Ring([C.sb([128, 128], F32, "of") for _ in range(3)])
        OST = Ring([C.sb([128, 512], BF16, "ost") for _ in range(2)])
        for h in range(4):
            kt_ = KT.next(); qt_ = QT.next(); va = VA.next(); bt = BT.next(); cb = CB.next()
            S.dma(kt_[:, :], qk[512 + h * 128:512 + (h + 1) * 128, :])
            S.dma(qt_[:, :], qk[h * 128:(h + 1) * 128, :])
            S.dma(va[:, :, 0:128], V(vb.t[:, h * 128:(h + 1) * 128].rearrange("(n p) e -> p n e", p=128), vb.d))
            S.memset(va[:, :, 128:130], 1.0, eng="pool")
            for rho in range(6):
                c_ = 639 - (rho - 1) * 128
                S.dma(bt[:, rho, :], V(hsc.t[h, :, c_:c_ + 512], hsc.d))
            S.dma(cb[:, 0:1], V(bass.AP(tensor=gsc.t.tensor, offset=gsc.t.offset + h * (NB_LEN + 1) + 1278, ap=[[0, 128], [1, 1]]), gsc.d))
            S.dma(cb[:, 1:2], V(bass.AP(tensor=gsc.t.tensor, offset=gsc.t.offset + h * (NB_LEN + 1), ap=[[0, 128], [1, 1]]), gsc.d))
            def acc_of(c, qb):
                return (accA[c], qb * 130) if qb < 3 else (accB[c], 0)

            tiles = [(q5, c, kt) for q5 in range(NT5) for c in range(2) for kt in range(NCH)]
            LOOK = 2
            Es = {}
            for idx in range(len(tiles) + LOOK):
                if idx < len(tiles):
                    q5, c, kt = tiles[idx]
                    pb = c * 64
                    st = stR.next()
                    S.mm(st[:, :], kt_[pb:pb + 64, kt * 128:(kt + 1) * 128], qt_[pb:pb + 64, q5 * 512:(q5 + 1) * 512])
                    rho = kt - 4 * q5 + 1
                    E = ER.next()
                    if 0 <= rho <= 5:
                        tmp = TMP.next()
                        S.stt(tmp[:, :], st[:, :], 0.125, bt[:, rho, :], ALU.mult, ALU.add)
                        S.act(E[:, :], tmp[:, :], AF.Exp)
                    else:
                        S.act(E[:, :], st[:, :], AF.Exp, bias=(cb[:, 0:1] if rho < 0 else cb[:, 1:2]), scale=0.125)
                    Es[idx] = E
                j = idx - LOOK
                if j < 0:
                    continue
                q5, c, kt = tiles[j]
                E = Es.pop(j)
                for qb in range(4):
                    a, o = acc_of(c, qb)
                    S.mm(a[:, o:o + 130], E[:, qb * 128:(qb + 1) * 128], va[:, kt, 0:130],
                         start=(kt == 0 and qb in (0, 3)), stop=(kt == NCH - 1))
                if kt != NCH - 1:
                    continue
                for qb in range(4):
                    a, o = acc_of(c, qb)
                    sm = SM.next()
                    S.op("dve", lambda e, sm=sm, a=a, o=o: e.reciprocal(out=sm.t[:, 0:1], in_=a.t[:, o + 128:o + 129]),
                         [a.d], [sm.d])
                    S.ts(ON[c][:, qb, :], a[:, o:o + 128], sm[:, 0:1], None, ALU.mult)
                if c == 0:
                    continue
                ost = OST.next()
                for qb in range(4):
                    of = OF.next(); ob = OB.next(); sm = SM.next()
                    S.stt(of[:, :], ON[1][:, qb, :], nlam[:, 0:1], ON[0][:, qb, :], ALU.mult, ALU.add)
                    S.act(ob[:, :], of[:, :], AF.Square, accum=sm[:, 0:1])
                    S.act(sm[:, 1:2], sm[:, 0:1], AF.Sqrt, bias=epsln[:, 0:1], scale=1.0 / 128.0)
                    S.op("dve", lambda e, sm=sm: e.reciprocal(out=sm.t[:, 2:3], in_=sm.t[:, 1:2]), [sm.d], [sm.d])
                    S.stt(ob[:, :], of[:, :], sm[:, 2:3], subg[:, :], ALU.mult, ALU.mult)
                    S.tr(ptr[:, qb * 128:(qb + 1) * 128], ob[:, :], identb[:, :])
                S.copy(ost[:, :], ptr[:, :])
                S.dma(mix[512 + h * 128:512 + (h + 1) * 128, q5 * 512:(q5 + 1) * 512], ost[:, :])
        C.close()

    def phase_wout(l, s):
        C = Ctx(nc, S)
        w = C.sb([128, 8, D], BF16, "wout")
        load_w_bf16(w, P["w_out"][l], 8, D)
        g_bc = C.sb([128, D], F32, "gbc"); b_bc = C.sb([128, D], F32, "bbc")
        S.dma(g_bc[:, :], V(bcast_rows(P["ln1_g"][l], D), None))
        S.dma(b_bc[:, :], V(bcast_rows(P["ln1_b"][l], D), None))
        CC = {"st": Ring([C.sb([128, 16], F32, "st") for _ in range(4)])}
        MT = Ring([C.sb([128, 8, 512], BF16, "mixT") for _ in range(2)])
        XB = Ring([C.sb([128, D], F32, "xb") for _ in range(4)])
        HB = Ring([C.sb([128, D], F32, "hb") for _ in range(4)])
        PS = Ring([C.ps([128, 1024], F32, "pso") for _ in range(3)])
        for t5 in range(NT5):
            t0 = t5 * 512
            mt = MT.next()
            S.dma(mt[:, :, :], V(mix.t[:, t0:t0 + 512].rearrange("(k p) t -> p k t", p=128), mix.d))
            for j in range(4):
                r0 = t0 + j * 128
                xb = XB.next(); hb = HB.next(); p = PS.next()
                S.dma(xb[:, :], xres[r0:r0 + 128, :])
                for half in range(2):
                    for k in range(8):
                        S.mm(p[:, half * 512:(half + 1) * 512], mt[:, k, j * 128:(j + 1) * 128],
                             w[:, k, half * 512:(half + 1) * 512], start=(k == 0), stop=(k == 7))
                S.stt(hb[:, :], xb[:, :], ALPHA, p[:, :], ALU.mult, ALU.add)
                layer_norm_rows(CC, hb[:, :], g_bc[:, :], b_bc[:, :], hb[:, :])
                S.dma(x1s[r0:r0 + 128, :], hb[:, :])
        C.close()

    def phase_ffn(l, s):
        C = Ctx(nc, S)
        wd = C.sb([128, 22, D], BF16, "wdown")
        load_w_bf16(wd, P["ffn_down"][l], 22, D)
        g_bc = C.sb([128, D], F32, "gbc"); b_bc = C.sb([128, D], F32, "bbc")
        S.dma(g_bc[:, :], V(bcast_rows(P["ln2_g"][l], D), None))
        S.dma(b_bc[:, :], V(bcast_rows(P["ln2_b"][l], D), None))
        cw = C.sb([128, 4, 44], F32, "cw")
        stg = C.sb([64, 128], F32, "stg")
        pst = Ring([C.ps([128, 512], F32, "pst") for _ in range(2)])
        for j in range(3):
            load_cols(C, cw[:, j, :], P["conv_w"][l, j], 44, pst.next(), stg)
        load_cols(C, cw[:, 3, :], P["conv_b"][l], 44, pst.next(), stg)
        CC = {"st": Ring([C.sb([128, 16], F32, "st") for _ in range(4)])}
        XB = Ring([C.sb([128, D], F32, "xb") for _ in range(8)])
        XH = Ring([C.sb([2, D], F32, "xh") for _ in range(2)])
        XT = Ring([C.sb([128, 8, 512], BF16, "xT") for _ in range(2)])
        XHT = Ring([C.sb([128, 8, 2], BF16, "xhT") for _ in range(2)])
        WC = Ring([C.sb([128, 8 * 128], BF16, "wc") for _ in range(4)])
        PH = Ring([C.ps([128, 512], F32, "ph") for _ in range(2)])
        PHH = C.ps([128, 512], F32, "phh")
        PD = Ring([C.ps([128, 1024], F32, "pd") for _ in range(1)])
        HR = Ring([C.sb([128, 514], F32, "hr") for _ in range(3)])
        OO = Ring([C.sb([128, 512], F32, "oo") for _ in range(3)])
        GA = C.sb([128, 22, 512], BF16, "ga")
        GT = C.sb([128, 22, 512], BF16, "gt")
        HB = Ring([C.sb([128, D], F32, "hb") for _ in range(3)])
        for t5 in range(NT5):
            t0 = t5 * 512
            xt = XT.next(); xh = XH.next(); xht = XHT.next()
            blocks = []
            for j in range(4):
                xb = XB.next()
                S.dma(xb[:, :], x1s[t0 + j * 128:t0 + (j + 1) * 128, :])
                blocks.append(xb)
            S.memset(xh[:, :], 0.0, eng="pool")
            if t0 > 0:
                S.dma(xh[0:1, :], x1s[t0 - 1:t0, :])
            if t0 + 512 < T:
                S.dma(xh[1:2, :], x1s[t0 + 512:t0 + 513, :])
            for k in range(8):
                p = pst.next()
                for j in range(4):
                    S.tr(p[:, j * 128:(j + 1) * 128], blocks[j][:, k * 128:(k + 1) * 128], ident[:, :])
                S.copy(xt[:, k, :], p[:, :], eng=("act" if k % 2 == 0 else "dve"))
            p = pst.next()
            for k in range(8):
                S.tr(p[:, 2 * k:2 * k + 2], xh[0:2, k * 128:(k + 1) * 128], ident[0:2, 0:2])
            S.copy(xht[:, :, :], V(p.t[:, 0:16].rearrange("p (k t) -> p k t", t=2), p.d))
            for c in range(44):
                wc = WC.next()
                S.dma(wc[:, :], upbf[l, c])
                ph = PH.next()
                for k in range(8):
                    S.mm(ph[:, :], wc[:, k * 128:(k + 1) * 128], xt[:, k, :], start=(k == 0), stop=(k == 7))
                for k in range(8):
                    S.mm(PHH[:, 0:2], wc[:, k * 128:(k + 1) * 128], xht[:, k, :], start=(k == 0), stop=(k == 7))
                hr = HR.next(); oo = OO.next()
                S.copy(hr[:, 1:513], ph[:, :], eng="act")
                S.copy(hr[:, 0:1], PHH[:, 0:1], eng="dve")
                S.copy(hr[:, 513:514], PHH[:, 1:2], eng="dve")
                S.act(oo[:, :], hr[:, 1:513], AF.Identity, bias=cw[:, 3, c:c + 1], scale=cw[:, 1, c:c + 1])
                S.stt(oo[:, :], hr[:, 0:512], cw[:, 0, c:c + 1], oo[:, :], ALU.mult, ALU.add)
                if c < 22:
                    S.stt(oo[:, :], hr[:, 2:514], cw[:, 2, c:c + 1], oo[:, :], ALU.mult, ALU.add)
                    S.act(GA[:, c, :], oo[:, :], AF.Gelu)
                else:
                    S.stt(oo[:, :], hr[:, 2:514], cw[:, 2, c:c + 1], oo[:, :], ALU.mult, ALU.add)
                    S.tt(GT[:, c - 22, :], oo[:, :], GA[:, c - 22, :], ALU.mult)
            for j in range(4):
                r0 = t0 + j * 128
                p = PD.next(); hb = HB.next()
                for half in range(2):
                    for k in range(22):
                        S.mm(p[:, half * 512:(half + 1) * 512], GT[:, k, j * 128:(j + 1) * 128],
                             wd[:, k, half * 512:(half + 1) * 512], start=(k == 0), stop=(k == 21))
                S.stt(hb[:, :], blocks[j][:, :], ALPHA, p[:, :], ALU.mult, ALU.add)
                layer_norm_rows(CC, hb[:, :], g_bc[:, :], b_bc[:, :], hb[:, :])
                if l == DEPTH - 1:
                    S.dma(V(y_out.t[s, r0:r0 + 128, :], y_out.d), hb[:, :])
                else:
                    S.dma(xres[r0:r0 + 128, :], hb[:, :])
        C.close()

    def RV(v, pattern, **kw):
        return V(v.ap.rearrange(pattern, **kw), v.d)

    def phase_rwkv(l, s):
        C = Ctx(nc, S)
        pstc = C.ps([128, 512], F32, "pstc")
        stg = C.sb([64, 128], F32, "stg")
        mu = C.sb([128, 3, 14], F32, "mu")
        for j in range(2):
            S.memset(stg[0:14, :], 0.0)
            S.dma(stg[0:13, :], V(P["shift_mu"][l, j, 0:1664].rearrange("(c p) -> c p", p=128), None))
            S.dma(stg[13:14, 0:96], V(P["shift_mu"][l, j, 1664:1760].rearrange("(c p) -> c p", p=96), None))
            S.tr(pstc[:, 0:14], stg[0:14, :], ident[0:14, 0:14])
            S.copy(mu[:, 1 + j, :], pstc[:, 0:14])
        S.tt(mu[:, 0, :], mu[:, 1, :], mu[:, 2, :], ALU.add)
        S.ts(mu[:, 0, :], mu[:, 0, :], -1.0, 1.0, ALU.mult, ALU.add)
        PRM = C.sb([128, 12, 4], F32, "prm")
        srcs = [P["w0"][l, 0], P["w0"][l, 1], P["a0"][l, 0], P["a0"][l, 1], P["k_k"][l], P["k_a"][l], P["r_k"][l],
                P["gn_g"][l], P["gn_b"][l], P["vres_0"][0]]
        for i, sap in enumerate(srcs):
            load_cols(C, PRM[:, i, :], sap, 4, pstc, stg)
        S.ts(PRM[:, 10, :], PRM[:, 5, :], -1.0, 1.0, ALU.mult, ALU.add)
        S.ts(PRM[:, 11, :], PRM[:, 5, :], -2.0, 2.0, ALU.mult, ALU.add)
        dup = C.sb([64, AW], BF16, "dup")
        iup = [C.sb([128, AW], BF16, "iup") for _ in range(2)]
        gup = C.sb([96, AW], BF16, "gup")
        vdn = C.sb([128, 4, 32], BF16, "vdn")
        vup = C.sb([32, AW], BF16, "vup")
        S.dma(dup[:, :], V(P["decay_up"][l].rearrange("d r c -> (d r) c"), None), q="pool")
        for dd in range(2):
            S.memset(iup[dd][64:128, :], 0.0)
            S.dma(iup[dd][64 + 32 * dd:96 + 32 * dd, :], V(P["iclr_up"][l, dd], None), q="pool")
        S.dma(gup[:, :], V(P["g_up"][l], None), q="pool")
        S.dma(vdn[:, :, :], V(P["vres_down"][0].rearrange("(f p) r -> p f r", p=128), None), q="pool")
        S.dma(vup[:, :], V(P["vres_up"][0], None), q="pool")
        blkm = C.sb([128, 128], BF16, "blkm")
        S.ts(blkm[:, :], blk[:, :], 1.0 / 64.0, None, ALU.mult)
        idblk = C.sb([128, 64], F32, "idblk")
        S.tt(idblk[:, :], ident[:, 0:64], ident[:, 64:128], ALU.add)
        ones = C.sb([128, 128], F32, "ones")
        S.memset(ones[:, :], 1.0)
        eps12 = C.sb([128, 1], F32, "eps12")
        S.memset(eps12[:, :], 1e-12)
        epsgn = C.sb([128, 1], F32, "epsgn")
        S.memset(epsgn[:, :], GN_EPS)
        MS2 = C.sb([128, 2, 128], F32, "ms2")
        MI2 = C.sb([128, 2, 128], F32, "mi2")
        MT2 = C.sb([128, 2, 128], F32, "mt2")

        _stage(1)
        XIN = Ring([C.sb([128, 14, 130], F32, "xin") for _ in range(2)])
        SH = [[C.sb([128, 128], F32, "sh") for _ in range(14)] for _ in range(2)]
        RA = Ring([C.ps([128, 512], F32, "ra") for _ in range(6)])
        PTB = C.ps([128, 512], BF16, "ptb")
        TWt = C.sb([64, 128], BF16, "tw")
        ADt = C.sb([128, 128], BF16, "ad")
        SGt = C.sb([96, 128], BF16, "sg")
        VD = C.sb([32, 128], BF16, "vd")

        def fset(n, dt, shape=(128, 128), name="f"):
            return [[C.sb(list(shape), dt, name) for _ in range(4)] for _ in range(n)]
        (Ee, Cu, La, Gm, Gi, Gp, Ge, K0, Kk, Bb, Kd, T1, Yy, Dl, Yn, T2) = fset(16, F32)
        A0 = [fset(1, F32)[0] for _ in range(2)]
        A1 = [fset(1, F32)[0] for _ in range(2)]
        Vv = [fset(1, F32)[0] for _ in range(2)]
        SMs = [[C.sb([128, 8], F32, "sms") for _ in range(4)] for _ in range(2)]
        (SQ, BH, KH, VB16, RH, YB, RKV, OA) = fset(8, BF16)
        AT = [fset(1, BF16)[0] for _ in range(2)]
        VT = [fset(1, BF16)[0] for _ in range(2)]
        AR = [[C.sb([128, 256], BF16, "ar") for _ in range(4)] for _ in range(2)]
        BK = [[C.sb([128, 256], BF16, "bk") for _ in range(4)] for _ in range(2)]
        DI = [[C.sb([128, 2, 192], BF16, "di") for _ in range(4)] for _ in range(2)]
        DX = [[C.sb([128, 2, 192], BF16, "dx") for _ in range(4)] for _ in range(2)]
        GQa = [[C.sb([128, 2, 192], BF16, "gqa") for _ in range(4)] for _ in range(2)]
        GQ = [C.sb([128, 2, 192], BF16, "gq") for _ in range(4)]
        MTAK = [C.sb([128, 2, 128], BF16, "mtak") for _ in range(4)]
        PW = [[C.sb([128, 2, 2, 128], BF16, "pw") for _ in range(4)] for _ in range(3)]
        PM = [C.sb([128, 64], BF16, "pm") for _ in range(4)]
        ST = [[C.sb([128, 64], BF16, "st") for _ in range(4)] for _ in range(2)]
        VF = [C.sb([128, 128], F32, "vf") for _ in range(4)]
        YF = [C.sb([128, 128], F32, "yf") for _ in range(4)]

        def prep(d, ci, par):
            c0 = ci * 128
            xin = XIN.next()
            sh = SH[par]
            lo = max(c0 - 1, 0)
            hi = min(c0 + 129, T)
            if c0 == 0:
                S.memset(xin[:, :, 0:1], 0.0, eng="pool")
            if c0 + 129 > T:
                S.memset(xin[:, :, 129:130], 0.0, eng="pool")
            o0 = lo - (c0 - 1)
            S.dma(xin[:, :, o0:o0 + (hi - lo)], V(pa.t[:, lo:hi].rearrange("(r p) t -> p r t", p=128), pa.d))
            for rt in range(14):
                S.act(sh[rt][:, :], xin[:, rt, 1:129], AF.Copy, scale=mu[:, 0, rt:rt + 1])
                S.stt(sh[rt][:, :], xin[:, rt, 0:128], mu[:, 1, rt:rt + 1], sh[rt][:, :], ALU.mult, ALU.add)
                S.stt(sh[rt][:, :], xin[:, rt, 2:130], mu[:, 2, rt:rt + 1], sh[rt][:, :], ALU.mult, ALU.add)
                if rt % 2 == 1:
                    yield
            r_ = sh[0:4]; k_ = sh[4:8]; v_ = sh[8:12]
            S.act(TWt[32 * d:32 * d + 32, :], sh[12][32 * d:32 * d + 32, :], AF.Tanh)
            S.copy(ADt[64:128, :], sh[12][64:128, :], eng="pool")
            for ft in range(4):
                fs = slice(ft * 128, (ft + 1) * 128)
                pz = RA.next()
                S.mm(pz[:, 0:128], dup[32 * d:32 * d + 32, fs], TWt[32 * d:32 * d + 32, :])
                S.act(Ee[ft][:, :], pz[:, 0:128], AF.Sigmoid, bias=PRM[:, d, ft:ft + 1], scale=1.0)
                S.op("dve", lambda e, ft=ft: e.tensor_tensor_scan(out=Cu[ft].t[:, :], data0=ones.t[:, :], data1=Ee[ft].t[:, :],
                                                                   initial=0.0, op0=ALU.mult, op1=ALU.add),
                     [ones.d, Ee[ft].d], [Cu[ft].d])
                sm = SMs[par][ft]
                if d == 0:
                    lam_t = Cu[ft]
                else:
                    S.stt(La[ft][:, :], Cu[ft][:, :], -1.0, Ee[ft][:, :], ALU.mult, ALU.add)
                    S.ts(La[ft][:, :], La[ft][:, :], Cu[ft][:, 127:128], None, ALU.add)
                    lam_t = La[ft]
                S.ts(sm[:, 0:1], Cu[ft][:, 127:128], -CEXP, None, ALU.mult)
                S.act(sm[:, 1:2], Cu[ft][:, 127:128], AF.Exp, scale=-CEXP)
                S.act(Gm[ft][:, :], lam_t[:, :], AF.Exp, scale=-CEXP)
                S.act(Gi[ft][:, :], lam_t[:, :], AF.Exp, scale=CEXP)
                S.tt(Gp[ft][:, :], lam_t[:, :], Ee[ft][:, :], ALU.subtract, eng="pool")
                S.act(Gp[ft][:, :], Gp[ft][:, :], AF.Exp, scale=-CEXP)
                S.act(Ge[ft][:, :], lam_t[:, :], AF.Exp, bias=sm[:, 0:1], scale=CEXP)
                for dd in ((0,) if d == 0 else (0, 1)):
                    pq = RA.next()
                    S.mm(pq[:, 0:128], iup[dd][64:128, fs], ADt[64:128, :])
                    S.act((A0 if dd == 0 else A1)[par][ft][:, :], pq[:, 0:128], AF.Sigmoid, bias=PRM[:, 2 + dd, ft:ft + 1], scale=1.0)
                yield
            for ft in range(4):
                S.copy(Vv[par][ft][:, :], v_[ft][:, :], eng="pool")
            if l == 1:
                pvd = RA.next()
                for ft in range(4):
                    S.copy(VB16[ft][:, :], Vv[par][ft][:, :], eng="pool")
                    S.mm(pvd[0:32, 0:128], vdn[:, ft, :], VB16[ft][:, :], start=(ft == 0), stop=(ft == 3))
                S.copy(VD[:, :], pvd[0:32, 0:128], eng="act")
                yield
                for ft in range(4):
                    fs = slice(ft * 128, (ft + 1) * 128)
                    pg = RA.next()
                    S.dma(VF[ft][:, :], vfs[ft * 128:(ft + 1) * 128, c0:c0 + 128])
                    S.mm(pg[:, 0:128], vup[:, fs], VD[:, :])
                    S.act(T1[ft][:, :], pg[:, 0:128], AF.Sigmoid, bias=PRM[:, 9, ft:ft + 1], scale=1.0)
                    S.tt(Dl[ft][:, :], VF[ft][:, :], Vv[par][ft][:, :], ALU.subtract, eng="pool")
                    S.tt(Dl[ft][:, :], Dl[ft][:, :], T1[ft][:, :], ALU.mult, eng="pool")
                    S.tt(Vv[par][ft][:, :], Vv[par][ft][:, :], Dl[ft][:, :], ALU.add, eng="pool")
                    yield
            elif d == 0:
                for ft in range(4):
                    S.dma(vfs[ft * 128:(ft + 1) * 128, c0:c0 + 128], Vv[par][ft][:, :])
            for ft in range(4):
                S.ts(K0[ft][:, :], k_[ft][:, :], PRM[:, 4, ft:ft + 1], None, ALU.mult)
                S.act(SQ[ft][:, :], K0[ft][:, :], AF.Square)
                pss = RA.next()
                S.mm(pss[:, 0:128], blk[:, :], SQ[ft][:, :])
                S.act(Kk[ft][:, :], pss[:, 0:128], AF.Sqrt, bias=eps12[:, 0:1], scale=1.0)
                S.op("dve", lambda e, ft=ft: e.reciprocal(out=Kk[ft].t[:, :], in_=Kk[ft].t[:, :]), [Kk[ft].d], [Kk[ft].d])
                S.tt(Kk[ft][:, :], Kk[ft][:, :], K0[ft][:, :], ALU.mult)
                Ad = (A0 if d == 0 else A1)[par][ft]
                S.stt(AR[par][ft][:, 0:128], Kk[ft][:, :], -1.0, Gp[ft][:, :], ALU.mult, ALU.mult)
                S.tt(AR[par][ft][:, 128:256], r_[ft][:, :], Gm[ft][:, :], ALU.mult, eng="pool")
                S.tt(Bb[ft][:, :], Kk[ft][:, :], Ad[:, :], ALU.mult)
                S.tt(BK[par][ft][:, 0:128], Bb[ft][:, :], Gi[ft][:, :], ALU.mult, eng="pool")
                S.tt(BH[ft][:, :], Bb[ft][:, :], Ge[ft][:, :], ALU.mult)
                S.ts(T1[ft][:, :], Ad[:, :], PRM[:, 5, ft:ft + 1], PRM[:, 10, ft:ft + 1], ALU.mult, ALU.add)
                S.tt(Kd[ft][:, :], k_[ft][:, :], T1[ft][:, :], ALU.mult, eng="pool")
                S.tt(BK[par][ft][:, 128:256], Kd[ft][:, :], Gi[ft][:, :], ALU.mult, eng="pool")
                S.tt(KH[ft][:, :], Kd[ft][:, :], Ge[ft][:, :], ALU.mult)
                S.copy(VB16[ft][:, :], Vv[par][ft][:, :], eng="pool")
                yield
            for ft in range(4):
                S.tr(PTB[:, 0:128], AR[par][ft][:, 0:128], identb[:, :])
                S.tr(PTB[:, 128:256], BH[ft][:, :], identb[:, :])
                S.tr(PTB[:, 256:384], KH[ft][:, :], identb[:, :])
                S.tr(PTB[:, 384:512], VB16[ft][:, :], identb[:, :])
                S.copy(AT[par][ft][:, :], PTB[:, 0:128], eng="dve")
                for h in range(2):
                    S.copy(DI[par][ft][:, h, 128:192], PTB[:, 128 + h * 64:192 + h * 64], eng="dve")
                for h in range(2):
                    S.copy(GQa[par][ft][:, h, 128:192], PTB[:, 256 + h * 64:320 + h * 64], eng="dve")
                S.copy(VT[par][ft][:, :], PTB[:, 384:512], eng="dve")
                yield

        def mats(d, ci, par, sti):
            c0 = ci * 128
            sh = SH[par]
            r_ = sh[0:4]; k_ = sh[4:8]
            for ft in range(4):
                for h in range(2):
                    hb = h * 64
                    pA = RA.next()
                    S.mm(pA[:, 0:256], BK[par][ft][hb:hb + 64, 0:128], AR[par][ft][hb:hb + 64, :])
                    S.tt(PW[0][ft][:, h, 0, :], pA[:, 0:128], MS2[:, h, :], ALU.mult)
                    S.tt(DI[par][ft][:, h, 0:128], pA[:, 128:256], MI2[:, h, :], ALU.mult)
                    pB = RA.next()
                    S.mm(pB[:, 0:128], BK[par][ft][hb:hb + 64, 128:256], AR[par][ft][hb:hb + 64, 128:256])
                    S.tt(GQa[par][ft][:, h, 0:128], pB[:, 0:128], MI2[:, h, :], ALU.mult)
                    pC = RA.next()
                    S.mm(pC[:, 0:256], AR[par][ft][hb:hb + 64, 0:128], BK[par][ft][hb:hb + 64, :])
                    S.tt(PW[0][ft][:, h, 1, :], pC[:, 0:128], MT2[:, h, :], ALU.mult)
                    S.tt(MTAK[ft][:, h, :], pC[:, 128:256], MT2[:, h, :], ALU.mult)
                yield
            Dc = [DI[par][ft] for ft in range(4)]
            for j in range(7):
                pwc = PW[j % 3]
                pwn = PW[(j + 1) % 3]
                for ft in range(4):
                    if j < 6:
                        pP = RA.next()
                        for h in range(2):
                            S.mm(pP[:, h * 256:h * 256 + 128], pwc[ft][:, h, 1, :], pwc[ft][:, h, 0, :])
                            S.mm(pP[:, h * 256 + 128:h * 256 + 256], pwc[ft][:, h, 0, :], pwc[ft][:, h, 1, :])
                        S.copy(RV(pwn[ft][:, :, :, :], "p h t c -> p (h t c)"), pP[:, :], eng="act")
                    pD = RA.next()
                    for h in range(2):
                        S.mm(pD[:, h * 192:(h + 1) * 192], identb[:, :], Dc[ft][:, h, :], start=True, stop=False)
                        S.mm(pD[:, h * 192:(h + 1) * 192], pwc[ft][:, h, 1, :], Dc[ft][:, h, :], start=False, stop=True)
                    Dn = DX[j % 2][ft]
                    S.copy(RV(Dn[:, :, :], "p h c -> p (h c)"), pD[:, 0:384], eng="dve")
                    Dc[ft] = Dn
                    yield
            stc = ST[sti % 2]
            stn = ST[(sti + 1) % 2]
            for ft in range(4):
                DF = Dc[ft]
                sm = SMs[par][ft]
                pR = RA.next()
                for h in range(2):
                    hb = h * 64
                    S.mm(pR[hb:hb + 64, 0:192], AT[par][ft][:, hb:hb + 64], DF[:, h, :])
                S.tt(RH[ft][:, :], pR[:, 0:128], AR[par][ft][:, 128:256], ALU.add)
                S.stt(PM[ft][:, :], idblk[:, :], sm[:, 1:2], pR[:, 128:192], ALU.mult, ALU.add)
                pG = RA.next()
                for h in range(2):
                    S.mm(pG[:, h * 192:(h + 1) * 192], MTAK[ft][:, h, :], DF[:, h, :])
                S.tt(RV(GQ[ft][:, :, :], "p h c -> p (h c)"), pG[:, 0:384], RV(GQa[par][ft][:, :, :], "p h c -> p (h c)"), ALU.add)
                yield
            for ft in range(4):
                if d == 1:
                    S.dma(YF[ft][:, :], yfs[ft * 128:(ft + 1) * 128, c0:c0 + 128])
                for h in range(2):
                    hb = h * 64
                    pY = RA.next()
                    pS = RA.next()
                    S.mm(pY[hb:hb + 64, 0:128], stc[ft][hb:hb + 64, :], RH[ft][hb:hb + 64, :], start=True, stop=False)
                    S.mm(pY[hb:hb + 64, 0:128], VT[par][ft][:, hb:hb + 64], GQ[ft][:, h, 0:128], start=False, stop=True)
                    S.mm(pS[hb:hb + 64, 0:64], PM[ft][hb:hb + 64, :], stc[ft][hb:hb + 64, :], start=True, stop=False)
                    S.mm(pS[hb:hb + 64, 0:64], GQ[ft][:, h, 128:192], VT[par][ft][:, hb:hb + 64], start=False, stop=True)
                    S.copy(stn[ft][hb:hb + 64, :], pS[hb:hb + 64, 0:64], eng="act")
                    if d == 0:
                        S.copy(Yy[ft][hb:hb + 64, :], pY[hb:hb + 64, 0:128], eng="dve")
                    else:
                        S.tt(Yy[ft][hb:hb + 64, :], pY[hb:hb + 64, 0:128], YF[ft][hb:hb + 64, :], ALU.add)
                yield
                if d == 0:
                    S.dma(yfs[ft * 128:(ft + 1) * 128, c0:c0 + 128], Yy[ft][:, :])
                    continue
                fs = slice(ft * 128, (ft + 1) * 128)
                S.copy(YB[ft][:, :], Yy[ft][:, :], eng="pool")
                pm_ = RA.next()
                S.mm(pm_[:, 0:128], blkm[:, :], YB[ft][:, :])
                S.tt(Dl[ft][:, :], Yy[ft][:, :], pm_[:, 0:128], ALU.subtract)
                S.act(SQ[ft][:, :], Dl[ft][:, :], AF.Square)
                pv_ = RA.next()
                S.mm(pv_[:, 0:128], blkm[:, :], SQ[ft][:, :])
                S.act(Yn[ft][:, :], pv_[:, 0:128], AF.Sqrt, bias=epsgn[:, 0:1], scale=1.0)
                S.op("dve", lambda e, ft=ft: e.reciprocal(out=Yn[ft].t[:, :], in_=Yn[ft].t[:, :]), [Yn[ft].d], [Yn[ft].d])
                S.tt(Yn[ft][:, :], Yn[ft][:, :], Dl[ft][:, :], ALU.mult)
                S.ts(Yn[ft][:, :], Yn[ft][:, :], PRM[:, 7, ft:ft + 1], PRM[:, 8, ft:ft + 1], ALU.mult, ALU.add)
                S.tt(T2[ft][:, :], A0[par][ft][:, :], A1[par][ft][:, :], ALU.add, eng="pool")
                S.ts(T2[ft][:, :], T2[ft][:, :], PRM[:, 5, ft:ft + 1], PRM[:, 11, ft:ft + 1], ALU.mult, ALU.add)
                S.tt(T2[ft][:, :], T2[ft][:, :], k_[ft][:, :], ALU.mult, eng="pool")
                S.stt(RKV[ft][:, :], r_[ft][:, :], PRM[:, 6, ft:ft + 1], T2[ft][:, :], ALU.mult, ALU.mult)
                pb_ = RA.next()
                S.mm(pb_[:, 0:128], blk[:, :], RKV[ft][:, :])
                S.tt(T2[ft][:, :], pb_[:, 0:128], Vv[par][ft][:, :], ALU.mult)
                S.tt(Yn[ft][:, :], Yn[ft][:, :], T2[ft][:, :], ALU.add)
                if ft == 0:
                    S.act(SGt[:, :], sh[13][0:96, :], AF.Sigmoid)
                pg_ = RA.next()
                S.mm(pg_[:, 0:128], gup[:, fs], SGt[:, :])
                S.tt(OA[ft][:, :], Yn[ft][:, :], pg_[:, 0:128], ALU.mult)
                S.dma(mix[ft * 128:(ft + 1) * 128, c0:c0 + 128], OA[ft][:, :])
                yield

        def run_both(a, b):
            alive = [g for g in (a, b) if g is not None]
            while alive:
                for g in list(alive):
                    try:
                        next(g)
                    except StopIteration:
                        alive.remove(g)

        for d in range(2):
            iLs, iLi, iLt = (0, 1, 2) if d == 0 else (2, 3, 0)
            for h in range(2):
                S.copy(MS2[:, h, :], masks[:, iLs, :])
                S.copy(MI2[:, h, :], masks[:, iLi, :])
                S.copy(MT2[:, h, :], masks[:, iLt, :])
            for ft in range(4):
                S.memset(ST[0][ft][:, :], 0.0)
            order = list(range(NCH)) if d == 0 else list(range(NCH - 1, -1, -1))
            gm = None
            for n, ci in enumerate(order):
                run_both(prep(d, ci, n % 2), gm)
                gm = mats(d, ci, n % 2, n)
            run_both(gm, None)
            S.barrier()
        C.close()

    PH = {"proj": phase_proj, "setup": phase_setup, "attn": phase_attn, "wout": phase_wout, "ffn": phase_ffn,
          "rwkv": phase_rwkv}
    if plan is None:
        plan = [("setup", 0, 0)]
        for s in range(NSEQ):
            for l in range(DEPTH):
                plan += [("proj", l, s), ("rwkv", l, s), ("attn", l, s), ("wout", l, s), ("ffn", l, s)]
    for (name, l, s) in plan:
        try:
            PH[name](l, s)
        except StopBuild:
            print('STOPPED')
    S.barrier()
    return nc


def _t5_bucket_np(rel):
    nb = 16
    max_exact = 8
    ret = np.where(rel > 0, nb, 0)
    n = np.abs(rel)
    nf = np.maximum(n, 1).astype(np.float32)
    large = max_exact + (np.log(nf / np.float32(max_exact)) / np.float32(math.log(128 / max_exact))
                         * np.float32(nb - max_exact)).astype(np.int32)
    large = np.minimum(large, nb - 1)
    return ret + np.where(n < max_exact, n, large)


def _consts():
    ident = np.eye(128, dtype=np.float32)
    a = np.arange(128)[:, None]
    b = np.arange(128)[None, :]
    masks = np.stack([(a < b), (a <= b), (a > b), (a >= b)], 1).astype(np.float32)
    blk = np.zeros((128, 128), np.float32)
    blk[:64, :64] = 1
    blk[64:, 64:] = 1
    delta = 639 - np.arange(NB_LEN + 1)
    bk = _t5_bucket_np(delta.astype(np.int32))
    oh = np.zeros((32, NB_LEN + 1), np.float32)
    oh[bk, np.arange(NB_LEN + 1)] = 1
    return dict(c_ident=ident, c_masks=np.ascontiguousarray(masks), c_blk=blk, c_onehot=oh)


_NC_CACHE = {}


def kernel(**inputs):
    xp = np.asarray(inputs["x_prompt"], np.float32)
    xs = np.asarray(inputs["x_sample"], np.float32)
    T = xp.shape[1]
    allx = [xp[i] for i in range(xp.shape[0])] + [xs[i] for i in range(xs.shape[0])]
    n = len(allx)
    NC = 8
    slots = [(c, 8 + (c % 4)) for c in range(NC)]
    key = (T, 2)
    if key not in _NC_CACHE:
        _NC_CACHE[key] = build(T, 2)
    nc = _NC_CACHE[key]
    base = {k: np.ascontiguousarray(np.asarray(v, np.float32)) for k, v in inputs.items()
            if k not in ("x_prompt", "x_sample")}
    base["r_k"] = base["r_k"].reshape(2, AW)
    base.update(_consts())
    in_maps = []
    for c in range(NC):
        m = dict(base)
        m["x"] = np.ascontiguousarray(np.stack([allx[slots[c][0]], allx[slots[c][1]]], 0))
        in_maps.append(m)
    res = run_bass_kernel_spmd(nc, in_maps, core_ids=list(range(NC)))
    outs = [None] * n
    for c in range(NC):
        y = res.results[c]["y"]
        outs[slots[c][0]] = y[0]
        if c < 4:
            outs[slots[c][1]] = y[1]
    yp = np.stack(outs[:xp.shape[0]], 0).astype(np.float32)
    ys = np.stack(outs[xp.shape[0]:], 0).astype(np.float32)
    return (yp, ys)
```

```python
import contextlib
import os
import math
import numpy as np
import concourse.bass as bass
import concourse.mybir as mybir
from concourse.bass_utils import run_bass_kernel_spmd

F32 = mybir.dt.float32
BF16 = mybir.dt.bfloat16
AF = mybir.ActivationFunctionType
ALU = mybir.AluOpType

D = 1024
DEPTH = 2
AW = 512
A_COLS = 1760
P_COLS = 3296
D_FF = 2816
LN_EPS = 1e-5
GN_EPS = 64e-5
ALPHA = (2 * DEPTH) ** 0.25
NB_LEN = 1279
CEXP = math.exp(-0.5)


class StopBuild(Exception):
    pass


def _stage(n):
    if int(os.environ.get('RWKV_STOP', '99')) == n:
        raise StopBuild()


class Dep:
    __slots__ = ("w", "r")

    def __init__(self):
        self.w = {}
        self.r = {}


class V:
    __slots__ = ("ap", "d")

    def __init__(self, ap, d):
        self.ap = ap
        self.d = d


class Tile:
    def __init__(self, t, d=None):
        self.t = t
        self.d = d if d is not None else Dep()

    def __getitem__(self, idx):
        return V(self.t[idx], self.d)


class Sched:
    ENG = ("pe", "act", "dve", "pool", "sp")

    def __init__(self, nc, n_dma=28, n_sw=10):
        self.nc = nc
        self.eng = {"pe": nc.tensor, "act": nc.scalar, "dve": nc.vector, "pool": nc.gpsimd, "sp": nc.sync}
        self.sem = {e: nc.semaphore("s_" + e).__enter__() for e in self.ENG}
        self.cnt = {e: 0 for e in self.ENG}
        self.dsem = [nc.semaphore("d%d" % i).__enter__() for i in range(n_dma)]
        self.dcnt = [0] * n_dma
        self.dnext = 0
        self.n_sw = n_sw
        self.swnext = 0
        self.waited = {e: {} for e in self.ENG}
        self.uid = 0

    def _s(self, k):
        return self.sem[k] if isinstance(k, str) else self.dsem[k]

    def op(self, eng, fn, reads=(), writes=(), dma=False):
        need = {}
        for d in reads:
            if d is None:
                continue
            for k, v in d.w.items():
                if need.get(k, 0) < v:
                    need[k] = v
        for d in writes:
            if d is None:
                continue
            for k, v in d.w.items():
                if k == eng and not dma:
                    continue
                if need.get(k, 0) < v:
                    need[k] = v
            for k, v in d.r.items():
                if k == eng and not dma:
                    continue
                if need.get(k, 0) < v:
                    need[k] = v
        if dma:
            if eng == "pool":
                j = self.swnext
                self.swnext = (j + 1) % self.n_sw
            else:
                j = self.n_sw + self.dnext
                self.dnext = (self.dnext + 1) % (len(self.dsem) - self.n_sw)
            if self.dcnt[j] > 0 and need.get(j, 0) < self.dcnt[j]:
                need[j] = self.dcnt[j]
            self.dcnt[j] += 16
            tk = (j, self.dcnt[j])
        else:
            self.cnt[eng] += 1
            tk = (eng, self.cnt[eng])
        E = self.eng[eng]
        wd = self.waited[eng]
        for k, v in need.items():
            if k == "pe" and eng == "pe" and not dma:
                continue
            if wd.get(k, 0) >= v:
                continue
            wd[k] = v
            E.wait_ge(self._s(k), v)
        ins = fn(E)
        ins.then_inc(self._s(tk[0]), 16 if dma else 1)
        for d in reads:
            if d is not None:
                d.r[tk[0]] = tk[1]
        for d in writes:
            if d is not None:
                d.w[tk[0]] = tk[1]
                d.r = {}
        return tk

    def barrier(self):
        for e in self.ENG:
            E = self.eng[e]
            wd = self.waited[e]
            for k in self.ENG:
                if k != e and self.cnt[k] > wd.get(k, 0):
                    wd[k] = self.cnt[k]
                    E.wait_ge(self.sem[k], self.cnt[k])
            for j in range(len(self.dsem)):
                if self.dcnt[j] > wd.get(j, 0):
                    wd[j] = self.dcnt[j]
                    E.wait_ge(self.dsem[j], self.dcnt[j])

    def dma(self, out, in_, q="sp", **kw):
        return self.op(q, lambda e: e.dma_start(out=out.ap, in_=in_.ap, **kw), [in_.d], [out.d], dma=True)

    def mm(self, out, lhsT, rhs, start=True, stop=True, extra_reads=()):
        return self.op("pe", lambda e: e.matmul(out.ap, lhsT=lhsT.ap, rhs=rhs.ap, start=start, stop=stop,
                                                skip_group_check=True),
                       [lhsT.d, rhs.d] + list(extra_reads), [out.d])

    def tr(self, out, in_, ident):
        return self.op("pe", lambda e: e.transpose(out.ap, in_.ap, ident.ap), [in_.d, ident.d], [out.d])

    def act(self, out, in_, func, bias=None, scale=None, accum=None, eng="act"):
        kw = {}
        rd = [in_.d]
        wr = [out.d]
        if bias is not None:
            if isinstance(bias, V):
                kw["bias"] = bias.ap
                rd.append(bias.d)
            else:
                kw["bias"] = bias
        if scale is not None:
            if isinstance(scale, V):
                kw["scale"] = scale.ap
                rd.append(scale.d)
            else:
                kw["scale"] = scale
        if accum is not None:
            kw["accum_out"] = accum.ap
            wr.append(accum.d)
        return self.op("act", lambda e: e.activation(out=out.ap, in_=in_.ap, func=func, **kw), rd, wr)

    def tt(self, out, a, b, op, eng="dve"):
        return self.op(eng, lambda e: e.tensor_tensor(out=out.ap, in0=a.ap, in1=b.ap, op=op), [a.d, b.d], [out.d])

    def ts(self, out, a, s1, s2, op0, op1=None, eng="dve"):
        rd = [a.d]
        x1, x2 = s1, s2
        if isinstance(s1, V):
            rd.append(s1.d)
            x1 = s1.ap
        if isinstance(s2, V):
            rd.append(s2.d)
            x2 = s2.ap
        if op1 is None:
            return self.op(eng, lambda e: e.tensor_scalar(out=out.ap, in0=a.ap, scalar1=x1, scalar2=None, op0=op0),
                           rd, [out.d])
        return self.op(eng, lambda e: e.tensor_scalar(out=out.ap, in0=a.ap, scalar1=x1, scalar2=x2, op0=op0, op1=op1),
                       rd, [out.d])

    def stt(self, out, a, s, b, op0, op1):
        rd = [a.d, b.d]
        x = s
        if isinstance(s, V):
            rd.append(s.d)
            x = s.ap
        return self.op("dve", lambda e: e.scalar_tensor_tensor(out=out.ap, in0=a.ap, scalar=x, in1=b.ap, op0=op0, op1=op1),
                       rd, [out.d])

    def copy(self, out, in_, eng="dve"):
        if eng == "act":
            return self.act(out, in_, AF.Copy)
        return self.op(eng, lambda e: e.tensor_copy(out=out.ap, in_=in_.ap), [in_.d], [out.d])

    def memset(self, out, val, eng="dve"):
        return self.op(eng, lambda e: e.memset(out.ap, val), [], [out.d])


class Ctx:
    def __init__(self, nc, S):
        self.nc = nc
        self.S = S
        self.es = contextlib.ExitStack()
        self.n = 0

    def sb(self, shape, dt, name="t"):
        self.S.uid += 1
        return Tile(self.es.enter_context(self.nc.sbuf_tensor("%s_%d" % (name, self.S.uid), list(shape), dt)))

    def ps(self, shape, dt=F32, name="p"):
        self.S.uid += 1
        return Tile(self.es.enter_context(self.nc.psum_tensor("%s_%d" % (name, self.S.uid), list(shape), dt)))

    def close(self):
        self.S.barrier()
        self.es.close()


class Ring:
    def __init__(self, tiles):
        self.tiles = tiles
        self.i = 0

    def next(self):
        t = self.tiles[self.i % len(self.tiles)]
        self.i += 1
        return t


def bcast_rows(ap1d, n, parts=128):
    return bass.AP(tensor=ap1d.tensor, offset=ap1d.offset, ap=[[0, parts], [1, n]])


def build(T, NSEQ, dbg=False, plan=None):
    nc = bass.Bass("TRN2", target_bir_lowering=False)
    S = Sched(nc)
    NT5 = T // 512
    NCH = T // 128

    def din(name, shape, dt=F32):
        return nc.dram_tensor(name, list(shape), dt, kind="ExternalInput").ap()

    def dscr(name, shape, dt=F32):
        kind = "ExternalOutput" if dbg else "Internal"
        return Tile(nc.dram_tensor(name, list(shape), dt, kind=kind).ap())

    x_in = din("x", [NSEQ, T, D])
    y_out = Tile(nc.dram_tensor("y", [NSEQ, T, D], F32, kind="ExternalOutput").ap())
    P = {}
    for name, shape in [("ln_in_g", [D]), ("ln_in_b", [D]), ("w_in", [2, D, P_COLS]), ("shift_mu", [2, 2, A_COLS]),
                        ("w0", [2, 2, AW]), ("decay_up", [2, 2, 32, AW]), ("a0", [2, 2, AW]), ("iclr_up", [2, 2, 32, AW]),
                        ("g_up", [2, 96, AW]), ("k_k", [2, AW]), ("k_a", [2, AW]), ("r_k", [2, AW]), ("gn_g", [2, AW]),
                        ("gn_b", [2, AW]), ("vres_down", [1, AW, 32]), ("vres_up", [1, 32, AW]), ("vres_0", [1, AW]),
                        ("lam", [2, 4, 64]), ("subln_g", [2, 128]), ("rel_bias", [32, 4]), ("w_out", [2, D, D]),
                        ("ln1_g", [2, D]), ("ln1_b", [2, D]), ("ffn_up", [2, D, 2 * D_FF]), ("conv_w", [2, 3, 2 * D_FF]),
                        ("conv_b", [2, 2 * D_FF]), ("ffn_down", [2, D_FF, D]), ("ln2_g", [2, D]), ("ln2_b", [2, D]),
                        ("c_ident", [128, 128]), ("c_masks", [128, 4, 128]), ("c_blk", [128, 128]),
                        ("c_onehot", [32, NB_LEN + 1])]:
        P[name] = din(name, shape)

    xres = dscr("xres", [T, D])
    pa = dscr("pa", [1792, T])
    qk = dscr("qk", [1024, T], BF16)
    vb = dscr("vb", [T, 512], BF16)
    mix = dscr("mix", [1024, T], BF16)
    x1s = dscr("x1s", [T, D])
    yfs = dscr("yfs", [512, T])
    vfs = dscr("vfs", [512, T])
    gsc = dscr("gsc", [4, NB_LEN + 1])
    hsc = dscr("hsc", [4, 128, 1536])
    upbf = dscr("upbf", [2, 44, 128, 8 * 128], BF16)

    G = Ctx(nc, S)
    ident = G.sb([128, 128], F32, "ident")
    identb = G.sb([128, 128], BF16, "identb")
    masks = G.sb([128, 4, 128], F32, "masks")
    blk = G.sb([128, 128], BF16, "blk")
    epsln = G.sb([128, 1], F32, "epsln")
    S.dma(ident[:, :], V(P["c_ident"], None))
    S.dma(identb[:, :], V(P["c_ident"], None), q="pool")
    S.dma(masks[:, :, :], V(P["c_masks"], None))
    S.dma(blk[:, :], V(P["c_blk"], None), q="pool")
    S.memset(epsln[:, :], LN_EPS)

    def layer_norm_rows(C, xt, g_bc, b_bc, out):
        st = C["st"].next()
        for c in range(2):
            S.op("dve", lambda e, c=c: e.bn_stats(out=st.t[:, c * 6:(c + 1) * 6], in_=xt.ap[:, c * 512:(c + 1) * 512]),
                 [xt.d], [st.d])
        S.op("dve", lambda e: e.bn_aggr(out=st.t[:, 12:14], in_=st.t[:, 0:12]), [st.d], [st.d])
        S.act(st[:, 14:15], st[:, 13:14], AF.Sqrt, bias=epsln[:, 0:1], scale=1.0)
        S.op("dve", lambda e: e.reciprocal(out=st.t[:, 15:16], in_=st.t[:, 14:15]), [st.d], [st.d])
        S.ts(out, xt, st[:, 12:13], st[:, 15:16], ALU.subtract, ALU.mult)
        S.tt(out, out, g_bc, ALU.mult)
        S.tt(out, out, b_bc, ALU.add)

    def load_w_bf16(dst, src_ap, KT, ncols, col0=0):
        v = src_ap.rearrange("(k p) c -> p k c", p=128)
        for k in range(KT):
            S.dma(dst[:, k, :], V(v[:, k, col0:col0 + ncols], None), q="pool")

    def phase_proj(l, s):
        C = Ctx(nc, S)
        w = C.sb([128, 8, P_COLS], BF16, "win")
        load_w_bf16(w, P["w_in"][l], 8, P_COLS)
        g_bc = C.sb([128, D], F32, "gbc")
        b_bc = C.sb([128, D], F32, "bbc")
        if l == 0:
            S.dma(g_bc[:, :], V(bcast_rows(P["ln_in_g"], D), None))
            S.dma(b_bc[:, :], V(bcast_rows(P["ln_in_b"], D), None))
        CC = {"st": Ring([C.sb([128, 16], F32, "st") for _ in range(4)])}
        xtk = Ring([C.sb([128, D], F32, "xtk") for _ in range(6)])
        xT = Ring([C.sb([128, 8, 512], BF16, "xT") for _ in range(2)])
        pst = Ring([C.ps([128, 512], F32, "pst") for _ in range(3)])
        psm = Ring([C.ps([128, 512], F32, "psm") for _ in range(4)])
        stA = Ring([C.sb([128, 512], F32, "stA") for _ in range(4)])
        stB = Ring([C.sb([128, 512], BF16, "stB") for _ in range(4)])
        ev = 0
        for t5 in range(NT5):
            t0 = t5 * 512
            xt_t = xT.next()
            blocks = []
            for j in range(4):
                xb = xtk.next()
                r0 = t0 + j * 128
                if l == 0:
                    S.dma(xb[:, :], V(x_in[s, r0:r0 + 128, :], None))
                    layer_norm_rows(CC, xb[:, :], g_bc[:, :], b_bc[:, :], xb[:, :])
                    S.dma(xres[r0:r0 + 128, :], xb[:, :], q="sp")
                else:
                    S.dma(xb[:, :], xres[r0:r0 + 128, :])
                blocks.append(xb)
            for k in range(8):
                p = pst.next()
                for j in range(4):
                    S.tr(p[:, j * 128:(j + 1) * 128], blocks[j][:, k * 128:(k + 1) * 128], ident[:, :])
                if k % 2 == 0:
                    S.copy(xt_t[:, k, :], p[:, :], eng="act")
                else:
                    S.copy(xt_t[:, k, :], p[:, :], eng="dve")
            for c in range(22):
                if c < 14:
                    c0 = c * 128
                    m = min(128, A_COLS - c0)
                else:
                    c0 = A_COLS + (c - 14) * 128
                    m = 128
                p = psm.next()
                for k in range(8):
                    S.mm(p[0:m, :], w[:, k, c0:c0 + m], xt_t[:, k, :], start=(k == 0), stop=(k == 7))
                if c < 14:
                    o = stA.next()
                    S.copy(o[0:m, :], p[0:m, :], eng=("act" if ev % 2 == 0 else "dve"))
                    S.dma(pa[c0:c0 + m, t0:t0 + 512], o[0:m, :])
                else:
                    o = stB.next()
                    S.copy(o[:, :], p[:, :], eng=("act" if ev % 2 == 0 else "dve"))
                    S.dma(qk[(c - 14) * 128:(c - 13) * 128, t0:t0 + 512], o[:, :])
                ev += 1
            for j in range(4):
                p = psm.next()
                for k in range(8):
                    S.mm(p[:, :], xt_t[:, k, j * 128:(j + 1) * 128], w[:, k, A_COLS + 1024:A_COLS + 1536],
                         start=(k == 0), stop=(k == 7))
                o = stB.next()
                S.copy(o[:, :], p[:, :], eng=("act" if ev % 2 == 0 else "dve"))
                ev += 1
                S.dma(vb[t0 + j * 128:t0 + (j + 1) * 128, :], o[:, :])
        C.close()

    def load_cols(C, dst, src1d, n, pst, stg):
        S.dma(stg[0:n, :], V(src1d.rearrange("(c p) -> c p", p=128), None))
        S.tr(pst[:, 0:n], stg[0:n, :], ident[0:n, 0:n])
        S.copy(dst, pst[:, 0:n])

    def phase_setup(l_, s_):
        C = Ctx(nc, S)
        for l in range(DEPTH):
            v = P["ffn_up"][l].rearrange("(k p) c -> p k c", p=128)
            for c in range(44):
                stg = None
                S.dma(V(upbf.t[l, c].rearrange("p (k c) -> p k c", k=8), upbf.d), V(v[:, :, c * 128:(c + 1) * 128], None),
                      q="pool")
        rb = C.sb([32, 4], F32, "rb")
        oh = C.sb([32, NB_LEN + 1], F32, "oh")
        gsb = C.sb([4, NB_LEN + 1], F32, "gsb")
        S.dma(rb[:, :], V(P["rel_bias"], None))
        S.dma(oh[:, :], V(P["c_onehot"], None))
        pp = C.ps([128, 512], F32, "pp")
        for c0 in (0, 512, 1024):
            n = min(512, NB_LEN + 1 - c0)
            S.mm(pp[0:4, 0:n], rb[:, :], oh[:, c0:c0 + n])
            S.copy(gsb[:, c0:c0 + n], pp[0:4, 0:n])
        S.dma(gsc[:, :], gsb[:, :])
        for h in range(4):
            dst = bass.AP(tensor=hsc.t.tensor, offset=hsc.t.offset + h * 128 * 1536, ap=[[1537, 128], [1, NB_LEN + 1]])
            src = bass.AP(tensor=gsc.t.tensor, offset=gsc.t.offset + h * (NB_LEN + 1), ap=[[0, 128], [1, NB_LEN + 1]])
            S.dma(V(dst, hsc.d), V(src, gsc.d))
        C.close()

    def phase_attn(l, s):
        C = Ctx(nc, S)
        lam_init = 0.8 - 0.6 * math.exp(-0.3 * l)
        lt = C.sb([1, 4, 64], F32, "lt")
        l2 = C.sb([1, 8], F32, "l2")
        ones1 = C.sb([1, 128], F32, "ones1")
        nlam = C.sb([128, 1], F32, "nlam")
        subg = C.sb([128, 128], F32, "subg")
        S.dma(lt[:, :, :], V(P["lam"][l:l + 1], None))
        S.memset(ones1[:, :], 1.0)
        S.tt(lt[:, 0, :], lt[:, 0, :], lt[:, 1, :], ALU.mult)
        S.tt(lt[:, 2, :], lt[:, 2, :], lt[:, 3, :], ALU.mult)
        S.op("dve", lambda e: e.reduce_sum(out=l2.t[:, 0:1], in_=lt.t[:, 0, :], axis=mybir.AxisListType.X), [lt.d], [l2.d])
        S.op("dve", lambda e: e.reduce_sum(out=l2.t[:, 1:2], in_=lt.t[:, 2, :], axis=mybir.AxisListType.X), [lt.d], [l2.d])
        S.act(l2[:, 2:4], l2[:, 0:2], AF.Exp)
        S.tt(l2[:, 4:5], l2[:, 3:4], l2[:, 2:3], ALU.subtract)
        S.ts(l2[:, 5:6], l2[:, 4:5], -lam_init, None, ALU.add)
        accA = [C.ps([128, 512], F32, "accA") for _ in range(2)]
        accB = [C.ps([128, 512], F32, "accB") for _ in range(2)]
        pl = accB[0]
        S.mm(pl[:, 0:1], ones1[:, :], l2[:, 5:6])
        S.copy(nlam[:, :], pl[:, 0:1])
        S.dma(subg[:, :], V(bcast_rows(P["subln_g"][l], 128), None))
        S.ts(subg[:, :], subg[:, :], 1.0 - lam_init, None, ALU.mult)

        KT = Ring([C.sb([128, T], BF16, "KT") for _ in range(2)])
        QT = Ring([[C.sb([128, T], BF16, "QT") for _ in range(2)] for _ in range(2)])
        VA = Ring([C.sb([128, NCH, 130], BF16, "VA") for _ in range(2)])
        BT = Ring([C.sb([128, 6, 512], F32, "BT") for _ in range(2)])
        CB = Ring([C.sb([128, 2], F32, "CB") for _ in range(2)])
        stR = Ring([C.ps([128, 512], F32, "st") for _ in range(3)])
        ptr = C.ps([128, 512], BF16, "ptr")
        ER = Ring([C.sb([128, 512], BF16, "E") for _ in range(4)])
        TMP = Ring([C.sb([128, 512], F32, "tmp") for _ in range(3)])
        ON = [C.sb([128, 4, 128], F32, "on") for _ in range(2)]
        SM = Ring([C.sb([128, 8], F32, "sm") for _ in range(8)])
        OB = Ring([C.sb([128, 128], BF16, "ob") for _ in range(3)])
        OF = Ring([C.sb([128, 128], F32, "of") for _ in range(3)])
        OST = Ring([C.sb([128, 512], BF16, "ost") for _ in range(2)])
        for h in range(4):
            kt_ = KT.next(); qt_ = QT.next(); va = VA.next(); bt = BT.next(); cb = CB.next()
            S.dma(kt_[:, :], qk[512 + h * 128:512 + (h + 1) * 128, :])
            S.dma(qt_[0][0:64, :], qk[h * 128:h * 128 + 64, :])
            S.memset(qt_[0][64:128, :], 0.0, eng="pool")
            S.dma(qt_[1][64:128, :], qk[h * 128 + 64:(h + 1) * 128, :])
            S.memset(qt_[1][0:64, :], 0.0, eng="pool")
            S.dma(va[:, :, 0:128], V(vb.t[:, h * 128:(h + 1) * 128].rearrange("(n p) e -> p n e", p=128), vb.d))
            S.memset(va[:, :, 128:130], 1.0, eng="pool")
            for rho in range(6):
                c_ = 639 - (rho - 1) * 128
                S.dma(bt[:, rho, :], V(hsc.t[h, :, c_:c_ + 512], hsc.d))
            S.dma(cb[:, 0:1], V(bass.AP(tensor=gsc.t.tensor, offset=gsc.t.offset + h * (NB_LEN + 1) + 1278, ap=[[0, 128], [1, 1]]), gsc.d))
            S.dma(cb[:, 1:2], V(bass.AP(tensor=gsc.t.tensor, offset=gsc.t.offset + h * (NB_LEN + 1), ap=[[0, 128], [1, 1]]), gsc.d))
            def acc_of(c, qb):
                return (accA[c], qb * 130) if qb < 3 else (accB[c], 0)

            tiles = [(q5, c, kt) for q5 in range(NT5) for c in range(2) for kt in range(NCH)]
            LOOK = 2
            Es = {}
            for idx in range(len(tiles) + LOOK):
                if idx < len(tiles):
                    q5, c, kt = tiles[idx]
                    pb = c * 64
                    st = stR.next()
                    S.mm(st[:, :], kt_[:, kt * 128:(kt + 1) * 128], qt_[c][:, q5 * 512:(q5 + 1) * 512])
                    rho = kt - 4 * q5 + 1
                    E = ER.next()
                    if 0 <= rho <= 5:
                        tmp = TMP.next()
                        S.stt(tmp[:, :], st[:, :], 0.125, bt[:, rho, :], ALU.mult, ALU.add)
                        S.act(E[:, :], tmp[:, :], AF.Exp)
                    else:
                        S.act(E[:, :], st[:, :], AF.Exp, bias=(cb[:, 0:1] if rho < 0 else cb[:, 1:2]), scale=0.125)
                    Es[idx] = E
                j = idx - LOOK
                if j < 0:
                    continue
                q5, c, kt = tiles[j]
                E = Es.pop(j)
                for qb in range(4):
                    a, o = acc_of(c, qb)
                    S.mm(a[:, o:o + 130], E[:, qb * 128:(qb + 1) * 128], va[:, kt, 0:130],
                         start=(kt == 0 and qb in (0, 3)), stop=(kt == NCH - 1))
                if kt != NCH - 1:
                    continue
                for qb in range(4):
                    a, o = acc_of(c, qb)
                    sm = SM.next()
                    S.op("dve", lambda e, sm=sm, a=a, o=o: e.reciprocal(out=sm.t[:, 0:1], in_=a.t[:, o + 128:o + 129]),
                         [a.d], [sm.d])
                    S.ts(ON[c][:, qb, :], a[:, o:o + 128], sm[:, 0:1], None, ALU.mult)
                if c == 0:
                    continue
                ost = OST.next()
                for qb in range(4):
                    of = OF.next(); ob = OB.next(); sm = SM.next()
                    S.stt(of[:, :], ON[1][:, qb, :], nlam[:, 0:1], ON[0][:, qb, :], ALU.mult, ALU.add)
                    S.act(ob[:, :], of[:, :], AF.Square, accum=sm[:, 0:1])
                    S.act(sm[:, 1:2], sm[:, 0:1], AF.Sqrt, bias=epsln[:, 0:1], scale=1.0 / 128.0)
                    S.op("dve", lambda e, sm=sm: e.reciprocal(out=sm.t[:, 2:3], in_=sm.t[:, 1:2]), [sm.d], [sm.d])
                    S.stt(ob[:, :], of[:, :], sm[:, 2:3], subg[:, :], ALU.mult, ALU.mult)
                    S.tr(ptr[:, qb * 128:(qb + 1) * 128], ob[:, :], identb[:, :])
                S.copy(ost[:, :], ptr[:, :])
                S.dma(mix[512 + h * 128:512 + (h + 1) * 128, q5 * 512:(q5 + 1) * 512], ost[:, :])
        C.close()

    def phase_wout(l, s):
        C = Ctx(nc, S)
        w = C.sb([128, 8, D], BF16, "wout")
        load_w_bf16(w, P["w_out"][l], 8, D)
        g_bc = C.sb([128, D], F32, "gbc"); b_bc = C.sb([128, D], F32, "bbc")
        S.dma(g_bc[:, :], V(bcast_rows(P["ln1_g"][l], D), None))
        S.dma(b_bc[:, :], V(bcast_rows(P["ln1_b"][l], D), None))
        CC = {"st": Ring([C.sb([128, 16], F32, "st") for _ in range(4)])}
        MT = Ring([C.sb([128, 8, 512], BF16, "mixT") for _ in range(2)])
        XB = Ring([C.sb([128, D], F32, "xb") for _ in range(4)])
        HB = Ring([C.sb([128, D], F32, "hb") for _ in range(4)])
        PS = Ring([C.ps([128, 1024], F32, "pso") for _ in range(3)])
        for t5 in range(NT5):
            t0 = t5 * 512
            mt = MT.next()
            S.dma(mt[:, :, :], V(mix.t[:, t0:t0 + 512].rearrange("(k p) t -> p k t", p=128), mix.d))
            for j in range(4):
                r0 = t0 + j * 128
                xb = XB.next(); hb = HB.next(); p = PS.next()
                S.dma(xb[:, :], xres[r0:r0 + 128, :])
                for half in range(2):
                    for k in range(8):
                        S.mm(p[:, half * 512:(half + 1) * 512], mt[:, k, j * 128:(j + 1) * 128],
                             w[:, k, half * 512:(half + 1) * 512], start=(k == 0), stop=(k == 7))
                S.stt(hb[:, :], xb[:, :], ALPHA, p[:, :], ALU.mult, ALU.add)
                layer_norm_rows(CC, hb[:, :], g_bc[:, :], b_bc[:, :], hb[:, :])
                S.dma(x1s[r0:r0 + 128, :], hb[:, :])
        C.close()

    def phase_ffn(l, s):
        C = Ctx(nc, S)
        wd = C.sb([128, 22, D], BF16, "wdown")
        load_w_bf16(wd, P["ffn_down"][l], 22, D)
        g_bc = C.sb([128, D], F32, "gbc"); b_bc = C.sb([128, D], F32, "bbc")
        S.dma(g_bc[:, :], V(bcast_rows(P["ln2_g"][l], D), None))
        S.dma(b_bc[:, :], V(bcast_rows(P["ln2_b"][l], D), None))
        cw = C.sb([128, 4, 44], F32, "cw")
        stg = C.sb([64, 128], F32, "stg")
        pst = Ring([C.ps([128, 512], F32, "pst") for _ in range(2)])
        for j in range(3):
            load_cols(C, cw[:, j, :], P["conv_w"][l, j], 44, pst.next(), stg)
        load_cols(C, cw[:, 3, :], P["conv_b"][l], 44, pst.next(), stg)
        CC = {"st": Ring([C.sb([128, 16], F32, "st") for _ in range(4)])}
        XB = Ring([C.sb([128, D], F32, "xb") for _ in range(8)])
        XH = Ring([C.sb([2, D], F32, "xh") for _ in range(2)])
        XT = Ring([C.sb([128, 8, 512], BF16, "xT") for _ in range(2)])
        XHT = Ring([C.sb([128, 8, 2], BF16, "xhT") for _ in range(2)])
        WC = Ring([C.sb([128, 8 * 128], BF16, "wc") for _ in range(4)])
        PH = Ring([C.ps([128, 512], F32, "ph") for _ in range(2)])
        PHHS = Ring([(C.ps([128, 512], F32, "phh"), 0) for i in range(2)])
        PD = Ring([C.ps([128, 1024], F32, "pd") for _ in range(1)])
        HR = Ring([C.sb([128, 514], F32, "hr") for _ in range(5)])
        OO = Ring([C.sb([128, 512], F32, "oo") for _ in range(3)])
        GA = C.sb([128, 22, 512], BF16, "ga")
        GT = C.sb([128, 22, 512], BF16, "gt")
        HB = Ring([C.sb([128, D], F32, "hb") for _ in range(3)])
        for t5 in range(NT5):
            t0 = t5 * 512
            xt = XT.next(); xh = XH.next(); xht = XHT.next()
            blocks = []
            for j in range(4):
                xb = XB.next()
                S.dma(xb[:, :], x1s[t0 + j * 128:t0 + (j + 1) * 128, :])
                blocks.append(xb)
            S.memset(xh[:, :], 0.0, eng="pool")
            if t0 > 0:
                S.dma(xh[0:1, :], x1s[t0 - 1:t0, :])
            if t0 + 512 < T:
                S.dma(xh[1:2, :], x1s[t0 + 512:t0 + 513, :])
            for k in range(8):
                p = pst.next()
                for j in range(4):
                    S.tr(p[:, j * 128:(j + 1) * 128], blocks[j][:, k * 128:(k + 1) * 128], ident[:, :])
                S.copy(xt[:, k, :], p[:, :], eng=("act" if k % 2 == 0 else "dve"))
            p = pst.next()
            for k in range(8):
                S.tr(p[:, 2 * k:2 * k + 2], xh[0:2, k * 128:(k + 1) * 128], ident[0:2, 0:2])
            S.copy(xht[:, :, :], V(p.t[:, 0:16].rearrange("p (k t) -> p k t", t=2), p.d))
            def part_a(c):
                wc = WC.next()
                S.dma(wc[:, :], upbf[l, c])
                ph = PH.next()
                for k in range(8):
                    S.mm(ph[:, :], wc[:, k * 128:(k + 1) * 128], xt[:, k, :], start=(k == 0), stop=(k == 7))
                PHH, ho = PHHS.next()
                for k in range(8):
                    S.mm(PHH[:, ho:ho + 2], wc[:, k * 128:(k + 1) * 128], xht[:, k, :], start=(k == 0), stop=(k == 7))
                hr = HR.next()
                S.copy(hr[:, 1:513], ph[:, :], eng="act")
                S.copy(hr[:, 0:1], PHH[:, ho:ho + 1], eng="dve")
                S.copy(hr[:, 513:514], PHH[:, ho + 1:ho + 2], eng="dve")
                return hr

            def part_b(c, hr):
                oo = OO.next()
                if c < 22:
                    S.ts(oo[:, :], hr[:, 1:513], cw[:, 1, c:c + 1], cw[:, 3, c:c + 1], ALU.mult, ALU.add, eng="pool")
                else:
                    S.act(oo[:, :], hr[:, 1:513], AF.Identity, bias=cw[:, 3, c:c + 1], scale=cw[:, 1, c:c + 1])
                S.stt(oo[:, :], hr[:, 0:512], cw[:, 0, c:c + 1], oo[:, :], ALU.mult, ALU.add)
                S.stt(oo[:, :], hr[:, 2:514], cw[:, 2, c:c + 1], oo[:, :], ALU.mult, ALU.add)
                if c < 22:
                    S.act(GA[:, c, :], oo[:, :], AF.Gelu)
                else:
                    S.tt(GT[:, c - 22, :], oo[:, :], GA[:, c - 22, :], ALU.mult, eng="pool")

            hrs = {}
            for c in range(46):
                if c < 44:
                    hrs[c] = part_a(c)
                if c >= 2:
                    part_b(c - 2, hrs.pop(c - 2))
            for j in range(4):
                r0 = t0 + j * 128
                p = PD.next(); hb = HB.next()
                for half in range(2):
                    for k in range(22):
                        S.mm(p[:, half * 512:(half + 1) * 512], GT[:, k, j * 128:(j + 1) * 128],
                             wd[:, k, half * 512:(half + 1) * 512], start=(k == 0), stop=(k == 21))
                S.stt(hb[:, :], blocks[j][:, :], ALPHA, p[:, :], ALU.mult, ALU.add)
                layer_norm_rows(CC, hb[:, :], g_bc[:, :], b_bc[:, :], hb[:, :])
                if l == DEPTH - 1:
                    S.dma(V(y_out.t[s, r0:r0 + 128, :], y_out.d), hb[:, :])
                else:
                    S.dma(xres[r0:r0 + 128, :], hb[:, :])
        C.close()

    def RV(v, pattern, **kw):
        return V(v.ap.rearrange(pattern, **kw), v.d)

    def phase_rwkv(l, s):
        C = Ctx(nc, S)
        pstc = C.ps([128, 512], F32, "pstc")
        stg = C.sb([64, 128], F32, "stg")
        mu = C.sb([128, 3, 14], F32, "mu")
        for j in range(2):
            S.memset(stg[0:14, :], 0.0)
            S.dma(stg[0:13, :], V(P["shift_mu"][l, j, 0:1664].rearrange("(c p) -> c p", p=128), None))
            S.dma(stg[13:14, 0:96], V(P["shift_mu"][l, j, 1664:1760].rearrange("(c p) -> c p", p=96), None))
            S.tr(pstc[:, 0:14], stg[0:14, :], ident[0:14, 0:14])
            S.copy(mu[:, 1 + j, :], pstc[:, 0:14])
        S.tt(mu[:, 0, :], mu[:, 1, :], mu[:, 2, :], ALU.add)
        S.ts(mu[:, 0, :], mu[:, 0, :], -1.0, 1.0, ALU.mult, ALU.add)
        PRM = C.sb([128, 12, 4], F32, "prm")
        srcs = [P["w0"][l, 0], P["w0"][l, 1], P["a0"][l, 0], P["a0"][l, 1], P["k_k"][l], P["k_a"][l], P["r_k"][l],
                P["gn_g"][l], P["gn_b"][l], P["vres_0"][0]]
        for i, sap in enumerate(srcs):
            load_cols(C, PRM[:, i, :], sap, 4, pstc, stg)
        S.ts(PRM[:, 10, :], PRM[:, 5, :], -1.0, 1.0, ALU.mult, ALU.add)
        S.ts(PRM[:, 11, :], PRM[:, 5, :], -2.0, 2.0, ALU.mult, ALU.add)
        dup = C.sb([64, AW], BF16, "dup")
        iup = [C.sb([128, AW], BF16, "iup") for _ in range(2)]
        gup = C.sb([96, AW], BF16, "gup")
        vdn = C.sb([128, 4, 32], BF16, "vdn")
        vup = C.sb([32, AW], BF16, "vup")
        S.dma(dup[:, :], V(P["decay_up"][l].rearrange("d r c -> (d r) c"), None), q="pool")
        for dd in range(2):
            S.memset(iup[dd][64:128, :], 0.0)
            S.dma(iup[dd][64 + 32 * dd:96 + 32 * dd, :], V(P["iclr_up"][l, dd], None), q="pool")
        S.dma(gup[:, :], V(P["g_up"][l], None), q="pool")
        S.dma(vdn[:, :, :], V(P["vres_down"][0].rearrange("(f p) r -> p f r", p=128), None), q="pool")
        S.dma(vup[:, :], V(P["vres_up"][0], None), q="pool")
        blkm = C.sb([128, 128], BF16, "blkm")
        S.ts(blkm[:, :], blk[:, :], 1.0 / 64.0, None, ALU.mult)
        idblk = C.sb([128, 64], F32, "idblk")
        S.tt(idblk[:, :], ident[:, 0:64], ident[:, 64:128], ALU.add)
        ones = C.sb([128, 128], F32, "ones")
        S.memset(ones[:, :], 1.0)
        eps12 = C.sb([128, 1], F32, "eps12")
        S.memset(eps12[:, :], 1e-12)
        epsgn = C.sb([128, 1], F32, "epsgn")
        S.memset(epsgn[:, :], GN_EPS)
        MS2 = C.sb([128, 2, 128], F32, "ms2")
        MI2 = C.sb([128, 2, 128], F32, "mi2")
        MT2 = C.sb([128, 2, 128], F32, "mt2")

        _stage(1)
        XIN = Ring([C.sb([128, 14, 130], F32, "xin") for _ in range(2)])
        SH = [[C.sb([128, 128], F32, "sh") for _ in range(14)] for _ in range(2)]
        RA = Ring([C.ps([128, 512], F32, "ra") for _ in range(6)])
        PTB = C.ps([128, 512], BF16, "ptb")
        TWt = C.sb([64, 128], BF16, "tw")
        ADt = C.sb([128, 128], BF16, "ad")
        SGt = C.sb([96, 128], BF16, "sg")
        VD = C.sb([32, 128], BF16, "vd")

        def fset(n, dt, shape=(128, 128), name="f"):
            return [[C.sb(list(shape), dt, name) for _ in range(4)] for _ in range(n)]
        (Ee, Cu, La, Gm, Gi, Gp, Ge, K0, Kk, Bb, Kd, T1, Yy, Dl, Yn, T2) = fset(16, F32)
        A0 = [fset(1, F32)[0] for _ in range(2)]
        A1 = [fset(1, F32)[0] for _ in range(2)]
        Vv = [fset(1, F32)[0] for _ in range(2)]
        SMs = [[C.sb([128, 8], F32, "sms") for _ in range(4)] for _ in range(2)]
        (SQ, BH, KH, VB16, RH, YB, RKV, OA) = fset(8, BF16)
        AT = [fset(1, BF16)[0] for _ in range(2)]
        VT = [fset(1, BF16)[0] for _ in range(2)]
        AR = [[C.sb([128, 256], BF16, "ar") for _ in range(4)] for _ in range(2)]
        BK = [[C.sb([128, 256], BF16, "bk") for _ in range(4)] for _ in range(2)]
        DI = [[C.sb([128, 2, 192], BF16, "di") for _ in range(4)] for _ in range(2)]
        DX = [[C.sb([128, 2, 192], BF16, "dx") for _ in range(4)] for _ in range(2)]
        GQa = [[C.sb([128, 2, 192], BF16, "gqa") for _ in range(4)] for _ in range(2)]
        GQ = [C.sb([128, 2, 192], BF16, "gq") for _ in range(4)]
        MTAK = [C.sb([128, 2, 128], BF16, "mtak") for _ in range(4)]
        PW = [[C.sb([128, 2, 2, 128], BF16, "pw") for _ in range(4)] for _ in range(3)]
        PM = [C.sb([128, 64], BF16, "pm") for _ in range(4)]
        ST = [[C.sb([128, 64], BF16, "st") for _ in range(4)] for _ in range(2)]
        VF = [C.sb([128, 128], F32, "vf") for _ in range(4)]
        YF = [C.sb([128, 128], F32, "yf") for _ in range(4)]

        def prep(d, ci, par):
            c0 = ci * 128
            xin = XIN.next()
            sh = SH[par]
            lo = max(c0 - 1, 0)
            hi = min(c0 + 129, T)
            if c0 == 0:
                S.memset(xin[:, :, 0:1], 0.0, eng="pool")
            if c0 + 129 > T:
                S.memset(xin[:, :, 129:130], 0.0, eng="pool")
            o0 = lo - (c0 - 1)
            S.dma(xin[:, :, o0:o0 + (hi - lo)], V(pa.t[:, lo:hi].rearrange("(r p) t -> p r t", p=128), pa.d))
            for rt in range(14):
                S.act(sh[rt][:, :], xin[:, rt, 1:129], AF.Copy, scale=mu[:, 0, rt:rt + 1])
                S.stt(sh[rt][:, :], xin[:, rt, 0:128], mu[:, 1, rt:rt + 1], sh[rt][:, :], ALU.mult, ALU.add)
                S.stt(sh[rt][:, :], xin[:, rt, 2:130], mu[:, 2, rt:rt + 1], sh[rt][:, :], ALU.mult, ALU.add)
                if rt % 2 == 1:
                    yield
            r_ = sh[0:4]; k_ = sh[4:8]; v_ = sh[8:12]
            S.act(TWt[32 * d:32 * d + 32, :], sh[12][32 * d:32 * d + 32, :], AF.Tanh)
            S.copy(ADt[64:128, :], sh[12][64:128, :], eng="pool")
            for ft in range(4):
                fs = slice(ft * 128, (ft + 1) * 128)
                pz = RA.next()
                S.mm(pz[:, 0:128], dup[32 * d:32 * d + 32, fs], TWt[32 * d:32 * d + 32, :])
                S.act(Ee[ft][:, :], pz[:, 0:128], AF.Sigmoid, bias=PRM[:, d, ft:ft + 1], scale=1.0)
                S.op("dve", lambda e, ft=ft: e.tensor_tensor_scan(out=Cu[ft].t[:, :], data0=ones.t[:, :], data1=Ee[ft].t[:, :],
                                                                   initial=0.0, op0=ALU.mult, op1=ALU.add),
                     [ones.d, Ee[ft].d], [Cu[ft].d])
                sm = SMs[par][ft]
                if d == 0:
                    lam_t = Cu[ft]
                else:
                    S.stt(La[ft][:, :], Cu[ft][:, :], -1.0, Ee[ft][:, :], ALU.mult, ALU.add)
                    S.ts(La[ft][:, :], La[ft][:, :], Cu[ft][:, 127:128], None, ALU.add)
                    lam_t = La[ft]
                S.ts(sm[:, 0:1], Cu[ft][:, 127:128], -CEXP, None, ALU.mult)
                S.act(sm[:, 1:2], Cu[ft][:, 127:128], AF.Exp, scale=-CEXP)
                S.act(Gm[ft][:, :], lam_t[:, :], AF.Exp, scale=-CEXP)
                S.act(Gi[ft][:, :], lam_t[:, :], AF.Exp, scale=CEXP)
                S.tt(Gp[ft][:, :], lam_t[:, :], Ee[ft][:, :], ALU.subtract, eng="pool")
                S.act(Gp[ft][:, :], Gp[ft][:, :], AF.Exp, scale=-CEXP)
                S.act(Ge[ft][:, :], lam_t[:, :], AF.Exp, bias=sm[:, 0:1], scale=CEXP)
                for dd in ((0,) if d == 0 else (0, 1)):
                    pq = RA.next()
                    S.mm(pq[:, 0:128], iup[dd][64:128, fs], ADt[64:128, :])
                    S.act((A0 if dd == 0 else A1)[par][ft][:, :], pq[:, 0:128], AF.Sigmoid, bias=PRM[:, 2 + dd, ft:ft + 1], scale=1.0)
                yield
            for ft in range(4):
                S.copy(Vv[par][ft][:, :], v_[ft][:, :], eng="pool")
            if l == 1:
                pvd = RA.next()
                for ft in range(4):
                    S.copy(VB16[ft][:, :], Vv[par][ft][:, :], eng="pool")
                    S.mm(pvd[0:32, 0:128], vdn[:, ft, :], VB16[ft][:, :], start=(ft == 0), stop=(ft == 3))
                S.copy(VD[:, :], pvd[0:32, 0:128], eng="act")
                yield
                for ft in range(4):
                    fs = slice(ft * 128, (ft + 1) * 128)
                    pg = RA.next()
                    S.dma(VF[ft][:, :], vfs[ft * 128:(ft + 1) * 128, c0:c0 + 128])
                    S.mm(pg[:, 0:128], vup[:, fs], VD[:, :])
                    S.act(T1[ft][:, :], pg[:, 0:128], AF.Sigmoid, bias=PRM[:, 9, ft:ft + 1], scale=1.0)
                    S.tt(Dl[ft][:, :], VF[ft][:, :], Vv[par][ft][:, :], ALU.subtract, eng="pool")
                    S.tt(Dl[ft][:, :], Dl[ft][:, :], T1[ft][:, :], ALU.mult, eng="pool")
                    S.tt(Vv[par][ft][:, :], Vv[par][ft][:, :], Dl[ft][:, :], ALU.add, eng="pool")
                    yield
            elif d == 0:
                for ft in range(4):
                    S.dma(vfs[ft * 128:(ft + 1) * 128, c0:c0 + 128], Vv[par][ft][:, :])
            for ft in range(4):
                S.ts(K0[ft][:, :], k_[ft][:, :], PRM[:, 4, ft:ft + 1], None, ALU.mult)
                S.act(SQ[ft][:, :], K0[ft][:, :], AF.Square)
                pss = RA.next()
                S.mm(pss[:, 0:128], blk[:, :], SQ[ft][:, :])
                S.act(Kk[ft][:, :], pss[:, 0:128], AF.Sqrt, bias=eps12[:, 0:1], scale=1.0)
                S.op("dve", lambda e, ft=ft: e.reciprocal(out=Kk[ft].t[:, :], in_=Kk[ft].t[:, :]), [Kk[ft].d], [Kk[ft].d])
                S.tt(Kk[ft][:, :], Kk[ft][:, :], K0[ft][:, :], ALU.mult)
                Ad = (A0 if d == 0 else A1)[par][ft]
                S.stt(AR[par][ft][:, 0:128], Kk[ft][:, :], -1.0, Gp[ft][:, :], ALU.mult, ALU.mult)
                S.tt(AR[par][ft][:, 128:256], r_[ft][:, :], Gm[ft][:, :], ALU.mult, eng="pool")
                S.tt(Bb[ft][:, :], Kk[ft][:, :], Ad[:, :], ALU.mult)
                S.tt(BK[par][ft][:, 0:128], Bb[ft][:, :], Gi[ft][:, :], ALU.mult, eng="pool")
                S.tt(BH[ft][:, :], Bb[ft][:, :], Ge[ft][:, :], ALU.mult)
                S.ts(T1[ft][:, :], Ad[:, :], PRM[:, 5, ft:ft + 1], PRM[:, 10, ft:ft + 1], ALU.mult, ALU.add)
                S.tt(Kd[ft][:, :], k_[ft][:, :], T1[ft][:, :], ALU.mult, eng="pool")
                S.tt(BK[par][ft][:, 128:256], Kd[ft][:, :], Gi[ft][:, :], ALU.mult, eng="pool")
                S.tt(KH[ft][:, :], Kd[ft][:, :], Ge[ft][:, :], ALU.mult)
                S.copy(VB16[ft][:, :], Vv[par][ft][:, :], eng="pool")
                yield
            for ft in range(4):
                S.tr(PTB[:, 0:128], AR[par][ft][:, 0:128], identb[:, :])
                S.tr(PTB[:, 128:256], BH[ft][:, :], identb[:, :])
                S.tr(PTB[:, 256:384], KH[ft][:, :], identb[:, :])
                S.tr(PTB[:, 384:512], VB16[ft][:, :], identb[:, :])
                S.copy(AT[par][ft][:, :], PTB[:, 0:128], eng="dve")
                for h in range(2):
                    S.copy(DI[par][ft][:, h, 128:192], PTB[:, 128 + h * 64:192 + h * 64], eng="dve")
                for h in range(2):
                    S.copy(GQa[par][ft][:, h, 128:192], PTB[:, 256 + h * 64:320 + h * 64], eng="dve")
                S.copy(VT[par][ft][:, :], PTB[:, 384:512], eng="dve")
                yield

        def mats(d, ci, par, sti):
            c0 = ci * 128
            sh = SH[par]
            r_ = sh[0:4]; k_ = sh[4:8]
            for ft in range(4):
                for h in range(2):
                    hb = h * 64
                    pA = RA.next()
                    S.mm(pA[:, 0:256], BK[par][ft][hb:hb + 64, 0:128], AR[par][ft][hb:hb + 64, :])
                    S.tt(PW[0][ft][:, h, 0, :], pA[:, 0:128], MS2[:, h, :], ALU.mult)
                    S.tt(DI[par][ft][:, h, 0:128], pA[:, 128:256], MI2[:, h, :], ALU.mult)
                    pB = RA.next()
                    S.mm(pB[:, 0:128], BK[par][ft][hb:hb + 64, 128:256], AR[par][ft][hb:hb + 64, 128:256])
                    S.tt(GQa[par][ft][:, h, 0:128], pB[:, 0:128], MI2[:, h, :], ALU.mult)
                    pC = RA.next()
                    S.mm(pC[:, 0:256], AR[par][ft][hb:hb + 64, 0:128], BK[par][ft][hb:hb + 64, :])
                    S.tt(PW[0][ft][:, h, 1, :], pC[:, 0:128], MT2[:, h, :], ALU.mult)
                    S.tt(MTAK[ft][:, h, :], pC[:, 128:256], MT2[:, h, :], ALU.mult)
                yield
            Dc = [DI[par][ft] for ft in range(4)]
            for j in range(7):
                pwc = PW[j % 3]
                pwn = PW[(j + 1) % 3]
                for ft in range(4):
                    if j < 6:
                        pP = RA.next()
                        for h in range(2):
                            S.mm(pP[:, h * 256:h * 256 + 128], pwc[ft][:, h, 1, :], pwc[ft][:, h, 0, :])
                            S.mm(pP[:, h * 256 + 128:h * 256 + 256], pwc[ft][:, h, 0, :], pwc[ft][:, h, 1, :])
                        S.copy(RV(pwn[ft][:, :, :, :], "p h t c -> p (h t c)"), pP[:, :], eng="act")
                    pD = RA.next()
                    for h in range(2):
                        S.mm(pD[:, h * 192:(h + 1) * 192], identb[:, :], Dc[ft][:, h, :], start=True, stop=False)
                        S.mm(pD[:, h * 192:(h + 1) * 192], pwc[ft][:, h, 1, :], Dc[ft][:, h, :], start=False, stop=True)
                    Dn = DX[j % 2][ft]
                    S.copy(RV(Dn[:, :, :], "p h c -> p (h c)"), pD[:, 0:384], eng="dve")
                    Dc[ft] = Dn
                    yield
            stc = ST[sti % 2]
            stn = ST[(sti + 1) % 2]
            for ft in range(4):
                DF = Dc[ft]
                sm = SMs[par][ft]
                pR = RA.next()
                for h in range(2):
                    hb = h * 64
                    S.mm(pR[hb:hb + 64, 0:192], AT[par][ft][:, hb:hb + 64], DF[:, h, :])
                S.tt(RH[ft][:, :], pR[:, 0:128], AR[par][ft][:, 128:256], ALU.add)
                S.stt(PM[ft][:, :], idblk[:, :], sm[:, 1:2], pR[:, 128:192], ALU.mult, ALU.add)
                pG = RA.next()
                for h in range(2):
                    S.mm(pG[:, h * 192:(h + 1) * 192], MTAK[ft][:, h, :], DF[:, h, :])
                S.tt(RV(GQ[ft][:, :, :], "p h c -> p (h c)"), pG[:, 0:384], RV(GQa[par][ft][:, :, :], "p h c -> p (h c)"), ALU.add)
                yield
            for ft in range(4):
                if d == 1:
                    S.dma(YF[ft][:, :], yfs[ft * 128:(ft + 1) * 128, c0:c0 + 128])
                for h in range(2):
                    hb = h * 64
                    pY = RA.next()
                    pS = RA.next()
                    S.mm(pY[hb:hb + 64, 0:128], stc[ft][hb:hb + 64, :], RH[ft][hb:hb + 64, :], start=True, stop=False)
                    S.mm(pY[hb:hb + 64, 0:128], VT[par][ft][:, hb:hb + 64], GQ[ft][:, h, 0:128], start=False, stop=True)
                    S.mm(pS[hb:hb + 64, 0:64], PM[ft][hb:hb + 64, :], stc[ft][hb:hb + 64, :], start=True, stop=False)
                    S.mm(pS[hb:hb + 64, 0:64], GQ[ft][:, h, 128:192], VT[par][ft][:, hb:hb + 64], start=False, stop=True)
                    S.copy(stn[ft][hb:hb + 64, :], pS[hb:hb + 64, 0:64], eng="act")
                    if d == 0:
                        S.copy(Yy[ft][hb:hb + 64, :], pY[hb:hb + 64, 0:128], eng="dve")
                    else:
                        S.tt(Yy[ft][hb:hb + 64, :], pY[hb:hb + 64, 0:128], YF[ft][hb:hb + 64, :], ALU.add)
                yield
                if d == 0:
                    S.dma(yfs[ft * 128:(ft + 1) * 128, c0:c0 + 128], Yy[ft][:, :])
                    continue
                fs = slice(ft * 128, (ft + 1) * 128)
                S.copy(YB[ft][:, :], Yy[ft][:, :], eng="pool")
                pm_ = RA.next()
                S.mm(pm_[:, 0:128], blkm[:, :], YB[ft][:, :])
                S.tt(Dl[ft][:, :], Yy[ft][:, :], pm_[:, 0:128], ALU.subtract)
                S.act(SQ[ft][:, :], Dl[ft][:, :], AF.Square)
                pv_ = RA.next()
                S.mm(pv_[:, 0:128], blkm[:, :], SQ[ft][:, :])
                S.act(Yn[ft][:, :], pv_[:, 0:128], AF.Sqrt, bias=epsgn[:, 0:1], scale=1.0)
                S.op("dve", lambda e, ft=ft: e.reciprocal(out=Yn[ft].t[:, :], in_=Yn[ft].t[:, :]), [Yn[ft].d], [Yn[ft].d])
                S.tt(Yn[ft][:, :], Yn[ft][:, :], Dl[ft][:, :], ALU.mult)
                S.ts(Yn[ft][:, :], Yn[ft][:, :], PRM[:, 7, ft:ft + 1], PRM[:, 8, ft:ft + 1], ALU.mult, ALU.add)
                S.tt(T2[ft][:, :], A0[par][ft][:, :], A1[par][ft][:, :], ALU.add, eng="pool")
                S.ts(T2[ft][:, :], T2[ft][:, :], PRM[:, 5, ft:ft + 1], PRM[:, 11, ft:ft + 1], ALU.mult, ALU.add)
                S.tt(T2[ft][:, :], T2[ft][:, :], k_[ft][:, :], ALU.mult, eng="pool")
                S.stt(RKV[ft][:, :], r_[ft][:, :], PRM[:, 6, ft:ft + 1], T2[ft][:, :], ALU.mult, ALU.mult)
                pb_ = RA.next()
                S.mm(pb_[:, 0:128], blk[:, :], RKV[ft][:, :])
                S.tt(T2[ft][:, :], pb_[:, 0:128], Vv[par][ft][:, :], ALU.mult)
                S.tt(Yn[ft][:, :], Yn[ft][:, :], T2[ft][:, :], ALU.add)
                if ft == 0:
                    S.act(SGt[:, :], sh[13][0:96, :], AF.Sigmoid)
                pg_ = RA.next()
                S.mm(pg_[:, 0:128], gup[:, fs], SGt[:, :])
                S.tt(OA[ft][:, :], Yn[ft][:, :], pg_[:, 0:128], ALU.mult)
                S.dma(mix[ft * 128:(ft + 1) * 128, c0:c0 + 128], OA[ft][:, :])
                yield

        def run_both(a, b):
            alive = [g for g in (a, b) if g is not None]
            while alive:
                for g in list(alive):
                    try:
                        next(g)
                    except StopIteration:
                        alive.remove(g)

        for d in range(2):
            iLs, iLi, iLt = (0, 1, 2) if d == 0 else (2, 3, 0)
            for h in range(2):
                S.copy(MS2[:, h, :], masks[:, iLs, :])
                S.copy(MI2[:, h, :], masks[:, iLi, :])
                S.copy(MT2[:, h, :], masks[:, iLt, :])
            for ft in range(4):
                S.memset(ST[0][ft][:, :], 0.0)
            order = list(range(NCH)) if d == 0 else list(range(NCH - 1, -1, -1))
            gm = None
            for n, ci in enumerate(order):
                run_both(prep(d, ci, n % 2), gm)
                gm = mats(d, ci, n % 2, n)
            run_both(gm, None)
            S.barrier()
        C.close()

    PH = {"proj": phase_proj, "setup": phase_setup, "attn": phase_attn, "wout": phase_wout, "ffn": phase_ffn,
          "rwkv": phase_rwkv}
    if plan is None:
        plan = [("setup", 0, 0)]
        for s in range(NSEQ):
            for l in range(DEPTH):
                plan += [("proj", l, s), ("rwkv", l, s), ("attn", l, s), ("wout", l, s), ("ffn", l, s)]
    for (name, l, s) in plan:
        try:
            PH[name](l, s)
        except StopBuild:
            print('STOPPED')
    S.barrier()
    return nc


def _t5_bucket_np(rel):
    nb = 16
    max_exact = 8
    ret = np.where(rel > 0, nb, 0)
    n = np.abs(rel)
    nf = np.maximum(n, 1).astype(np.float32)
    large = max_exact + (np.log(nf / np.float32(max_exact)) / np.float32(math.log(128 / max_exact))
                         * np.float32(nb - max_exact)).astype(np.int32)
    large = np.minimum(large, nb - 1)
    return ret + np.where(n < max_exact, n, large)


def _consts():
    ident = np.eye(128, dtype=np.float32)
    a = np.arange(128)[:, None]
    b = np.arange(128)[None, :]
    masks = np.stack([(a < b), (a <= b), (a > b), (a >= b)], 1).astype(np.float32)
    blk = np.zeros((128, 128), np.float32)
    blk[:64, :64] = 1
    blk[64:, 64:] = 1
    delta = 639 - np.arange(NB_LEN + 1)
    bk = _t5_bucket_np(delta.astype(np.int32))
    oh = np.zeros((32, NB_LEN + 1), np.float32)
    oh[bk, np.arange(NB_LEN + 1)] = 1
    return dict(c_ident=ident, c_masks=np.ascontiguousarray(masks), c_blk=blk, c_onehot=oh)


_NC_CACHE = {}


def kernel(**inputs):
    xp = np.asarray(inputs["x_prompt"], np.float32)
    xs = np.asarray(inputs["x_sample"], np.float32)
    T = xp.shape[1]
    allx = [xp[i] for i in range(xp.shape[0])] + [xs[i] for i in range(xs.shape[0])]
    n = len(allx)
    NC = 8
    slots = [(c, 8 + (c % 4)) for c in range(NC)]
    key = (T, 2)
    if key not in _NC_CACHE:
        _NC_CACHE[key] = build(T, 2)
    nc = _NC_CACHE[key]
    base = {k: np.ascontiguousarray(np.asarray(v, np.float32)) for k, v in inputs.items()
            if k not in ("x_prompt", "x_sample")}
    base["r_k"] = base["r_k"].reshape(2, AW)
    base.update(_consts())
    in_maps = []
    for c in range(NC):
        m = dict(base)
        m["x"] = np.ascontiguousarray(np.stack([allx[slots[c][0]], allx[slots[c][1]]], 0))
        in_maps.append(m)
    res = run_bass_kernel_spmd(nc, in_maps, core_ids=list(range(NC)))
    outs = [None] * n
    for c in range(NC):
        y = res.results[c]["y"]
        outs[slots[c][0]] = y[0]
        if c < 4:
            outs[slots[c][1]] = y[1]
    yp = np.stack(outs[:xp.shape[0]], 0).astype(np.float32)
    ys = np.stack(outs[xp.shape[0]:], 0).astype(np.float32)
    return (yp, ys)
```

```python
import contextlib
import os
import math
import numpy as np
import concourse.bass as bass
import concourse.mybir as mybir
from concourse.bass_utils import run_bass_kernel_spmd

F32 = mybir.dt.float32
BF16 = mybir.dt.bfloat16
AF = mybir.ActivationFunctionType
ALU = mybir.AluOpType

D = 1024
DEPTH = 2
AW = 512
A_COLS = 1760
P_COLS = 3296
D_FF = 2816
LN_EPS = 1e-5
GN_EPS = 64e-5
ALPHA = (2 * DEPTH) ** 0.25
NB_LEN = 1279
CEXP = math.exp(-0.5)


class StopBuild(Exception):
    pass


def _stage(n):
    if int(os.environ.get('RWKV_STOP', '99')) == n:
        raise StopBuild()


class Dep:
    __slots__ = ("w", "r")

    def __init__(self):
        self.w = {}
        self.r = {}


class V:
    __slots__ = ("ap", "d")

    def __init__(self, ap, d):
        self.ap = ap
        self.d = d


class Tile:
    def __init__(self, t, d=None):
        self.t = t
        self.d = d if d is not None else Dep()

    def __getitem__(self, idx):
        return V(self.t[idx], self.d)


class Sched:
    ENG = ("pe", "act", "dve", "pool", "sp")

    def __init__(self, nc, n_dma=28, n_sw=10):
        self.nc = nc
        self.eng = {"pe": nc.tensor, "act": nc.scalar, "dve": nc.vector, "pool": nc.gpsimd, "sp": nc.sync}
        self.sem = {e: nc.semaphore("s_" + e).__enter__() for e in self.ENG}
        self.cnt = {e: 0 for e in self.ENG}
        self.dsem = [nc.semaphore("d%d" % i).__enter__() for i in range(n_dma)]
        self.dcnt = [0] * n_dma
        self.dnext = 0
        self.n_sw = n_sw
        self.swnext = 0
        self.waited = {e: {} for e in self.ENG}
        self.uid = 0

    def _s(self, k):
        return self.sem[k] if isinstance(k, str) else self.dsem[k]

    def op(self, eng, fn, reads=(), writes=(), dma=False):
        need = {}
        for d in reads:
            if d is None:
                continue
            for k, v in d.w.items():
                if need.get(k, 0) < v:
                    need[k] = v
        for d in writes:
            if d is None:
                continue
            for k, v in d.w.items():
                if k == eng and not dma:
                    continue
                if need.get(k, 0) < v:
                    need[k] = v
            for k, v in d.r.items():
                if k == eng and not dma:
                    continue
                if need.get(k, 0) < v:
                    need[k] = v
        if dma:
            if eng == "pool":
                j = self.swnext
                self.swnext = (j + 1) % self.n_sw
            else:
                j = self.n_sw + self.dnext
                self.dnext = (self.dnext + 1) % (len(self.dsem) - self.n_sw)
            if self.dcnt[j] > 0 and need.get(j, 0) < self.dcnt[j]:
                need[j] = self.dcnt[j]
            self.dcnt[j] += 16
            tk = (j, self.dcnt[j])
        else:
            self.cnt[eng] += 1
            tk = (eng, self.cnt[eng])
        E = self.eng[eng]
        wd = self.waited[eng]
        for k, v in need.items():
            if k == "pe" and eng == "pe" and not dma:
                continue
            if wd.get(k, 0) >= v:
                continue
            wd[k] = v
            E.wait_ge(self._s(k), v)
        ins = fn(E)
        ins.then_inc(self._s(tk[0]), 16 if dma else 1)
        for d in reads:
            if d is not None:
                d.r[tk[0]] = tk[1]
        for d in writes:
            if d is not None:
                d.w[tk[0]] = tk[1]
                d.r = {}
        return tk

    def barrier(self):
        for e in self.ENG:
            E = self.eng[e]
            wd = self.waited[e]
            for k in self.ENG:
                if k != e and self.cnt[k] > wd.get(k, 0):
                    wd[k] = self.cnt[k]
                    E.wait_ge(self.sem[k], self.cnt[k])
            for j in range(len(self.dsem)):
                if self.dcnt[j] > wd.get(j, 0):
                    wd[j] = self.dcnt[j]
                    E.wait_ge(self.dsem[j], self.dcnt[j])

    def dma(self, out, in_, q="sp", **kw):
        return self.op(q, lambda e: e.dma_start(out=out.ap, in_=in_.ap, **kw), [in_.d], [out.d], dma=True)

    def mm(self, out, lhsT, rhs, start=True, stop=True, extra_reads=()):
        return self.op("pe", lambda e: e.matmul(out.ap, lhsT=lhsT.ap, rhs=rhs.ap, start=start, stop=stop,
                                                skip_group_check=True),
                       [lhsT.d, rhs.d] + list(extra_reads), [out.d])

    def tr(self, out, in_, ident):
        return self.op("pe", lambda e: e.transpose(out.ap, in_.ap, ident.ap), [in_.d, ident.d], [out.d])

    def act(self, out, in_, func, bias=None, scale=None, accum=None, eng="act"):
        kw = {}
        rd = [in_.d]
        wr = [out.d]
        if bias is not None:
            if isinstance(bias, V):
                kw["bias"] = bias.ap
                rd.append(bias.d)
            else:
                kw["bias"] = bias
        if scale is not None:
            if isinstance(scale, V):
                kw["scale"] = scale.ap
                rd.append(scale.d)
            else:
                kw["scale"] = scale
        if accum is not None:
            kw["accum_out"] = accum.ap
            wr.append(accum.d)
        return self.op("act", lambda e: e.activation(out=out.ap, in_=in_.ap, func=func, **kw), rd, wr)

    def tt(self, out, a, b, op, eng="dve"):
        return self.op(eng, lambda e: e.tensor_tensor(out=out.ap, in0=a.ap, in1=b.ap, op=op), [a.d, b.d], [out.d])

    def ts(self, out, a, s1, s2, op0, op1=None, eng="dve"):
        rd = [a.d]
        x1, x2 = s1, s2
        if isinstance(s1, V):
            rd.append(s1.d)
            x1 = s1.ap
        if isinstance(s2, V):
            rd.append(s2.d)
            x2 = s2.ap
        if op1 is None:
            return self.op(eng, lambda e: e.tensor_scalar(out=out.ap, in0=a.ap, scalar1=x1, scalar2=None, op0=op0),
                           rd, [out.d])
        return self.op(eng, lambda e: e.tensor_scalar(out=out.ap, in0=a.ap, scalar1=x1, scalar2=x2, op0=op0, op1=op1),
                       rd, [out.d])

    def stt(self, out, a, s, b, op0, op1):
        rd = [a.d, b.d]
        x = s
        if isinstance(s, V):
            rd.append(s.d)
            x = s.ap
        return self.op("dve", lambda e: e.scalar_tensor_tensor(out=out.ap, in0=a.ap, scalar=x, in1=b.ap, op0=op0, op1=op1),
                       rd, [out.d])

    def copy(self, out, in_, eng="dve"):
        if eng == "act":
            return self.act(out, in_, AF.Copy)
        return self.op(eng, lambda e: e.tensor_copy(out=out.ap, in_=in_.ap), [in_.d], [out.d])

    def memset(self, out, val, eng="dve"):
        return self.op(eng, lambda e: e.memset(out.ap, val), [], [out.d])


class Ctx:
    def __init__(self, nc, S):
        self.nc = nc
        self.S = S
        self.es = contextlib.ExitStack()
        self.n = 0

    def sb(self, shape, dt, name="t"):
        self.S.uid += 1
        return Tile(self.es.enter_context(self.nc.sbuf_tensor("%s_%d" % (name, self.S.uid), list(shape), dt)))

    def ps(self, shape, dt=F32, name="p"):
        self.S.uid += 1
        return Tile(self.es.enter_context(self.nc.psum_tensor("%s_%d" % (name, self.S.uid), list(shape), dt)))

    def close(self):
        self.S.barrier()
        self.es.close()


class Ring:
    def __init__(self, tiles):
        self.tiles = tiles
        self.i = 0

    def next(self):
        t = self.tiles[self.i % len(self.tiles)]
        self.i += 1
        return t


def bcast_rows(ap1d, n, parts=128):
    return bass.AP(tensor=ap1d.tensor, offset=ap1d.offset, ap=[[0, parts], [1, n]])


def build(T, NSEQ, dbg=False, plan=None):
    nc = bass.Bass("TRN2", target_bir_lowering=False)
    S = Sched(nc)
    NT5 = T // 512
    NCH = T // 128

    def din(name, shape, dt=F32):
        return nc.dram_tensor(name, list(shape), dt, kind="ExternalInput").ap()

    def dscr(name, shape, dt=F32):
        kind = "ExternalOutput" if dbg else "Internal"
        return Tile(nc.dram_tensor(name, list(shape), dt, kind=kind).ap())

    x_in = din("x", [NSEQ, T, D])
    y_out = Tile(nc.dram_tensor("y", [NSEQ, T, D], F32, kind="ExternalOutput").ap())
    P = {}
    for name, shape in [("ln_in_g", [D]), ("ln_in_b", [D]), ("w_in", [2, D, P_COLS]), ("shift_mu", [2, 2, A_COLS]),
                        ("w0", [2, 2, AW]), ("decay_up", [2, 2, 32, AW]), ("a0", [2, 2, AW]), ("iclr_up", [2, 2, 32, AW]),
                        ("g_up", [2, 96, AW]), ("k_k", [2, AW]), ("k_a", [2, AW]), ("r_k", [2, AW]), ("gn_g", [2, AW]),
                        ("gn_b", [2, AW]), ("vres_down", [1, AW, 32]), ("vres_up", [1, 32, AW]), ("vres_0", [1, AW]),
                        ("lam", [2, 4, 64]), ("subln_g", [2, 128]), ("rel_bias", [32, 4]), ("w_out", [2, D, D]),
                        ("ln1_g", [2, D]), ("ln1_b", [2, D]), ("ffn_up", [2, D, 2 * D_FF]), ("conv_w", [2, 3, 2 * D_FF]),
                        ("conv_b", [2, 2 * D_FF]), ("ffn_down", [2, D_FF, D]), ("ln2_g", [2, D]), ("ln2_b", [2, D]),
                        ("c_ident", [128, 128]), ("c_masks", [128, 4, 128]), ("c_blk", [128, 128]),
                        ("c_onehot", [32, NB_LEN + 1])]:
        P[name] = din(name, shape)

    xres = dscr("xres", [T, D])
    pa = dscr("pa", [1792, T])
    qk = dscr("qk", [1024, T], BF16)
    vb = dscr("vb", [T, 512], BF16)
    mix = dscr("mix", [1024, T], BF16)
    x1s = dscr("x1s", [T, D])
    yfs = dscr("yfs", [512, T])
    vfs = dscr("vfs", [512, T])
    gsc = dscr("gsc", [4, NB_LEN + 1])
    hsc = dscr("hsc", [4, 128, 1536])
    upbf = dscr("upbf", [2, 44, 128, 8 * 128], BF16)

    G = Ctx(nc, S)
    ident = G.sb([128, 128], F32, "ident")
    identb = G.sb([128, 128], BF16, "identb")
    masks = G.sb([128, 4, 128], F32, "masks")
    blk = G.sb([128, 128], BF16, "blk")
    epsln = G.sb([128, 1], F32, "epsln")
    S.dma(ident[:, :], V(P["c_ident"], None))
    S.dma(identb[:, :], V(P["c_ident"], None), q="pool")
    S.dma(masks[:, :, :], V(P["c_masks"], None))
    S.dma(blk[:, :], V(P["c_blk"], None), q="pool")
    S.memset(epsln[:, :], LN_EPS)

    def layer_norm_rows(C, xt, g_bc, b_bc, out):
        st = C["st"].next()
        for c in range(2):
            S.op("dve", lambda e, c=c: e.bn_stats(out=st.t[:, c * 6:(c + 1) * 6], in_=xt.ap[:, c * 512:(c + 1) * 512]),
                 [xt.d], [st.d])
        S.op("dve", lambda e: e.bn_aggr(out=st.t[:, 12:14], in_=st.t[:, 0:12]), [st.d], [st.d])
        S.act(st[:, 14:15], st[:, 13:14], AF.Sqrt, bias=epsln[:, 0:1], scale=1.0)
        S.op("dve", lambda e: e.reciprocal(out=st.t[:, 15:16], in_=st.t[:, 14:15]), [st.d], [st.d])
        S.ts(out, xt, st[:, 12:13], st[:, 15:16], ALU.subtract, ALU.mult)
        S.tt(out, out, g_bc, ALU.mult)
        S.tt(out, out, b_bc, ALU.add)

    def load_w_bf16(dst, src_ap, KT, ncols, col0=0):
        v = src_ap.rearrange("(k p) c -> p k c", p=128)
        for k in range(KT):
            S.dma(dst[:, k, :], V(v[:, k, col0:col0 + ncols], None), q="pool")

    def phase_proj(l, s):
        C = Ctx(nc, S)
        w = C.sb([128, 8, P_COLS], BF16, "win")
        load_w_bf16(w, P["w_in"][l], 8, P_COLS)
        g_bc = C.sb([128, D], F32, "gbc")
        b_bc = C.sb([128, D], F32, "bbc")
        if l == 0:
            S.dma(g_bc[:, :], V(bcast_rows(P["ln_in_g"], D), None))
            S.dma(b_bc[:, :], V(bcast_rows(P["ln_in_b"], D), None))
        CC = {"st": Ring([C.sb([128, 16], F32, "st") for _ in range(4)])}
        xtk = Ring([C.sb([128, D], F32, "xtk") for _ in range(6)])
        xT = Ring([C.sb([128, 8, 512], BF16, "xT") for _ in range(2)])
        pst = Ring([C.ps([128, 512], F32, "pst") for _ in range(3)])
        psm = Ring([C.ps([128, 512], F32, "psm") for _ in range(4)])
        stA = Ring([C.sb([128, 512], F32, "stA") for _ in range(4)])
        stB = Ring([C.sb([128, 512], BF16, "stB") for _ in range(4)])
        ev = 0
        for t5 in range(NT5):
            t0 = t5 * 512
            xt_t = xT.next()
            blocks = []
            for j in range(4):
                xb = xtk.next()
                r0 = t0 + j * 128
                if l == 0:
                    S.dma(xb[:, :], V(x_in[s, r0:r0 + 128, :], None))
                    layer_norm_rows(CC, xb[:, :], g_bc[:, :], b_bc[:, :], xb[:, :])
                    S.dma(xres[r0:r0 + 128, :], xb[:, :], q="sp")
                else:
                    S.dma(xb[:, :], xres[r0:r0 + 128, :])
                blocks.append(xb)
            for k in range(8):
                p = pst.next()
                for j in range(4):
                    S.tr(p[:, j * 128:(j + 1) * 128], blocks[j][:, k * 128:(k + 1) * 128], ident[:, :])
                if k % 2 == 0:
                    S.copy(xt_t[:, k, :], p[:, :], eng="act")
                else:
                    S.copy(xt_t[:, k, :], p[:, :], eng="dve")
            for c in range(22):
                if c < 14:
                    c0 = c * 128
                    m = min(128, A_COLS - c0)
                else:
                    c0 = A_COLS + (c - 14) * 128
                    m = 128
                p = psm.next()
                for k in range(8):
                    S.mm(p[0:m, :], w[:, k, c0:c0 + m], xt_t[:, k, :], start=(k == 0), stop=(k == 7))
                if c < 14:
                    o = stA.next()
                    S.copy(o[0:m, :], p[0:m, :], eng=("act" if ev % 2 == 0 else "dve"))
                    S.dma(pa[c0:c0 + m, t0:t0 + 512], o[0:m, :])
                else:
                    o = stB.next()
                    S.copy(o[:, :], p[:, :], eng=("act" if ev % 2 == 0 else "dve"))
                    S.dma(qk[(c - 14) * 128:(c - 13) * 128, t0:t0 + 512], o[:, :])
                ev += 1
            for j in range(4):
                p = psm.next()
                for k in range(8):
                    S.mm(p[:, :], xt_t[:, k, j * 128:(j + 1) * 128], w[:, k, A_COLS + 1024:A_COLS + 1536],
                         start=(k == 0), stop=(k == 7))
                o = stB.next()
                S.copy(o[:, :], p[:, :], eng=("act" if ev % 2 == 0 else "dve"))
                ev += 1
                S.dma(vb[t0 + j * 128:t0 + (j + 1) * 128, :], o[:, :])
        C.close()

    def load_cols(C, dst, src1d, n, pst, stg):
        S.dma(stg[0:n, :], V(src1d.rearrange("(c p) -> c p", p=128), None))
        S.tr(pst[:, 0:n], stg[0:n, :], ident[0:n, 0:n])
        S.copy(dst, pst[:, 0:n])

    def phase_setup(l_, s_):
        C = Ctx(nc, S)
        for l in range(DEPTH):
            v = P["ffn_up"][l].rearrange("(k p) c -> p k c", p=128)
            for c in range(44):
                stg = None
                S.dma(V(upbf.t[l, c].rearrange("p (k c) -> p k c", k=8), upbf.d), V(v[:, :, c * 128:(c + 1) * 128], None),
                      q="pool")
        rb = C.sb([32, 4], F32, "rb")
        oh = C.sb([32, NB_LEN + 1], F32, "oh")
        gsb = C.sb([4, NB_LEN + 1], F32, "gsb")
        S.dma(rb[:, :], V(P["rel_bias"], None))
        S.dma(oh[:, :], V(P["c_onehot"], None))
        pp = C.ps([128, 512], F32, "pp")
        for c0 in (0, 512, 1024):
            n = min(512, NB_LEN + 1 - c0)
            S.mm(pp[0:4, 0:n], rb[:, :], oh[:, c0:c0 + n])
            S.copy(gsb[:, c0:c0 + n], pp[0:4, 0:n])
        S.dma(gsc[:, :], gsb[:, :])
        for h in range(4):
            dst = bass.AP(tensor=hsc.t.tensor, offset=hsc.t.offset + h * 128 * 1536, ap=[[1537, 128], [1, NB_LEN + 1]])
            src = bass.AP(tensor=gsc.t.tensor, offset=gsc.t.offset + h * (NB_LEN + 1), ap=[[0, 128], [1, NB_LEN + 1]])
            S.dma(V(dst, hsc.d), V(src, gsc.d))
        C.close()

    def phase_attn(l, s):
        C = Ctx(nc, S)
        lam_init = 0.8 - 0.6 * math.exp(-0.3 * l)
        lt = C.sb([1, 4, 64], F32, "lt")
        l2 = C.sb([1, 8], F32, "l2")
        ones1 = C.sb([1, 128], F32, "ones1")
        nlam = C.sb([128, 1], F32, "nlam")
        subg = C.sb([128, 128], F32, "subg")
        S.dma(lt[:, :, :], V(P["lam"][l:l + 1], None))
        S.memset(ones1[:, :], 1.0)
        S.tt(lt[:, 0, :], lt[:, 0, :], lt[:, 1, :], ALU.mult)
        S.tt(lt[:, 2, :], lt[:, 2, :], lt[:, 3, :], ALU.mult)
        S.op("dve", lambda e: e.reduce_sum(out=l2.t[:, 0:1], in_=lt.t[:, 0, :], axis=mybir.AxisListType.X), [lt.d], [l2.d])
        S.op("dve", lambda e: e.reduce_sum(out=l2.t[:, 1:2], in_=lt.t[:, 2, :], axis=mybir.AxisListType.X), [lt.d], [l2.d])
        S.act(l2[:, 2:4], l2[:, 0:2], AF.Exp)
        S.tt(l2[:, 4:5], l2[:, 3:4], l2[:, 2:3], ALU.subtract)
        S.ts(l2[:, 5:6], l2[:, 4:5], -lam_init, None, ALU.add)
        accA = [C.ps([128, 512], F32, "accA") for _ in range(2)]
        accB = [C.ps([128, 512], F32, "accB") for _ in range(2)]
        pl = accB[0]
        S.mm(pl[:, 0:1], ones1[:, :], l2[:, 5:6])
        S.copy(nlam[:, :], pl[:, 0:1])
        S.dma(subg[:, :], V(bcast_rows(P["subln_g"][l], 128), None))
        S.ts(subg[:, :], subg[:, :], 1.0 - lam_init, None, ALU.mult)

        KT = Ring([C.sb([128, T], BF16, "KT") for _ in range(2)])
        QT = Ring([[C.sb([128, T], BF16, "QT") for _ in range(2)] for _ in range(2)])
        VA = Ring([C.sb([128, NCH, 130], BF16, "VA") for _ in range(2)])
        BT = Ring([C.sb([128, 6, 512], F32, "BT") for _ in range(2)])
        CB = Ring([C.sb([128, 2], F32, "CB") for _ in range(2)])
        stR = Ring([C.ps([128, 512], F32, "st") for _ in range(3)])
        ptr = C.ps([128, 512], BF16, "ptr")
        ER = Ring([C.sb([128, 512], BF16, "E") for _ in range(4)])
        TMP = Ring([C.sb([128, 512], F32, "tmp") for _ in range(3)])
        ON = [C.sb([128, 4, 128], F32, "on") for _ in range(2)]
        SM = Ring([C.sb([128, 8], F32, "sm") for _ in range(8)])
        OB = Ring([C.sb([128, 128], BF16, "ob") for _ in range(3)])
        OF = Ring([C.sb([128, 128], F32, "of") for _ in range(3)])
        OST = Ring([C.sb([128, 512], BF16, "ost") for _ in range(2)])
        for h in range(4):
            kt_ = KT.next(); qt_ = QT.next(); va = VA.next(); bt = BT.next(); cb = CB.next()
            S.dma(kt_[:, :], qk[512 + h * 128:512 + (h + 1) * 128, :])
            S.dma(qt_[0][0:64, :], qk[h * 128:h * 128 + 64, :])
            S.memset(qt_[0][64:128, :], 0.0, eng="pool")
            S.dma(qt_[1][64:128, :], qk[h * 128 + 64:(h + 1) * 128, :])
            S.memset(qt_[1][0:64, :], 0.0, eng="pool")
            S.dma(va[:, :, 0:128], V(vb.t[:, h * 128:(h + 1) * 128].rearrange("(n p) e -> p n e", p=128), vb.d))
            S.memset(va[:, :, 128:130], 1.0, eng="pool")
            for rho in range(6):
                c_ = 639 - (rho - 1) * 128
                S.dma(bt[:, rho, :], V(hsc.t[h, :, c_:c_ + 512], hsc.d))
            S.dma(cb[:, 0:1], V(bass.AP(tensor=gsc.t.tensor, offset=gsc.t.offset + h * (NB_LEN + 1) + 1278, ap=[[0, 128], [1, 1]]), gsc.d))
            S.dma(cb[:, 1:2], V(bass.AP(tensor=gsc.t.tensor, offset=gsc.t.offset + h * (NB_LEN + 1), ap=[[0, 128], [1, 1]]), gsc.d))
            def acc_of(c, qb):
                return (accA[c], qb * 130) if qb < 3 else (accB[c], 0)

            tiles = [(q5, c, kt) for q5 in range(NT5) for c in range(2) for kt in range(NCH)]
            LOOK = 2
            Es = {}
            for idx in range(len(tiles) + LOOK):
                if idx < len(tiles):
                    q5, c, kt = tiles[idx]
                    pb = c * 64
                    st = stR.next()
                    S.mm(st[:, :], kt_[:, kt * 128:(kt + 1) * 128], qt_[c][:, q5 * 512:(q5 + 1) * 512])
                    rho = kt - 4 * q5 + 1
                    E = ER.next()
                    if 0 <= rho <= 5:
                        tmp = TMP.next()
                        S.stt(tmp[:, :], st[:, :], 0.125, bt[:, rho, :], ALU.mult, ALU.add)
                        S.act(E[:, :], tmp[:, :], AF.Exp)
                    else:
                        S.act(E[:, :], st[:, :], AF.Exp, bias=(cb[:, 0:1] if rho < 0 else cb[:, 1:2]), scale=0.125)
                    Es[idx] = E
                j = idx - LOOK
                if j < 0:
                    continue
                q5, c, kt = tiles[j]
                E = Es.pop(j)
                for qb in range(4):
                    a, o = acc_of(c, qb)
                    S.mm(a[:, o:o + 130], E[:, qb * 128:(qb + 1) * 128], va[:, kt, 0:130],
                         start=(kt == 0 and qb in (0, 3)), stop=(kt == NCH - 1))
                if kt != NCH - 1:
                    continue
                for qb in range(4):
                    a, o = acc_of(c, qb)
                    sm = SM.next()
                    S.op("dve", lambda e, sm=sm, a=a, o=o: e.reciprocal(out=sm.t[:, 0:1], in_=a.t[:, o + 128:o + 129]),
                         [a.d], [sm.d])
                    S.ts(ON[c][:, qb, :], a[:, o:o + 128], sm[:, 0:1], None, ALU.mult)
                if c == 0:
                    continue
                ost = OST.next()
                for qb in range(4):
                    of = OF.next(); ob = OB.next(); sm = SM.next()
                    S.stt(of[:, :], ON[1][:, qb, :], nlam[:, 0:1], ON[0][:, qb, :], ALU.mult, ALU.add)
                    S.act(ob[:, :], of[:, :], AF.Square, accum=sm[:, 0:1])
                    S.act(sm[:, 1:2], sm[:, 0:1], AF.Sqrt, bias=epsln[:, 0:1], scale=1.0 / 128.0)
                    S.op("dve", lambda e, sm=sm: e.reciprocal(out=sm.t[:, 2:3], in_=sm.t[:, 1:2]), [sm.d], [sm.d])
                    S.stt(ob[:, :], of[:, :], sm[:, 2:3], subg[:, :], ALU.mult, ALU.mult)
                    S.tr(ptr[:, qb * 128:(qb + 1) * 128], ob[:, :], identb[:, :])
                S.copy(ost[:, :], ptr[:, :])
                S.dma(mix[512 + h * 128:512 + (h + 1) * 128, q5 * 512:(q5 + 1) * 512], ost[:, :])
        C.close()

    def phase_wout(l, s):
        C = Ctx(nc, S)
        w = C.sb([128, 8, D], BF16, "wout")
        load_w_bf16(w, P["w_out"][l], 8, D)
        g_bc = C.sb([128, D], F32, "gbc"); b_bc = C.sb([128, D], F32, "bbc")
        S.dma(g_bc[:, :], V(bcast_rows(P["ln1_g"][l], D), None))
        S.dma(b_bc[:, :], V(bcast_rows(P["ln1_b"][l], D), None))
        CC = {"st": Ring([C.sb([128, 16], F32, "st") for _ in range(4)])}
        MT = Ring([C.sb([128, 8, 512], BF16, "mixT") for _ in range(2)])
        XB = Ring([C.sb([128, D], F32, "xb") for _ in range(4)])
        HB = Ring([C.sb([128, D], F32, "hb") for _ in range(4)])
        PS = Ring([C.ps([128, 1024], F32, "pso") for _ in range(3)])
        for t5 in range(NT5):
            t0 = t5 * 512
            mt = MT.next()
            S.dma(mt[:, :, :], V(mix.t[:, t0:t0 + 512].rearrange("(k p) t -> p k t", p=128), mix.d))
            for j in range(4):
                r0 = t0 + j * 128
                xb = XB.next(); hb = HB.next(); p = PS.next()
                S.dma(xb[:, :], xres[r0:r0 + 128, :])
                for half in range(2):
                    for k in range(8):
                        S.mm(p[:, half * 512:(half + 1) * 512], mt[:, k, j * 128:(j + 1) * 128],
                             w[:, k, half * 512:(half + 1) * 512], start=(k == 0), stop=(k == 7))
                S.stt(hb[:, :], xb[:, :], ALPHA, p[:, :], ALU.mult, ALU.add)
                layer_norm_rows(CC, hb[:, :], g_bc[:, :], b_bc[:, :], hb[:, :])
                S.dma(x1s[r0:r0 + 128, :], hb[:, :])
        C.close()

    def phase_ffn(l, s):
        C = Ctx(nc, S)
        wd = C.sb([128, 22, D], BF16, "wdown")
        load_w_bf16(wd, P["ffn_down"][l], 22, D)
        g_bc = C.sb([128, D], F32, "gbc"); b_bc = C.sb([128, D], F32, "bbc")
        S.dma(g_bc[:, :], V(bcast_rows(P["ln2_g"][l], D), None))
        S.dma(b_bc[:, :], V(bcast_rows(P["ln2_b"][l], D), None))
        cw = C.sb([128, 4, 44], F32, "cw")
        stg = C.sb([64, 128], F32, "stg")
        pst = Ring([C.ps([128, 512], F32, "pst") for _ in range(2)])
        for j in range(3):
            load_cols(C, cw[:, j, :], P["conv_w"][l, j], 44, pst.next(), stg)
        load_cols(C, cw[:, 3, :], P["conv_b"][l], 44, pst.next(), stg)
        CC = {"st": Ring([C.sb([128, 16], F32, "st") for _ in range(4)])}
        XB = Ring([C.sb([128, D], F32, "xb") for _ in range(8)])
        XH = Ring([C.sb([2, D], F32, "xh") for _ in range(2)])
        XT = Ring([C.sb([128, 8, 512], BF16, "xT") for _ in range(2)])
        XHT = Ring([C.sb([128, 8, 2], BF16, "xhT") for _ in range(2)])
        WC = Ring([C.sb([128, 8 * 128], BF16, "wc") for _ in range(4)])
        PH = Ring([C.ps([128, 512], F32, "ph") for _ in range(2)])
        PHHS = Ring([(C.ps([128, 512], F32, "phh"), 0) for i in range(2)])
        PD = Ring([C.ps([128, 1024], F32, "pd") for _ in range(1)])
        HR = Ring([C.sb([128, 514], F32, "hr") for _ in range(5)])
        OO = Ring([C.sb([128, 512], F32, "oo") for _ in range(3)])
        GA = C.sb([128, 22, 512], BF16, "ga")
        GT = C.sb([128, 22, 512], BF16, "gt")
        HB = Ring([C.sb([128, D], F32, "hb") for _ in range(3)])
        for t5 in range(NT5):
            t0 = t5 * 512
            xt = XT.next(); xh = XH.next(); xht = XHT.next()
            blocks = []
            for j in range(4):
                xb = XB.next()
                S.dma(xb[:, :], x1s[t0 + j * 128:t0 + (j + 1) * 128, :])
                blocks.append(xb)
            S.memset(xh[:, :], 0.0, eng="pool")
            if t0 > 0:
                S.dma(xh[0:1, :], x1s[t0 - 1:t0, :])
            if t0 + 512 < T:
                S.dma(xh[1:2, :], x1s[t0 + 512:t0 + 513, :])
            for k in range(8):
                p = pst.next()
                for j in range(4):
                    S.tr(p[:, j * 128:(j + 1) * 128], blocks[j][:, k * 128:(k + 1) * 128], ident[:, :])
                S.copy(xt[:, k, :], p[:, :], eng=("act" if k % 2 == 0 else "dve"))
            p = pst.next()
            for k in range(8):
                S.tr(p[:, 2 * k:2 * k + 2], xh[0:2, k * 128:(k + 1) * 128], ident[0:2, 0:2])
            S.copy(xht[:, :, :], V(p.t[:, 0:16].rearrange("p (k t) -> p k t", t=2), p.d))
            def part_a(c):
                wc = WC.next()
                S.dma(wc[:, :], upbf[l, c])
                ph = PH.next()
                for k in range(8):
                    S.mm(ph[:, :], wc[:, k * 128:(k + 1) * 128], xt[:, k, :], start=(k == 0), stop=(k == 7))
                PHH, ho = PHHS.next()
                for k in range(8):
                    S.mm(PHH[:, ho:ho + 2], wc[:, k * 128:(k + 1) * 128], xht[:, k, :], start=(k == 0), stop=(k == 7))
                hr = HR.next()
                S.copy(hr[:, 1:513], ph[:, :], eng="act")
                S.copy(hr[:, 0:1], PHH[:, ho:ho + 1], eng="dve")
                S.copy(hr[:, 513:514], PHH[:, ho + 1:ho + 2], eng="dve")
                return hr

            def part_b(c, hr):
                oo = OO.next()
                if c < 22:
                    S.ts(oo[:, :], hr[:, 1:513], cw[:, 1, c:c + 1], cw[:, 3, c:c + 1], ALU.mult, ALU.add, eng="pool")
                else:
                    S.act(oo[:, :], hr[:, 1:513], AF.Identity, bias=cw[:, 3, c:c + 1], scale=cw[:, 1, c:c + 1])
                S.stt(oo[:, :], hr[:, 0:512], cw[:, 0, c:c + 1], oo[:, :], ALU.mult, ALU.add)
                S.stt(oo[:, :], hr[:, 2:514], cw[:, 2, c:c + 1], oo[:, :], ALU.mult, ALU.add)
                if c < 22:
                    S.act(GA[:, c, :], oo[:, :], AF.Gelu)
                else:
                    S.tt(GT[:, c - 22, :], oo[:, :], GA[:, c - 22, :], ALU.mult, eng="pool")

            hrs = {}
            for c in range(46):
                if c < 44:
                    hrs[c] = part_a(c)
                if c >= 2:
                    part_b(c - 2, hrs.pop(c - 2))
            for j in range(4):
                r0 = t0 + j * 128
                p = PD.next(); hb = HB.next()
                for half in range(2):
                    for k in range(22):
                        S.mm(p[:, half * 512:(half + 1) * 512], GT[:, k, j * 128:(j + 1) * 128],
                             wd[:, k, half * 512:(half + 1) * 512], start=(k == 0), stop=(k == 21))
                S.stt(hb[:, :], blocks[j][:, :], ALPHA, p[:, :], ALU.mult, ALU.add)
                layer_norm_rows(CC, hb[:, :], g_bc[:, :], b_bc[:, :], hb[:, :])
                if l == DEPTH - 1:
                    S.dma(V(y_out.t[s, r0:r0 + 128, :], y_out.d), hb[:, :])
                else:
                    S.dma(xres[r0:r0 + 128, :], hb[:, :])
        C.close()

    def RV(v, pattern, **kw):
        return V(v.ap.rearrange(pattern, **kw), v.d)

    def phase_rwkv(l, s):
        C = Ctx(nc, S)
        RA = Ring([C.ps([128, 512], F32, "ra") for _ in range(6)])
        pstc = RA.tiles[0]
        stg = C.sb([64, 128], F32, "stg")
        mu = C.sb([128, 3, 14], F32, "mu")
        for j in range(2):
            S.memset(stg[0:14, :], 0.0)
            S.dma(stg[0:13, :], V(P["shift_mu"][l, j, 0:1664].rearrange("(c p) -> c p", p=128), None))
            S.dma(stg[13:14, 0:96], V(P["shift_mu"][l, j, 1664:1760].rearrange("(c p) -> c p", p=96), None))
            S.tr(pstc[:, 0:14], stg[0:14, :], ident[0:14, 0:14])
            S.copy(mu[:, 1 + j, :], pstc[:, 0:14])
        S.tt(mu[:, 0, :], mu[:, 1, :], mu[:, 2, :], ALU.add)
        S.ts(mu[:, 0, :], mu[:, 0, :], -1.0, 1.0, ALU.mult, ALU.add)
        PRM = C.sb([128, 12, 4], F32, "prm")
        srcs = [P["w0"][l, 0], P["w0"][l, 1], P["a0"][l, 0], P["a0"][l, 1], P["k_k"][l], P["k_a"][l], P["r_k"][l],
                P["gn_g"][l], P["gn_b"][l], P["vres_0"][0]]
        for i, sap in enumerate(srcs):
            load_cols(C, PRM[:, i, :], sap, 4, pstc, stg)
        S.ts(PRM[:, 10, :], PRM[:, 5, :], -1.0, 1.0, ALU.mult, ALU.add)
        S.ts(PRM[:, 11, :], PRM[:, 5, :], -2.0, 2.0, ALU.mult, ALU.add)
        dup = C.sb([64, AW], BF16, "dup")
        iup = [C.sb([128, AW], BF16, "iup") for _ in range(2)]
        gup = C.sb([96, AW], BF16, "gup")
        vdn = C.sb([128, 4, 32], BF16, "vdn")
        vup = C.sb([32, AW], BF16, "vup")
        S.dma(dup[:, :], V(P["decay_up"][l].rearrange("d r c -> (d r) c"), None), q="pool")
        for dd in range(2):
            S.memset(iup[dd][64:128, :], 0.0)
            S.dma(iup[dd][64 + 32 * dd:96 + 32 * dd, :], V(P["iclr_up"][l, dd], None), q="pool")
        S.dma(gup[:, :], V(P["g_up"][l], None), q="pool")
        S.dma(vdn[:, :, :], V(P["vres_down"][0].rearrange("(f p) r -> p f r", p=128), None), q="pool")
        S.dma(vup[:, :], V(P["vres_up"][0], None), q="pool")
        blkm = C.sb([128, 128], BF16, "blkm")
        S.ts(blkm[:, :], blk[:, :], 1.0 / 64.0, None, ALU.mult)
        idblk = C.sb([128, 64], F32, "idblk")
        S.tt(idblk[:, :], ident[:, 0:64], ident[:, 64:128], ALU.add)
        ones = C.sb([128, 128], F32, "ones")
        S.memset(ones[:, :], 1.0)
        eps12 = C.sb([128, 1], F32, "eps12")
        S.memset(eps12[:, :], 1e-12)
        epsgn = C.sb([128, 1], F32, "epsgn")
        S.memset(epsgn[:, :], GN_EPS)
        MS2 = C.sb([128, 2, 128], F32, "ms2")
        MI2 = C.sb([128, 2, 128], F32, "mi2")
        MT2 = C.sb([128, 2, 128], F32, "mt2")

        _stage(1)
        XIN = Ring([C.sb([128, 14, 130], F32, "xin") for _ in range(2)])
        SH = [[C.sb([128, 128], F32, "sh") for _ in range(14)] for _ in range(2)]
        PTBR = Ring([C.ps([128, 1024], BF16, "ptb") for _ in range(2)])
        TWt = C.sb([64, 128], BF16, "tw")
        ADt = C.sb([128, 128], BF16, "ad")
        SGt = C.sb([96, 128], BF16, "sg")
        VD = C.sb([32, 128], BF16, "vd")

        def fset(n, dt, shape=(128, 128), name="f"):
            return [[C.sb(list(shape), dt, name) for _ in range(4)] for _ in range(n)]
        (Ee, Cu, La, Gm, Gi, Gp, Ge, K0, Kk, Bb, Kd, T1, Yy, Dl, Yn, T2) = fset(16, F32)
        A0 = [fset(1, F32)[0] for _ in range(2)]
        A1 = [fset(1, F32)[0] for _ in range(2)]
        Vv = [fset(1, F32)[0] for _ in range(2)]
        SMs = [[C.sb([128, 8], F32, "sms") for _ in range(4)] for _ in range(2)]
        (SQ, BH, KH, VB16, RH, YB, RKV, OA) = fset(8, BF16)
        AT = [fset(1, BF16)[0] for _ in range(2)]
        VT = [fset(1, BF16)[0] for _ in range(2)]
        AR = [[C.sb([128, 256], BF16, "ar") for _ in range(4)] for _ in range(2)]
        BK = [[C.sb([128, 256], BF16, "bk") for _ in range(4)] for _ in range(2)]
        DI = [[C.sb([128, 2, 192], BF16, "di") for _ in range(4)] for _ in range(2)]
        DX = [[C.sb([128, 2, 192], BF16, "dx") for _ in range(4)] for _ in range(2)]
        GQa = [[C.sb([128, 2, 192], BF16, "gqa") for _ in range(4)] for _ in range(2)]
        GQ = [C.sb([128, 2, 192], BF16, "gq") for _ in range(4)]
        MTAK = [C.sb([128, 2, 128], BF16, "mtak") for _ in range(4)]
        PW = [[C.sb([128, 2, 2, 128], BF16, "pw") for _ in range(4)] for _ in range(3)]
        PM = [C.sb([128, 64], BF16, "pm") for _ in range(4)]
        ST = [[C.sb([128, 64], BF16, "st") for _ in range(4)] for _ in range(2)]
        VF = [C.sb([128, 128], F32, "vf") for _ in range(4)]
        YF = [C.sb([128, 128], F32, "yf") for _ in range(4)]

        def rr(gens):
            gens = list(gens)
            while gens:
                for g in list(gens):
                    try:
                        next(g)
                    except StopIteration:
                        gens.remove(g)
                yield

        def prep(d, ci, par):
            c0 = ci * 128
            xin = XIN.next()
            sh = SH[par]
            lo = max(c0 - 1, 0)
            hi = min(c0 + 129, T)
            if c0 == 0:
                S.memset(xin[:, :, 0:1], 0.0, eng="pool")
            if c0 + 129 > T:
                S.memset(xin[:, :, 129:130], 0.0, eng="pool")
            o0 = lo - (c0 - 1)
            S.dma(xin[:, :, o0:o0 + (hi - lo)], V(pa.t[:, lo:hi].rearrange("(r p) t -> p r t", p=128), pa.d))
            for rt in range(14):
                S.act(sh[rt][:, :], xin[:, rt, 1:129], AF.Copy, scale=mu[:, 0, rt:rt + 1])
                if rt % 4 == 3:
                    yield
            for rt in range(14):
                S.stt(sh[rt][:, :], xin[:, rt, 0:128], mu[:, 1, rt:rt + 1], sh[rt][:, :], ALU.mult, ALU.add)
                if rt % 4 == 3:
                    yield
            for rt in range(14):
                S.stt(sh[rt][:, :], xin[:, rt, 2:130], mu[:, 2, rt:rt + 1], sh[rt][:, :], ALU.mult, ALU.add)
                if rt % 4 == 3:
                    yield
            r_ = sh[0:4]; k_ = sh[4:8]; v_ = sh[8:12]
            S.act(TWt[32 * d:32 * d + 32, :], sh[12][32 * d:32 * d + 32, :], AF.Tanh)
            S.copy(ADt[64:128, :], sh[12][64:128, :], eng="pool")
            def decay_chain(ft):
                fs = slice(ft * 128, (ft + 1) * 128)
                pz = RA.next()
                S.mm(pz[:, 0:128], dup[32 * d:32 * d + 32, fs], TWt[32 * d:32 * d + 32, :])
                S.act(Ee[ft][:, :], pz[:, 0:128], AF.Sigmoid, bias=PRM[:, d, ft:ft + 1], scale=1.0)
                yield
                for dd in ((0,) if d == 0 else (0, 1)):
                    pq = RA.next()
                    S.mm(pq[:, 0:128], iup[dd][64:128, fs], ADt[64:128, :])
                    S.act((A0 if dd == 0 else A1)[par][ft][:, :], pq[:, 0:128], AF.Sigmoid, bias=PRM[:, 2 + dd, ft:ft + 1], scale=1.0)
                S.op("dve", lambda e, ft=ft: e.tensor_tensor_scan(out=Cu[ft].t[:, :], data0=ones.t[:, :], data1=Ee[ft].t[:, :],
                                                                   initial=0.0, op0=ALU.mult, op1=ALU.add),
                     [ones.d, Ee[ft].d], [Cu[ft].d])
                yield
                sm = SMs[par][ft]
                if d == 0:
                    lam_t = Cu[ft]
                else:
                    S.stt(La[ft][:, :], Cu[ft][:, :], -1.0, Ee[ft][:, :], ALU.mult, ALU.add)
                    yield
                    S.ts(La[ft][:, :], La[ft][:, :], Cu[ft][:, 127:128], None, ALU.add)
                    lam_t = La[ft]
                S.ts(sm[:, 0:1], Cu[ft][:, 127:128], -CEXP, None, ALU.mult)
                yield
                S.act(sm[:, 1:2], Cu[ft][:, 127:128], AF.Exp, scale=-CEXP)
                S.act(Gm[ft][:, :], lam_t[:, :], AF.Exp, scale=-CEXP)
                S.tt(Gp[ft][:, :], lam_t[:, :], Ee[ft][:, :], ALU.subtract, eng="pool")
                yield
                S.act(Gi[ft][:, :], lam_t[:, :], AF.Exp, scale=CEXP)
                S.act(Ge[ft][:, :], lam_t[:, :], AF.Exp, bias=sm[:, 0:1], scale=CEXP)
                yield
                S.act(Gp[ft][:, :], Gp[ft][:, :], AF.Exp, scale=-CEXP)
                yield

            yield from rr([decay_chain(ft) for ft in range(4)])
            for ft in range(4):
                S.copy(Vv[par][ft][:, :], v_[ft][:, :], eng="pool")
            if l == 1:
                pvd = RA.next()
                for ft in range(4):
                    S.copy(VB16[ft][:, :], Vv[par][ft][:, :], eng="pool")
                    S.mm(pvd[0:32, 0:128], vdn[:, ft, :], VB16[ft][:, :], start=(ft == 0), stop=(ft == 3))
                S.copy(VD[:, :], pvd[0:32, 0:128], eng="act")
                yield
                def vres_chain(ft):
                    fs = slice(ft * 128, (ft + 1) * 128)
                    S.dma(VF[ft][:, :], vfs[ft * 128:(ft + 1) * 128, c0:c0 + 128])
                    pg = RA.next()
                    S.mm(pg[:, 0:128], vup[:, fs], VD[:, :])
                    S.act(T1[ft][:, :], pg[:, 0:128], AF.Sigmoid, bias=PRM[:, 9, ft:ft + 1], scale=1.0)
                    S.tt(Dl[ft][:, :], VF[ft][:, :], Vv[par][ft][:, :], ALU.subtract, eng="pool")
                    yield
                    S.tt(Dl[ft][:, :], Dl[ft][:, :], T1[ft][:, :], ALU.mult, eng="pool")
                    yield
                    S.tt(Vv[par][ft][:, :], Vv[par][ft][:, :], Dl[ft][:, :], ALU.add, eng="pool")
                    yield

                yield from rr([vres_chain(ft) for ft in range(4)])
            elif d == 0:
                for ft in range(4):
                    S.dma(vfs[ft * 128:(ft + 1) * 128, c0:c0 + 128], Vv[par][ft][:, :])
            def kk_chain(ft):
                S.ts(K0[ft][:, :], k_[ft][:, :], PRM[:, 4, ft:ft + 1], None, ALU.mult)
                S.copy(VB16[ft][:, :], Vv[par][ft][:, :], eng="pool")
                yield
                S.act(SQ[ft][:, :], K0[ft][:, :], AF.Square)
                yield
                Ad = (A0 if d == 0 else A1)[par][ft]
                S.ts(T1[ft][:, :], Ad[:, :], PRM[:, 5, ft:ft + 1], PRM[:, 10, ft:ft + 1], ALU.mult, ALU.add)
                S.tt(AR[par][ft][:, 128:256], r_[ft][:, :], Gm[ft][:, :], ALU.mult, eng="pool")
                yield
                pss = RA.next()
                S.mm(pss[:, 0:128], blk[:, :], SQ[ft][:, :])
                S.act(Kk[ft][:, :], pss[:, 0:128], AF.Sqrt, bias=eps12[:, 0:1], scale=1.0)
                S.tt(Kd[ft][:, :], k_[ft][:, :], T1[ft][:, :], ALU.mult, eng="pool")
                yield
                S.op("dve", lambda e, ft=ft: e.reciprocal(out=Kk[ft].t[:, :], in_=Kk[ft].t[:, :]), [Kk[ft].d], [Kk[ft].d])
                S.tt(BK[par][ft][:, 128:256], Kd[ft][:, :], Gi[ft][:, :], ALU.mult, eng="pool")
                yield
                S.tt(Kk[ft][:, :], Kk[ft][:, :], K0[ft][:, :], ALU.mult)
                S.tt(KH[ft][:, :], Kd[ft][:, :], Ge[ft][:, :], ALU.mult, eng="pool")
                yield
                S.stt(AR[par][ft][:, 0:128], Kk[ft][:, :], -1.0, Gp[ft][:, :], ALU.mult, ALU.mult)
                S.tt(Bb[ft][:, :], Kk[ft][:, :], Ad[:, :], ALU.mult)
                yield
                S.tt(BK[par][ft][:, 0:128], Bb[ft][:, :], Gi[ft][:, :], ALU.mult, eng="pool")
                S.tt(BH[ft][:, :], Bb[ft][:, :], Ge[ft][:, :], ALU.mult)
                yield

            yield from rr([kk_chain(ft) for ft in range(4)])
            for ft in range(4):
                PTB = PTBR.next()
                S.tr(PTB[:, 0:128], AR[par][ft][:, 0:128], identb[:, :])
                S.tr(PTB[:, 128:256], BH[ft][:, :], identb[:, :])
                S.tr(PTB[:, 256:384], KH[ft][:, :], identb[:, :])
                S.tr(PTB[:, 384:512], VB16[ft][:, :], identb[:, :])
                S.copy(AT[par][ft][:, :], PTB[:, 0:128], eng="dve")
                for h in range(2):
                    S.copy(DI[par][ft][:, h, 128:192], PTB[:, 128 + h * 64:192 + h * 64], eng="dve")
                for h in range(2):
                    S.copy(GQa[par][ft][:, h, 128:192], PTB[:, 256 + h * 64:320 + h * 64], eng="dve")
                S.copy(VT[par][ft][:, :], PTB[:, 384:512], eng="dve")
                yield

        def mats(d, ci, par, sti):
            c0 = ci * 128
            sh = SH[par]
            r_ = sh[0:4]; k_ = sh[4:8]
            for ft in range(4):
                for h in range(2):
                    hb = h * 64
                    pA = RA.next()
                    S.mm(pA[:, 0:256], BK[par][ft][hb:hb + 64, 0:128], AR[par][ft][hb:hb + 64, :])
                    S.tt(PW[0][ft][:, h, 0, :], pA[:, 0:128], MS2[:, h, :], ALU.mult)
                    S.tt(DI[par][ft][:, h, 0:128], pA[:, 128:256], MI2[:, h, :], ALU.mult)
                    pB = RA.next()
                    S.mm(pB[:, 0:128], BK[par][ft][hb:hb + 64, 128:256], AR[par][ft][hb:hb + 64, 128:256])
                    S.tt(GQa[par][ft][:, h, 0:128], pB[:, 0:128], MI2[:, h, :], ALU.mult)
                    pC = RA.next()
                    S.mm(pC[:, 0:256], AR[par][ft][hb:hb + 64, 0:128], BK[par][ft][hb:hb + 64, :])
                    S.tt(PW[0][ft][:, h, 1, :], pC[:, 0:128], MT2[:, h, :], ALU.mult)
                    S.tt(MTAK[ft][:, h, :], pC[:, 128:256], MT2[:, h, :], ALU.mult)
                yield
            Dc = [DI[par][ft] for ft in range(4)]
            for j in range(7):
                pwc = PW[j % 3]
                pwn = PW[(j + 1) % 3]
                for ft in range(4):
                    if j < 6:
                        pP = RA.next()
                        for h in range(2):
                            S.mm(pP[:, h * 256:h * 256 + 128], pwc[ft][:, h, 1, :], pwc[ft][:, h, 0, :])
                            S.mm(pP[:, h * 256 + 128:h * 256 + 256], pwc[ft][:, h, 0, :], pwc[ft][:, h, 1, :])
                        S.copy(RV(pwn[ft][:, :, :, :], "p h t c -> p (h t c)"), pP[:, :], eng="act")
                    pD = RA.next()
                    for h in range(2):
                        S.mm(pD[:, h * 192:(h + 1) * 192], identb[:, :], Dc[ft][:, h, :], start=True, stop=False)
                        S.mm(pD[:, h * 192:(h + 1) * 192], pwc[ft][:, h, 1, :], Dc[ft][:, h, :], start=False, stop=True)
                    Dn = DX[j % 2][ft]
                    S.copy(RV(Dn[:, :, :], "p h c -> p (h c)"), pD[:, 0:384], eng="dve")
                    Dc[ft] = Dn
                    yield
            stc = ST[sti % 2]
            stn = ST[(sti + 1) % 2]
            for ft in range(4):
                DF = Dc[ft]
                sm = SMs[par][ft]
                pR = RA.next()
                for h in range(2):
                    hb = h * 64
                    S.mm(pR[hb:hb + 64, 0:192], AT[par][ft][:, hb:hb + 64], DF[:, h, :])
                S.tt(RH[ft][:, :], pR[:, 0:128], AR[par][ft][:, 128:256], ALU.add)
                S.stt(PM[ft][:, :], idblk[:, :], sm[:, 1:2], pR[:, 128:192], ALU.mult, ALU.add)
                pG = RA.next()
                for h in range(2):
                    S.mm(pG[:, h * 192:(h + 1) * 192], MTAK[ft][:, h, :], DF[:, h, :])
                S.tt(RV(GQ[ft][:, :, :], "p h c -> p (h c)"), pG[:, 0:384], RV(GQa[par][ft][:, :, :], "p h c -> p (h c)"), ALU.add)
                yield
            for ft in range(4):
                if d == 1:
                    S.dma(YF[ft][:, :], yfs[ft * 128:(ft + 1) * 128, c0:c0 + 128])
                for h in range(2):
                    hb = h * 64
                    pY = RA.next()
                    pS = RA.next()
                    S.mm(pY[hb:hb + 64, 0:128], stc[ft][hb:hb + 64, :], RH[ft][hb:hb + 64, :], start=True, stop=False)
                    S.mm(pY[hb:hb + 64, 0:128], VT[par][ft][:, hb:hb + 64], GQ[ft][:, h, 0:128], start=False, stop=True)
                    S.mm(pS[hb:hb + 64, 0:64], PM[ft][hb:hb + 64, :], stc[ft][hb:hb + 64, :], start=True, stop=False)
                    S.mm(pS[hb:hb + 64, 0:64], GQ[ft][:, h, 128:192], VT[par][ft][:, hb:hb + 64], start=False, stop=True)
                    S.copy(stn[ft][hb:hb + 64, :], pS[hb:hb + 64, 0:64], eng="act")
                    if d == 0:
                        S.copy(Yy[ft][hb:hb + 64, :], pY[hb:hb + 64, 0:128], eng="dve")
                    else:
                        S.tt(Yy[ft][hb:hb + 64, :], pY[hb:hb + 64, 0:128], YF[ft][hb:hb + 64, :], ALU.add)
                yield
                if d == 0:
                    S.dma(yfs[ft * 128:(ft + 1) * 128, c0:c0 + 128], Yy[ft][:, :])
            if d == 0:
                return

            def epi_chain(ft):
                fs = slice(ft * 128, (ft + 1) * 128)
                S.copy(YB[ft][:, :], Yy[ft][:, :], eng="pool")
                S.tt(T2[ft][:, :], A0[par][ft][:, :], A1[par][ft][:, :], ALU.add, eng="pool")
                yield
                pm_ = RA.next()
                S.mm(pm_[:, 0:128], blkm[:, :], YB[ft][:, :])
                S.tt(Dl[ft][:, :], Yy[ft][:, :], pm_[:, 0:128], ALU.subtract)
                S.ts(T2[ft][:, :], T2[ft][:, :], PRM[:, 5, ft:ft + 1], PRM[:, 11, ft:ft + 1], ALU.mult, ALU.add)
                yield
                S.act(SQ[ft][:, :], Dl[ft][:, :], AF.Square)
                S.tt(T2[ft][:, :], T2[ft][:, :], k_[ft][:, :], ALU.mult, eng="pool")
                yield
                pv_ = RA.next()
                S.mm(pv_[:, 0:128], blkm[:, :], SQ[ft][:, :])
                S.act(Yn[ft][:, :], pv_[:, 0:128], AF.Sqrt, bias=epsgn[:, 0:1], scale=1.0)
                S.stt(RKV[ft][:, :], r_[ft][:, :], PRM[:, 6, ft:ft + 1], T2[ft][:, :], ALU.mult, ALU.mult)
                yield
                S.op("dve", lambda e, ft=ft: e.reciprocal(out=Yn[ft].t[:, :], in_=Yn[ft].t[:, :]), [Yn[ft].d], [Yn[ft].d])
                pb_ = RA.next()
                S.mm(pb_[:, 0:128], blk[:, :], RKV[ft][:, :])
                S.tt(T2[ft][:, :], pb_[:, 0:128], Vv[par][ft][:, :], ALU.mult)
                yield
                S.tt(Yn[ft][:, :], Yn[ft][:, :], Dl[ft][:, :], ALU.mult)
                yield
                S.ts(Yn[ft][:, :], Yn[ft][:, :], PRM[:, 7, ft:ft + 1], PRM[:, 8, ft:ft + 1], ALU.mult, ALU.add)
                yield
                S.tt(Yn[ft][:, :], Yn[ft][:, :], T2[ft][:, :], ALU.add)
                pg_ = RA.next()
                S.mm(pg_[:, 0:128], gup[:, fs], SGt[:, :])
                S.tt(OA[ft][:, :], Yn[ft][:, :], pg_[:, 0:128], ALU.mult)
                S.dma(mix[ft * 128:(ft + 1) * 128, c0:c0 + 128], OA[ft][:, :])
                yield

            S.act(SGt[:, :], sh[13][0:96, :], AF.Sigmoid)
            yield from rr([epi_chain(ft) for ft in range(4)])

        def run_both(a, b):
            alive = [g for g in (a, b) if g is not None]
            while alive:
                for g in list(alive):
                    try:
                        next(g)
                    except StopIteration:
                        alive.remove(g)

        for d in range(2):
            iLs, iLi, iLt = (0, 1, 2) if d == 0 else (2, 3, 0)
            for h in range(2):
                S.copy(MS2[:, h, :], masks[:, iLs, :])
                S.copy(MI2[:, h, :], masks[:, iLi, :])
                S.copy(MT2[:, h, :], masks[:, iLt, :])
            for ft in range(4):
                S.memset(ST[0][ft][:, :], 0.0)
            order = list(range(NCH)) if d == 0 else list(range(NCH - 1, -1, -1))
            gm = None
            for n, ci in enumerate(order):
                run_both(prep(d, ci, n % 2), gm)
                gm = mats(d, ci, n % 2, n)
            run_both(gm, None)
            S.barrier()
        C.close()

    PH = {"proj": phase_proj, "setup": phase_setup, "attn": phase_attn, "wout": phase_wout, "ffn": phase_ffn,
          "rwkv": phase_rwkv}
    if plan is None:
        plan = [("setup", 0, 0)]
        for s in range(NSEQ):
            for l in range(DEPTH):
                plan += [("proj", l, s), ("rwkv", l, s), ("attn", l, s), ("wout", l, s), ("ffn", l, s)]
    for (name, l, s) in plan:
        try:
            PH[name](l, s)
        except StopBuild:
            print('STOPPED')
    S.barrier()
    return nc


def _t5_bucket_np(rel):
    nb = 16
    max_exact = 8
    ret = np.where(rel > 0, nb, 0)
    n = np.abs(rel)
    nf = np.maximum(n, 1).astype(np.float32)
    large = max_exact + (np.log(nf / np.float32(max_exact)) / np.float32(math.log(128 / max_exact))
                         * np.float32(nb - max_exact)).astype(np.int32)
    large = np.minimum(large, nb - 1)
    return ret + np.where(n < max_exact, n, large)


def _consts():
    ident = np.eye(128, dtype=np.float32)
    a = np.arange(128)[:, None]
    b = np.arange(128)[None, :]
    masks = np.stack([(a < b), (a <= b), (a > b), (a >= b)], 1).astype(np.float32)
    blk = np.zeros((128, 128), np.float32)
    blk[:64, :64] = 1
    blk[64:, 64:] = 1
    delta = 639 - np.arange(NB_LEN + 1)
    bk = _t5_bucket_np(delta.astype(np.int32))
    oh = np.zeros((32, NB_LEN + 1), np.float32)
    oh[bk, np.arange(NB_LEN + 1)] = 1
    return dict(c_ident=ident, c_masks=np.ascontiguousarray(masks), c_blk=blk, c_onehot=oh)


_NC_CACHE = {}


def kernel(**inputs):
    xp = np.asarray(inputs["x_prompt"], np.float32)
    xs = np.asarray(inputs["x_sample"], np.float32)
    T = xp.shape[1]
    allx = [xp[i] for i in range(xp.shape[0])] + [xs[i] for i in range(xs.shape[0])]
    n = len(allx)
    NC = 8
    slots = [(c, 8 + (c % 4)) for c in range(NC)]
    key = (T, 2)
    if key not in _NC_CACHE:
        _NC_CACHE[key] = build(T, 2)
    nc = _NC_CACHE[key]
    base = {k: np.ascontiguousarray(np.asarray(v, np.float32)) for k, v in inputs.items()
            if k not in ("x_prompt", "x_sample")}
    base["r_k"] = base["r_k"].reshape(2, AW)
    base.update(_consts())
    in_maps = []
    for c in range(NC):
        m = dict(base)
        m["x"] = np.ascontiguousarray(np.stack([allx[slots[c][0]], allx[slots[c][1]]], 0))
        in_maps.append(m)
    res = run_bass_kernel_spmd(nc, in_maps, core_ids=list(range(NC)))
    outs = [None] * n
    for c in range(NC):
        y = res.results[c]["y"]
        outs[slots[c][0]] = y[0]
        if c < 4:
            outs[slots[c][1]] = y[1]
    yp = np.stack(outs[:xp.shape[0]], 0).astype(np.float32)
    ys = np.stack(outs[xp.shape[0]:], 0).astype(np.float32)
    return (yp, ys)
```

```python
import contextlib
import os
import math
import numpy as np
import concourse.bass as bass
import concourse.mybir as mybir
from concourse.bass_utils import run_bass_kernel_spmd

F32 = mybir.dt.float32
BF16 = mybir.dt.bfloat16
AF = mybir.ActivationFunctionType
ALU = mybir.AluOpType

D = 1024
DEPTH = 2
AW = 512
A_COLS = 1760
P_COLS = 3296
D_FF = 2816
LN_EPS = 1e-5
GN_EPS = 64e-5
ALPHA = (2 * DEPTH) ** 0.25
NB_LEN = 1279
CEXP = math.exp(-0.5)


class StopBuild(Exception):
    pass


def _stage(n):
    if int(os.environ.get('RWKV_STOP', '99')) == n:
        raise StopBuild()


class Dep:
    __slots__ = ("w", "r")

    def __init__(self):
        self.w = {}
        self.r = {}


class V:
    __slots__ = ("ap", "d")

    def __init__(self, ap, d):
        self.ap = ap
        self.d = d


class Tile:
    def __init__(self, t, d=None):
        self.t = t
        self.d = d if d is not None else Dep()

    def __getitem__(self, idx):
        return V(self.t[idx], self.d)


class Sched:
    ENG = ("pe", "act", "dve", "pool", "sp")

    def __init__(self, nc, n_dma=28, n_sw=10):
        self.nc = nc
        self.eng = {"pe": nc.tensor, "act": nc.scalar, "dve": nc.vector, "pool": nc.gpsimd, "sp": nc.sync}
        self.sem = {e: nc.semaphore("s_" + e).__enter__() for e in self.ENG}
        self.cnt = {e: 0 for e in self.ENG}
        self.dsem = [nc.semaphore("d%d" % i).__enter__() for i in range(n_dma)]
        self.dcnt = [0] * n_dma
        self.dnext = 0
        self.n_sw = n_sw
        self.swnext = 0
        self.waited = {e: {} for e in self.ENG}
        self.uid = 0

    def _s(self, k):
        return self.sem[k] if isinstance(k, str) else self.dsem[k]

    def op(self, eng, fn, reads=(), writes=(), dma=False):
        need = {}
        for d in reads:
            if d is None:
                continue
            for k, v in d.w.items():
                if need.get(k, 0) < v:
                    need[k] = v
        for d in writes:
            if d is None:
                continue
            for k, v in d.w.items():
                if k == eng and not dma:
                    continue
                if need.get(k, 0) < v:
                    need[k] = v
            for k, v in d.r.items():
                if k == eng and not dma:
                    continue
                if need.get(k, 0) < v:
                    need[k] = v
        if dma:
            if eng == "pool":
                j = self.swnext
                self.swnext = (j + 1) % self.n_sw
            else:
                j = self.n_sw + self.dnext
                self.dnext = (self.dnext + 1) % (len(self.dsem) - self.n_sw)
            if self.dcnt[j] > 0 and need.get(j, 0) < self.dcnt[j]:
                need[j] = self.dcnt[j]
            self.dcnt[j] += 16
            tk = (j, self.dcnt[j])
        else:
            self.cnt[eng] += 1
            tk = (eng, self.cnt[eng])
        E = self.eng[eng]
        wd = self.waited[eng]
        for k, v in need.items():
            if k == "pe" and eng == "pe" and not dma:
                continue
            if wd.get(k, 0) >= v:
                continue
            wd[k] = v
            E.wait_ge(self._s(k), v)
        ins = fn(E)
        ins.then_inc(self._s(tk[0]), 16 if dma else 1)
        for d in reads:
            if d is not None:
                d.r[tk[0]] = tk[1]
        for d in writes:
            if d is not None:
                d.w[tk[0]] = tk[1]
                d.r = {}
        return tk

    def barrier(self):
        for e in self.ENG:
            E = self.eng[e]
            wd = self.waited[e]
            for k in self.ENG:
                if k != e and self.cnt[k] > wd.get(k, 0):
                    wd[k] = self.cnt[k]
                    E.wait_ge(self.sem[k], self.cnt[k])
            for j in range(len(self.dsem)):
                if self.dcnt[j] > wd.get(j, 0):
                    wd[j] = self.dcnt[j]
                    E.wait_ge(self.dsem[j], self.dcnt[j])

    def dma(self, out, in_, q="sp", **kw):
        return self.op(q, lambda e: e.dma_start(out=out.ap, in_=in_.ap, **kw), [in_.d], [out.d], dma=True)

    def mm(self, out, lhsT, rhs, start=True, stop=True, extra_reads=()):
        return self.op("pe", lambda e: e.matmul(out.ap, lhsT=lhsT.ap, rhs=rhs.ap, start=start, stop=stop,
                                                skip_group_check=True),
                       [lhsT.d, rhs.d] + list(extra_reads), [out.d])

    def tr(self, out, in_, ident):
        return self.op("pe", lambda e: e.transpose(out.ap, in_.ap, ident.ap), [in_.d, ident.d], [out.d])

    def act(self, out, in_, func, bias=None, scale=None, accum=None, eng="act"):
        kw = {}
        rd = [in_.d]
        wr = [out.d]
        if bias is not None:
            if isinstance(bias, V):
                kw["bias"] = bias.ap
                rd.append(bias.d)
            else:
                kw["bias"] = bias
        if scale is not None:
            if isinstance(scale, V):
                kw["scale"] = scale.ap
                rd.append(scale.d)
            else:
                kw["scale"] = scale
        if accum is not None:
            kw["accum_out"] = accum.ap
            wr.append(accum.d)
        return self.op("act", lambda e: e.activation(out=out.ap, in_=in_.ap, func=func, **kw), rd, wr)

    def tt(self, out, a, b, op, eng="dve"):
        return self.op(eng, lambda e: e.tensor_tensor(out=out.ap, in0=a.ap, in1=b.ap, op=op), [a.d, b.d], [out.d])

    def ts(self, out, a, s1, s2, op0, op1=None, eng="dve"):
        rd = [a.d]
        x1, x2 = s1, s2
        if isinstance(s1, V):
            rd.append(s1.d)
            x1 = s1.ap
        if isinstance(s2, V):
            rd.append(s2.d)
            x2 = s2.ap
        if op1 is None:
            return self.op(eng, lambda e: e.tensor_scalar(out=out.ap, in0=a.ap, scalar1=x1, scalar2=None, op0=op0),
                           rd, [out.d])
        return self.op(eng, lambda e: e.tensor_scalar(out=out.ap, in0=a.ap, scalar1=x1, scalar2=x2, op0=op0, op1=op1),
                       rd, [out.d])

    def stt(self, out, a, s, b, op0, op1):
        rd = [a.d, b.d]
        x = s
        if isinstance(s, V):
            rd.append(s.d)
            x = s.ap
        return self.op("dve", lambda e: e.scalar_tensor_tensor(out=out.ap, in0=a.ap, scalar=x, in1=b.ap, op0=op0, op1=op1),
                       rd, [out.d])

    def copy(self, out, in_, eng="dve"):
        if eng == "act":
            return self.act(out, in_, AF.Copy)
        return self.op(eng, lambda e: e.tensor_copy(out=out.ap, in_=in_.ap), [in_.d], [out.d])

    def memset(self, out, val, eng="dve"):
        return self.op(eng, lambda e: e.memset(out.ap, val), [], [out.d])


class Ctx:
    def __init__(self, nc, S):
        self.nc = nc
        self.S = S
        self.es = contextlib.ExitStack()
        self.n = 0

    def sb(self, shape, dt, name="t"):
        self.S.uid += 1
        return Tile(self.es.enter_context(self.nc.sbuf_tensor("%s_%d" % (name, self.S.uid), list(shape), dt)))

    def ps(self, shape, dt=F32, name="p"):
        self.S.uid += 1
        return Tile(self.es.enter_context(self.nc.psum_tensor("%s_%d" % (name, self.S.uid), list(shape), dt)))

    def close(self):
        self.S.barrier()
        self.es.close()


class Ring:
    def __init__(self, tiles):
        self.tiles = tiles
        self.i = 0

    def next(self):
        t = self.tiles[self.i % len(self.tiles)]
        self.i += 1
        return t


def bcast_rows(ap1d, n, parts=128):
    return bass.AP(tensor=ap1d.tensor, offset=ap1d.offset, ap=[[0, parts], [1, n]])


def build(T, NSEQ, dbg=False, plan=None):
    nc = bass.Bass("TRN2", target_bir_lowering=False)
    S = Sched(nc)
    NT5 = T // 512
    NCH = T // 128

    def din(name, shape, dt=F32):
        return nc.dram_tensor(name, list(shape), dt, kind="ExternalInput").ap()

    def dscr(name, shape, dt=F32):
        kind = "ExternalOutput" if dbg else "Internal"
        return Tile(nc.dram_tensor(name, list(shape), dt, kind=kind).ap())

    x_in = din("x", [NSEQ, T, D])
    y_out = Tile(nc.dram_tensor("y", [NSEQ, T, D], F32, kind="ExternalOutput").ap())
    P = {}
    for name, shape in [("ln_in_g", [D]), ("ln_in_b", [D]), ("w_in", [2, D, P_COLS]), ("shift_mu", [2, 2, A_COLS]),
                        ("w0", [2, 2, AW]), ("decay_up", [2, 2, 32, AW]), ("a0", [2, 2, AW]), ("iclr_up", [2, 2, 32, AW]),
                        ("g_up", [2, 96, AW]), ("k_k", [2, AW]), ("k_a", [2, AW]), ("r_k", [2, AW]), ("gn_g", [2, AW]),
                        ("gn_b", [2, AW]), ("vres_down", [1, AW, 32]), ("vres_up", [1, 32, AW]), ("vres_0", [1, AW]),
                        ("lam", [2, 4, 64]), ("subln_g", [2, 128]), ("rel_bias", [32, 4]), ("w_out", [2, D, D]),
                        ("ln1_g", [2, D]), ("ln1_b", [2, D]), ("ffn_up", [2, D, 2 * D_FF]), ("conv_w", [2, 3, 2 * D_FF]),
                        ("conv_b", [2, 2 * D_FF]), ("ffn_down", [2, D_FF, D]), ("ln2_g", [2, D]), ("ln2_b", [2, D]),
                        ("c_ident", [128, 128]), ("c_masks", [128, 4, 128]), ("c_blk", [128, 128]),
                        ("c_onehot", [32, NB_LEN + 1])]:
        P[name] = din(name, shape)

    xres = dscr("xres", [T, D])
    pa = dscr("pa", [1792, T])
    qk = dscr("qk", [1024, T], BF16)
    vb = dscr("vb", [T, 512], BF16)
    mix = dscr("mix", [1024, T], BF16)
    x1s = dscr("x1s", [T, D])
    yfs = dscr("yfs", [512, T])
    vfs = dscr("vfs", [512, T])
    gsc = dscr("gsc", [4, NB_LEN + 1])
    hsc = dscr("hsc", [4, 128, 1536])
    upbf = dscr("upbf", [2, 44, 128, 8 * 128], BF16)

    G = Ctx(nc, S)
    ident = G.sb([128, 128], F32, "ident")
    identb = G.sb([128, 128], BF16, "identb")
    masks = G.sb([128, 4, 128], F32, "masks")
    blk = G.sb([128, 128], BF16, "blk")
    epsln = G.sb([128, 1], F32, "epsln")
    S.dma(ident[:, :], V(P["c_ident"], None))
    S.dma(identb[:, :], V(P["c_ident"], None), q="pool")
    S.dma(masks[:, :, :], V(P["c_masks"], None))
    S.dma(blk[:, :], V(P["c_blk"], None), q="pool")
    S.memset(epsln[:, :], LN_EPS)

    def layer_norm_rows(C, xt, g_bc, b_bc, out):
        st = C["st"].next()
        for c in range(2):
            S.op("dve", lambda e, c=c: e.bn_stats(out=st.t[:, c * 6:(c + 1) * 6], in_=xt.ap[:, c * 512:(c + 1) * 512]),
                 [xt.d], [st.d])
        S.op("dve", lambda e: e.bn_aggr(out=st.t[:, 12:14], in_=st.t[:, 0:12]), [st.d], [st.d])
        S.act(st[:, 14:15], st[:, 13:14], AF.Sqrt, bias=epsln[:, 0:1], scale=1.0)
        S.op("dve", lambda e: e.reciprocal(out=st.t[:, 15:16], in_=st.t[:, 14:15]), [st.d], [st.d])
        S.ts(out, xt, st[:, 12:13], st[:, 15:16], ALU.subtract, ALU.mult)
        S.tt(out, out, g_bc, ALU.mult)
        S.tt(out, out, b_bc, ALU.add)

    def load_w_bf16(dst, src_ap, KT, ncols, col0=0):
        v = src_ap.rearrange("(k p) c -> p k c", p=128)
        for k in range(KT):
            S.dma(dst[:, k, :], V(v[:, k, col0:col0 + ncols], None), q="pool")

    def phase_proj(l, s):
        C = Ctx(nc, S)
        w = C.sb([128, 8, P_COLS], BF16, "win")
        load_w_bf16(w, P["w_in"][l], 8, P_COLS)
        g_bc = C.sb([128, D], F32, "gbc")
        b_bc = C.sb([128, D], F32, "bbc")
        if l == 0:
            S.dma(g_bc[:, :], V(bcast_rows(P["ln_in_g"], D), None))
            S.dma(b_bc[:, :], V(bcast_rows(P["ln_in_b"], D), None))
        CC = {"st": Ring([C.sb([128, 16], F32, "st") for _ in range(4)])}
        xtk = Ring([C.sb([128, D], F32, "xtk") for _ in range(6)])
        xT = Ring([C.sb([128, 8, 512], BF16, "xT") for _ in range(2)])
        pst = Ring([C.ps([128, 512], F32, "pst") for _ in range(3)])
        psm = Ring([C.ps([128, 512], F32, "psm") for _ in range(4)])
        stA = Ring([C.sb([128, 512], F32, "stA") for _ in range(4)])
        stB = Ring([C.sb([128, 512], BF16, "stB") for _ in range(4)])
        ev = 0
        for t5 in range(NT5):
            t0 = t5 * 512
            xt_t = xT.next()
            blocks = []
            for j in range(4):
                xb = xtk.next()
                r0 = t0 + j * 128
                if l == 0:
                    S.dma(xb[:, :], V(x_in[s, r0:r0 + 128, :], None))
                    layer_norm_rows(CC, xb[:, :], g_bc[:, :], b_bc[:, :], xb[:, :])
                    S.dma(xres[r0:r0 + 128, :], xb[:, :], q="sp")
                else:
                    S.dma(xb[:, :], xres[r0:r0 + 128, :])
                blocks.append(xb)
            for k in range(8):
                p = pst.next()
                for j in range(4):
                    S.tr(p[:, j * 128:(j + 1) * 128], blocks[j][:, k * 128:(k + 1) * 128], ident[:, :])
                if k % 2 == 0:
                    S.copy(xt_t[:, k, :], p[:, :], eng="act")
                else:
                    S.copy(xt_t[:, k, :], p[:, :], eng="dve")
            for c in range(22):
                if c < 14:
                    c0 = c * 128
                    m = min(128, A_COLS - c0)
                else:
                    c0 = A_COLS + (c - 14) * 128
                    m = 128
                p = psm.next()
                for k in range(8):
                    S.mm(p[0:m, :], w[:, k, c0:c0 + m], xt_t[:, k, :], start=(k == 0), stop=(k == 7))
                if c < 14:
                    o = stA.next()
                    S.copy(o[0:m, :], p[0:m, :], eng=("act" if ev % 2 == 0 else "dve"))
                    S.dma(pa[c0:c0 + m, t0:t0 + 512], o[0:m, :])
                else:
                    o = stB.next()
                    S.copy(o[:, :], p[:, :], eng=("act" if ev % 2 == 0 else "dve"))
                    S.dma(qk[(c - 14) * 128:(c - 13) * 128, t0:t0 + 512], o[:, :])
                ev += 1
            for j in range(4):
                p = psm.next()
                for k in range(8):
                    S.mm(p[:, :], xt_t[:, k, j * 128:(j + 1) * 128], w[:, k, A_COLS + 1024:A_COLS + 1536],
                         start=(k == 0), stop=(k == 7))
                o = stB.next()
                S.copy(o[:, :], p[:, :], eng=("act" if ev % 2 == 0 else "dve"))
                ev += 1
                S.dma(vb[t0 + j * 128:t0 + (j + 1) * 128, :], o[:, :])
        C.close()

    def load_cols(C, dst, src1d, n, pst, stg):
        S.dma(stg[0:n, :], V(src1d.rearrange("(c p) -> c p", p=128), None))
        S.tr(pst[:, 0:n], stg[0:n, :], ident[0:n, 0:n])
        S.copy(dst, pst[:, 0:n])

    def phase_setup(l_, s_):
        C = Ctx(nc, S)
        for l in range(DEPTH):
            v = P["ffn_up"][l].rearrange("(k p) c -> p k c", p=128)
            for c in range(44):
                stg = None
                S.dma(V(upbf.t[l, c].rearrange("p (k c) -> p k c", k=8), upbf.d), V(v[:, :, c * 128:(c + 1) * 128], None),
                      q="pool")
        rb = C.sb([32, 4], F32, "rb")
        oh = C.sb([32, NB_LEN + 1], F32, "oh")
        gsb = C.sb([4, NB_LEN + 1], F32, "gsb")
        S.dma(rb[:, :], V(P["rel_bias"], None))
        S.dma(oh[:, :], V(P["c_onehot"], None))
        pp = C.ps([128, 512], F32, "pp")
        for c0 in (0, 512, 1024):
            n = min(512, NB_LEN + 1 - c0)
            S.mm(pp[0:4, 0:n], rb[:, :], oh[:, c0:c0 + n])
            S.copy(gsb[:, c0:c0 + n], pp[0:4, 0:n])
        S.dma(gsc[:, :], gsb[:, :])
        for h in range(4):
            dst = bass.AP(tensor=hsc.t.tensor, offset=hsc.t.offset + h * 128 * 1536, ap=[[1537, 128], [1, NB_LEN + 1]])
            src = bass.AP(tensor=gsc.t.tensor, offset=gsc.t.offset + h * (NB_LEN + 1), ap=[[0, 128], [1, NB_LEN + 1]])
            S.dma(V(dst, hsc.d), V(src, gsc.d))
        C.close()

    def phase_attn(l, s):
        C = Ctx(nc, S)
        lam_init = 0.8 - 0.6 * math.exp(-0.3 * l)
        lt = C.sb([1, 4, 64], F32, "lt")
        l2 = C.sb([1, 8], F32, "l2")
        ones1 = C.sb([1, 128], F32, "ones1")
        nlam = C.sb([128, 1], F32, "nlam")
        subg = C.sb([128, 128], F32, "subg")
        S.dma(lt[:, :, :], V(P["lam"][l:l + 1], None))
        S.memset(ones1[:, :], 1.0)
        S.tt(lt[:, 0, :], lt[:, 0, :], lt[:, 1, :], ALU.mult)
        S.tt(lt[:, 2, :], lt[:, 2, :], lt[:, 3, :], ALU.mult)
        S.op("dve", lambda e: e.reduce_sum(out=l2.t[:, 0:1], in_=lt.t[:, 0, :], axis=mybir.AxisListType.X), [lt.d], [l2.d])
        S.op("dve", lambda e: e.reduce_sum(out=l2.t[:, 1:2], in_=lt.t[:, 2, :], axis=mybir.AxisListType.X), [lt.d], [l2.d])
        S.act(l2[:, 2:4], l2[:, 0:2], AF.Exp)
        S.tt(l2[:, 4:5], l2[:, 3:4], l2[:, 2:3], ALU.subtract)
        S.ts(l2[:, 5:6], l2[:, 4:5], -lam_init, None, ALU.add)
        accA = [C.ps([128, 512], F32, "accA") for _ in range(2)]
        accB = [C.ps([128, 512], F32, "accB") for _ in range(2)]
        pl = accB[0]
        S.mm(pl[:, 0:1], ones1[:, :], l2[:, 5:6])
        S.copy(nlam[:, :], pl[:, 0:1])
        S.dma(subg[:, :], V(bcast_rows(P["subln_g"][l], 128), None))
        S.ts(subg[:, :], subg[:, :], 1.0 - lam_init, None, ALU.mult)

        KT = Ring([C.sb([128, T], BF16, "KT") for _ in range(2)])
        QT = Ring([[C.sb([128, T], BF16, "QT") for _ in range(2)] for _ in range(2)])
        VA = Ring([C.sb([128, NCH, 130], BF16, "VA") for _ in range(2)])
        BT = Ring([C.sb([128, 6, 512], F32, "BT") for _ in range(2)])
        CB = Ring([C.sb([128, 2], F32, "CB") for _ in range(2)])
        stR = Ring([C.ps([128, 512], F32, "st") for _ in range(3)])
        ptr = C.ps([128, 512], BF16, "ptr")
        ER = Ring([C.sb([128, 512], BF16, "E") for _ in range(4)])
        TMP = Ring([C.sb([128, 512], F32, "tmp") for _ in range(3)])
        ON = [C.sb([128, 4, 128], F32, "on") for _ in range(2)]
        SM = Ring([C.sb([128, 8], F32, "sm") for _ in range(8)])
        OB = Ring([C.sb([128, 128], BF16, "ob") for _ in range(3)])
        OF = Ring([C.sb([128, 128], F32, "of") for _ in range(3)])
        OST = Ring([C.sb([128, 512], BF16, "ost") for _ in range(2)])
        for h in range(4):
            kt_ = KT.next(); qt_ = QT.next(); va = VA.next(); bt = BT.next(); cb = CB.next()
            S.dma(kt_[:, :], qk[512 + h * 128:512 + (h + 1) * 128, :])
            S.dma(qt_[0][0:64, :], qk[h * 128:h * 128 + 64, :])
            S.memset(qt_[0][64:128, :], 0.0, eng="pool")
            S.dma(qt_[1][64:128, :], qk[h * 128 + 64:(h + 1) * 128, :])
            S.memset(qt_[1][0:64, :], 0.0, eng="pool")
            S.dma(va[:, :, 0:128], V(vb.t[:, h * 128:(h + 1) * 128].rearrange("(n p) e -> p n e", p=128), vb.d))
            S.memset(va[:, :, 128:130], 1.0, eng="pool")
            for rho in range(6):
                c_ = 639 - (rho - 1) * 128
                S.dma(bt[:, rho, :], V(hsc.t[h, :, c_:c_ + 512], hsc.d))
            S.dma(cb[:, 0:1], V(bass.AP(tensor=gsc.t.tensor, offset=gsc.t.offset + h * (NB_LEN + 1) + 1278, ap=[[0, 128], [1, 1]]), gsc.d))
            S.dma(cb[:, 1:2], V(bass.AP(tensor=gsc.t.tensor, offset=gsc.t.offset + h * (NB_LEN + 1), ap=[[0, 128], [1, 1]]), gsc.d))
            def acc_of(c, qb):
                return (accA[c], qb * 130) if qb < 3 else (accB[c], 0)

            tiles = [(q5, c, kt) for q5 in range(NT5) for c in range(2) for kt in range(NCH)]
            LOOK = 2
            Es = {}
            for idx in range(len(tiles) + LOOK):
                if idx < len(tiles):
                    q5, c, kt = tiles[idx]
                    pb = c * 64
                    st = stR.next()
                    S.mm(st[:, :], kt_[:, kt * 128:(kt + 1) * 128], qt_[c][:, q5 * 512:(q5 + 1) * 512])
                    rho = kt - 4 * q5 + 1
                    E = ER.next()
                    if 0 <= rho <= 5:
                        tmp = TMP.next()
                        S.stt(tmp[:, :], st[:, :], 0.125, bt[:, rho, :], ALU.mult, ALU.add)
                        S.act(E[:, :], tmp[:, :], AF.Exp)
                    else:
                        S.act(E[:, :], st[:, :], AF.Exp, bias=(cb[:, 0:1] if rho < 0 else cb[:, 1:2]), scale=0.125)
                    Es[idx] = E
                j = idx - LOOK
                if j < 0:
                    continue
                q5, c, kt = tiles[j]
                E = Es.pop(j)
                for qb in range(4):
                    a, o = acc_of(c, qb)
                    S.mm(a[:, o:o + 130], E[:, qb * 128:(qb + 1) * 128], va[:, kt, 0:130],
                         start=(kt == 0 and qb in (0, 3)), stop=(kt == NCH - 1))
                if kt != NCH - 1:
                    continue
                for qb in range(4):
                    a, o = acc_of(c, qb)
                    sm = SM.next()
                    S.op("dve", lambda e, sm=sm, a=a, o=o: e.reciprocal(out=sm.t[:, 0:1], in_=a.t[:, o + 128:o + 129]),
                         [a.d], [sm.d])
                    S.ts(ON[c][:, qb, :], a[:, o:o + 128], sm[:, 0:1], None, ALU.mult)
                if c == 0:
                    continue
                ost = OST.next()
                for qb in range(4):
                    of = OF.next(); ob = OB.next(); sm = SM.next()
                    S.stt(of[:, :], ON[1][:, qb, :], nlam[:, 0:1], ON[0][:, qb, :], ALU.mult, ALU.add)
                    S.act(ob[:, :], of[:, :], AF.Square, accum=sm[:, 0:1])
                    S.act(sm[:, 1:2], sm[:, 0:1], AF.Sqrt, bias=epsln[:, 0:1], scale=1.0 / 128.0)
                    S.op("dve", lambda e, sm=sm: e.reciprocal(out=sm.t[:, 2:3], in_=sm.t[:, 1:2]), [sm.d], [sm.d])
                    S.stt(ob[:, :], of[:, :], sm[:, 2:3], subg[:, :], ALU.mult, ALU.mult)
                    S.tr(ptr[:, qb * 128:(qb + 1) * 128], ob[:, :], identb[:, :])
                S.copy(ost[:, :], ptr[:, :])
                S.dma(mix[512 + h * 128:512 + (h + 1) * 128, q5 * 512:(q5 + 1) * 512], ost[:, :])
        C.close()

    def phase_wout(l, s):
        C = Ctx(nc, S)
        w = C.sb([128, 8, D], BF16, "wout")
        load_w_bf16(w, P["w_out"][l], 8, D)
        g_bc = C.sb([128, D], F32, "gbc"); b_bc = C.sb([128, D], F32, "bbc")
        S.dma(g_bc[:, :], V(bcast_rows(P["ln1_g"][l], D), None))
        S.dma(b_bc[:, :], V(bcast_rows(P["ln1_b"][l], D), None))
        CC = {"st": Ring([C.sb([128, 16], F32, "st") for _ in range(4)])}
        MT = Ring([C.sb([128, 8, 512], BF16, "mixT") for _ in range(2)])
        XB = Ring([C.sb([128, D], F32, "xb") for _ in range(4)])
        HB = Ring([C.sb([128, D], F32, "hb") for _ in range(4)])
        PS = Ring([C.ps([128, 1024], F32, "pso") for _ in range(3)])
        for t5 in range(NT5):
            t0 = t5 * 512
            mt = MT.next()
            S.dma(mt[:, :, :], V(mix.t[:, t0:t0 + 512].rearrange("(k p) t -> p k t", p=128), mix.d))
            for j in range(4):
                r0 = t0 + j * 128
                xb = XB.next(); hb = HB.next(); p = PS.next()
                S.dma(xb[:, :], xres[r0:r0 + 128, :])
                for half in range(2):
                    for k in range(8):
                        S.mm(p[:, half * 512:(half + 1) * 512], mt[:, k, j * 128:(j + 1) * 128],
                             w[:, k, half * 512:(half + 1) * 512], start=(k == 0), stop=(k == 7))
                S.stt(hb[:, :], xb[:, :], ALPHA, p[:, :], ALU.mult, ALU.add)
                layer_norm_rows(CC, hb[:, :], g_bc[:, :], b_bc[:, :], hb[:, :])
                S.dma(x1s[r0:r0 + 128, :], hb[:, :])
        C.close()

    def phase_ffn(l, s):
        C = Ctx(nc, S)
        wd = C.sb([128, 22, D], BF16, "wdown")
        load_w_bf16(wd, P["ffn_down"][l], 22, D)
        g_bc = C.sb([128, D], F32, "gbc"); b_bc = C.sb([128, D], F32, "bbc")
        S.dma(g_bc[:, :], V(bcast_rows(P["ln2_g"][l], D), None))
        S.dma(b_bc[:, :], V(bcast_rows(P["ln2_b"][l], D), None))
        cw = C.sb([128, 4, 44], F32, "cw")
        stg = C.sb([64, 128], F32, "stg")
        pst = Ring([C.ps([128, 512], F32, "pst") for _ in range(2)])
        for j in range(3):
            load_cols(C, cw[:, j, :], P["conv_w"][l, j], 44, pst.next(), stg)
        load_cols(C, cw[:, 3, :], P["conv_b"][l], 44, pst.next(), stg)
        CC = {"st": Ring([C.sb([128, 16], F32, "st") for _ in range(4)])}
        XB = Ring([C.sb([128, D], F32, "xb") for _ in range(8)])
        XH = Ring([C.sb([2, D], F32, "xh") for _ in range(2)])
        XT = Ring([C.sb([128, 8, 512], BF16, "xT") for _ in range(2)])
        XHT = Ring([C.sb([128, 8, 2], BF16, "xhT") for _ in range(2)])
        WC = Ring([C.sb([128, 8 * 128], BF16, "wc") for _ in range(4)])
        PH = Ring([C.ps([128, 512], F32, "ph") for _ in range(2)])
        PHHS = Ring([(C.ps([128, 512], F32, "phh"), 0) for i in range(2)])
        PD = Ring([C.ps([128, 1024], F32, "pd") for _ in range(1)])
        HR = Ring([C.sb([128, 514], F32, "hr") for _ in range(6)])
        OO = Ring([C.sb([128, 512], F32, "oo") for _ in range(4)])
        GA = C.sb([128, 22, 512], BF16, "ga")
        GT = C.sb([128, 22, 512], BF16, "gt")
        HB = Ring([C.sb([128, D], F32, "hb") for _ in range(3)])
        for t5 in range(NT5):
            t0 = t5 * 512
            xt = XT.next(); xh = XH.next(); xht = XHT.next()
            blocks = []
            for j in range(4):
                xb = XB.next()
                S.dma(xb[:, :], x1s[t0 + j * 128:t0 + (j + 1) * 128, :])
                blocks.append(xb)
            S.memset(xh[:, :], 0.0, eng="pool")
            if t0 > 0:
                S.dma(xh[0:1, :], x1s[t0 - 1:t0, :])
            if t0 + 512 < T:
                S.dma(xh[1:2, :], x1s[t0 + 512:t0 + 513, :])
            for k in range(8):
                p = pst.next()
                for j in range(4):
                    S.tr(p[:, j * 128:(j + 1) * 128], blocks[j][:, k * 128:(k + 1) * 128], ident[:, :])
                S.copy(xt[:, k, :], p[:, :], eng=("act" if k % 2 == 0 else "dve"))
            p = pst.next()
            for k in range(8):
                S.tr(p[:, 2 * k:2 * k + 2], xh[0:2, k * 128:(k + 1) * 128], ident[0:2, 0:2])
            S.copy(xht[:, :, :], V(p.t[:, 0:16].rearrange("p (k t) -> p k t", t=2), p.d))
            def part_a(c):
                wc = WC.next()
                S.dma(wc[:, :], upbf[l, c])
                ph = PH.next()
                for k in range(8):
                    S.mm(ph[:, :], wc[:, k * 128:(k + 1) * 128], xt[:, k, :], start=(k == 0), stop=(k == 7))
                PHH, ho = PHHS.next()
                for k in range(8):
                    S.mm(PHH[:, ho:ho + 2], wc[:, k * 128:(k + 1) * 128], xht[:, k, :], start=(k == 0), stop=(k == 7))
                hr = HR.next()
                S.copy(hr[:, 1:513], ph[:, :], eng="act")
                S.copy(hr[:, 0:1], PHH[:, ho:ho + 1], eng="dve")
                S.copy(hr[:, 513:514], PHH[:, ho + 1:ho + 2], eng="dve")
                return hr

            def part_b1(c, hr):
                oo = OO.next()
                if c < 22:
                    S.ts(oo[:, :], hr[:, 1:513], cw[:, 1, c:c + 1], cw[:, 3, c:c + 1], ALU.mult, ALU.add, eng="pool")
                else:
                    S.act(oo[:, :], hr[:, 1:513], AF.Identity, bias=cw[:, 3, c:c + 1], scale=cw[:, 1, c:c + 1])
                S.stt(oo[:, :], hr[:, 0:512], cw[:, 0, c:c + 1], oo[:, :], ALU.mult, ALU.add)
                return oo

            def part_b2(c, hr, oo):
                S.stt(oo[:, :], hr[:, 2:514], cw[:, 2, c:c + 1], oo[:, :], ALU.mult, ALU.add)
                if c < 22:
                    S.act(GA[:, c, :], oo[:, :], AF.Gelu)
                else:
                    S.tt(GT[:, c - 22, :], oo[:, :], GA[:, c - 22, :], ALU.mult, eng="pool")

            hrs = {}
            oos = {}
            for c in range(47):
                if c < 44:
                    hrs[c] = part_a(c)
                if 2 <= c < 46:
                    oos[c - 2] = part_b1(c - 2, hrs[c - 2])
                if c >= 3:
                    part_b2(c - 3, hrs.pop(c - 3), oos.pop(c - 3))
            for j in range(4):
                r0 = t0 + j * 128
                p = PD.next(); hb = HB.next()
                for half in range(2):
                    for k in range(22):
                        S.mm(p[:, half * 512:(half + 1) * 512], GT[:, k, j * 128:(j + 1) * 128],
                             wd[:, k, half * 512:(half + 1) * 512], start=(k == 0), stop=(k == 21))
                S.stt(hb[:, :], blocks[j][:, :], ALPHA, p[:, :], ALU.mult, ALU.add)
                layer_norm_rows(CC, hb[:, :], g_bc[:, :], b_bc[:, :], hb[:, :])
                if l == DEPTH - 1:
                    S.dma(V(y_out.t[s, r0:r0 + 128, :], y_out.d), hb[:, :])
                else:
                    S.dma(xres[r0:r0 + 128, :], hb[:, :])
        C.close()

    def RV(v, pattern, **kw):
        return V(v.ap.rearrange(pattern, **kw), v.d)

    def phase_rwkv(l, s):
        C = Ctx(nc, S)
        RA = Ring([C.ps([128, 512], F32, "ra") for _ in range(6)])
        pstc = RA.tiles[0]
        stg = C.sb([64, 128], F32, "stg")
        mu = C.sb([128, 3, 14], F32, "mu")
        for j in range(2):
            S.memset(stg[0:14, :], 0.0)
            S.dma(stg[0:13, :], V(P["shift_mu"][l, j, 0:1664].rearrange("(c p) -> c p", p=128), None))
            S.dma(stg[13:14, 0:96], V(P["shift_mu"][l, j, 1664:1760].rearrange("(c p) -> c p", p=96), None))
            S.tr(pstc[:, 0:14], stg[0:14, :], ident[0:14, 0:14])
            S.copy(mu[:, 1 + j, :], pstc[:, 0:14])
        S.tt(mu[:, 0, :], mu[:, 1, :], mu[:, 2, :], ALU.add)
        S.ts(mu[:, 0, :], mu[:, 0, :], -1.0, 1.0, ALU.mult, ALU.add)
        PRM = C.sb([128, 12, 4], F32, "prm")
        srcs = [P["w0"][l, 0], P["w0"][l, 1], P["a0"][l, 0], P["a0"][l, 1], P["k_k"][l], P["k_a"][l], P["r_k"][l],
                P["gn_g"][l], P["gn_b"][l], P["vres_0"][0]]
        for i, sap in enumerate(srcs):
            load_cols(C, PRM[:, i, :], sap, 4, pstc, stg)
        S.ts(PRM[:, 10, :], PRM[:, 5, :], -1.0, 1.0, ALU.mult, ALU.add)
        S.ts(PRM[:, 11, :], PRM[:, 5, :], -2.0, 2.0, ALU.mult, ALU.add)
        dup = C.sb([64, AW], BF16, "dup")
        iup = [C.sb([128, AW], BF16, "iup") for _ in range(2)]
        gup = C.sb([96, AW], BF16, "gup")
        vdn = C.sb([128, 4, 32], BF16, "vdn")
        vup = C.sb([32, AW], BF16, "vup")
        S.dma(dup[:, :], V(P["decay_up"][l].rearrange("d r c -> (d r) c"), None), q="pool")
        for dd in range(2):
            S.memset(iup[dd][64:128, :], 0.0)
            S.dma(iup[dd][64 + 32 * dd:96 + 32 * dd, :], V(P["iclr_up"][l, dd], None), q="pool")
        S.dma(gup[:, :], V(P["g_up"][l], None), q="pool")
        S.dma(vdn[:, :, :], V(P["vres_down"][0].rearrange("(f p) r -> p f r", p=128), None), q="pool")
        S.dma(vup[:, :], V(P["vres_up"][0], None), q="pool")
        blkm = C.sb([128, 128], BF16, "blkm")
        S.ts(blkm[:, :], blk[:, :], 1.0 / 64.0, None, ALU.mult)
        idblk = C.sb([128, 64], F32, "idblk")
        S.tt(idblk[:, :], ident[:, 0:64], ident[:, 64:128], ALU.add)
        ones = C.sb([128, 128], F32, "ones")
        S.memset(ones[:, :], 1.0)
        eps12 = C.sb([128, 1], F32, "eps12")
        S.memset(eps12[:, :], 1e-12)
        epsgn = C.sb([128, 1], F32, "epsgn")
        S.memset(epsgn[:, :], GN_EPS)
        MS2 = C.sb([128, 2, 128], F32, "ms2")
        MI2 = C.sb([128, 2, 128], F32, "mi2")
        MT2 = C.sb([128, 2, 128], F32, "mt2")

        _stage(1)
        XIN = Ring([C.sb([128, 14, 130], F32, "xin") for _ in range(2)])
        SH = [[C.sb([128, 128], F32, "sh") for _ in range(14)] for _ in range(2)]
        PTBR = Ring([C.ps([128, 1024], BF16, "ptb") for _ in range(2)])
        TWt = C.sb([64, 128], BF16, "tw")
        ADt = C.sb([128, 128], BF16, "ad")
        SGt = C.sb([96, 128], BF16, "sg")
        VD = C.sb([32, 128], BF16, "vd")

        def fset(n, dt, shape=(128, 128), name="f"):
            return [[C.sb(list(shape), dt, name) for _ in range(4)] for _ in range(n)]
        (Ee, Cu, La, Gm, Gi, Gp, Ge, K0, Kk, Bb, Kd, T1, Yy, Dl, Yn, T2) = fset(16, F32)
        A0 = [fset(1, F32)[0] for _ in range(2)]
        A1 = [fset(1, F32)[0] for _ in range(2)]
        Vv = [fset(1, F32)[0] for _ in range(2)]
        SMs = [[C.sb([128, 8], F32, "sms") for _ in range(4)] for _ in range(2)]
        (SQ, BH, KH, VB16, RH, YB, RKV, OA) = fset(8, BF16)
        AT = [fset(1, BF16)[0] for _ in range(2)]
        VT = [fset(1, BF16)[0] for _ in range(2)]
        AR = [[C.sb([128, 256], BF16, "ar") for _ in range(4)] for _ in range(2)]
        BK = [[C.sb([128, 256], BF16, "bk") for _ in range(4)] for _ in range(2)]
        DI = [[C.sb([128, 2, 192], BF16, "di") for _ in range(4)] for _ in range(2)]
        DX = [[C.sb([128, 2, 192], BF16, "dx") for _ in range(4)] for _ in range(2)]
        GQa = [[C.sb([128, 2, 192], BF16, "gqa") for _ in range(4)] for _ in range(2)]
        GQ = [C.sb([128, 2, 192], BF16, "gq") for _ in range(4)]
        MTAK = [C.sb([128, 2, 128], BF16, "mtak") for _ in range(4)]
        PW = [[C.sb([128, 2, 2, 128], BF16, "pw") for _ in range(4)] for _ in range(3)]
        PM = [C.sb([128, 64], BF16, "pm") for _ in range(4)]
        ST = [[C.sb([128, 64], BF16, "st") for _ in range(4)] for _ in range(2)]
        VF = [C.sb([128, 128], F32, "vf") for _ in range(4)]
        YF = [C.sb([128, 128], F32, "yf") for _ in range(4)]

        def rr(gens):
            gens = list(gens)
            while gens:
                for g in list(gens):
                    try:
                        next(g)
                    except StopIteration:
                        gens.remove(g)
                yield

        def prep(d, ci, par):
            c0 = ci * 128
            xin = XIN.next()
            sh = SH[par]
            lo = max(c0 - 1, 0)
            hi = min(c0 + 129, T)
            if c0 == 0:
                S.memset(xin[:, :, 0:1], 0.0, eng="pool")
            if c0 + 129 > T:
                S.memset(xin[:, :, 129:130], 0.0, eng="pool")
            o0 = lo - (c0 - 1)
            S.dma(xin[:, :, o0:o0 + (hi - lo)], V(pa.t[:, lo:hi].rearrange("(r p) t -> p r t", p=128), pa.d))
            for rt in range(14):
                S.act(sh[rt][:, :], xin[:, rt, 1:129], AF.Copy, scale=mu[:, 0, rt:rt + 1])
                if rt % 4 == 3:
                    yield
            for rt in range(14):
                S.stt(sh[rt][:, :], xin[:, rt, 0:128], mu[:, 1, rt:rt + 1], sh[rt][:, :], ALU.mult, ALU.add)
                if rt % 4 == 3:
                    yield
            for rt in range(14):
                S.stt(sh[rt][:, :], xin[:, rt, 2:130], mu[:, 2, rt:rt + 1], sh[rt][:, :], ALU.mult, ALU.add)
                if rt % 4 == 3:
                    yield
            r_ = sh[0:4]; k_ = sh[4:8]; v_ = sh[8:12]
            S.act(TWt[32 * d:32 * d + 32, :], sh[12][32 * d:32 * d + 32, :], AF.Tanh)
            S.copy(ADt[64:128, :], sh[12][64:128, :], eng="pool")
            def decay_chain(ft):
                fs = slice(ft * 128, (ft + 1) * 128)
                pz = RA.next()
                S.mm(pz[:, 0:128], dup[32 * d:32 * d + 32, fs], TWt[32 * d:32 * d + 32, :])
                S.act(Ee[ft][:, :], pz[:, 0:128], AF.Sigmoid, bias=PRM[:, d, ft:ft + 1], scale=1.0)
                yield
                for dd in ((0,) if d == 0 else (0, 1)):
                    pq = RA.next()
                    S.mm(pq[:, 0:128], iup[dd][64:128, fs], ADt[64:128, :])
                    S.act((A0 if dd == 0 else A1)[par][ft][:, :], pq[:, 0:128], AF.Sigmoid, bias=PRM[:, 2 + dd, ft:ft + 1], scale=1.0)
                S.op("dve", lambda e, ft=ft: e.tensor_tensor_scan(out=Cu[ft].t[:, :], data0=ones.t[:, :], data1=Ee[ft].t[:, :],
                                                                   initial=0.0, op0=ALU.mult, op1=ALU.add),
                     [ones.d, Ee[ft].d], [Cu[ft].d])
                yield
                sm = SMs[par][ft]
                if d == 0:
                    lam_t = Cu[ft]
                else:
                    S.stt(La[ft][:, :], Cu[ft][:, :], -1.0, Ee[ft][:, :], ALU.mult, ALU.add)
                    yield
                    S.ts(La[ft][:, :], La[ft][:, :], Cu[ft][:, 127:128], None, ALU.add)
                    lam_t = La[ft]
                S.ts(sm[:, 0:1], Cu[ft][:, 127:128], -CEXP, None, ALU.mult)
                yield
                S.act(sm[:, 1:2], Cu[ft][:, 127:128], AF.Exp, scale=-CEXP)
                S.act(Gm[ft][:, :], lam_t[:, :], AF.Exp, scale=-CEXP)
                S.tt(Gp[ft][:, :], lam_t[:, :], Ee[ft][:, :], ALU.subtract, eng="pool")
                yield
                S.act(Gi[ft][:, :], lam_t[:, :], AF.Exp, scale=CEXP)
                S.act(Ge[ft][:, :], lam_t[:, :], AF.Exp, bias=sm[:, 0:1], scale=CEXP)
                yield
                S.act(Gp[ft][:, :], Gp[ft][:, :], AF.Exp, scale=-CEXP)
                yield

            yield from rr([decay_chain(ft) for ft in range(4)])
            for ft in range(4):
                S.copy(Vv[par][ft][:, :], v_[ft][:, :], eng="pool")
            if l == 1:
                pvd = RA.next()
                for ft in range(4):
                    S.copy(VB16[ft][:, :], Vv[par][ft][:, :], eng="pool")
                    S.mm(pvd[0:32, 0:128], vdn[:, ft, :], VB16[ft][:, :], start=(ft == 0), stop=(ft == 3))
                S.copy(VD[:, :], pvd[0:32, 0:128], eng="act")
                yield
                def vres_chain(ft):
                    fs = slice(ft * 128, (ft + 1) * 128)
                    S.dma(VF[ft][:, :], vfs[ft * 128:(ft + 1) * 128, c0:c0 + 128])
                    pg = RA.next()
                    S.mm(pg[:, 0:128], vup[:, fs], VD[:, :])
                    S.act(T1[ft][:, :], pg[:, 0:128], AF.Sigmoid, bias=PRM[:, 9, ft:ft + 1], scale=1.0)
                    S.tt(Dl[ft][:, :], VF[ft][:, :], Vv[par][ft][:, :], ALU.subtract, eng="pool")
                    yield
                    S.tt(Dl[ft][:, :], Dl[ft][:, :], T1[ft][:, :], ALU.mult, eng="pool")
                    yield
                    S.tt(Vv[par][ft][:, :], Vv[par][ft][:, :], Dl[ft][:, :], ALU.add, eng="pool")
                    yield

                yield from rr([vres_chain(ft) for ft in range(4)])
            elif d == 0:
                for ft in range(4):
                    S.dma(vfs[ft * 128:(ft + 1) * 128, c0:c0 + 128], Vv[par][ft][:, :])
            def kk_chain(ft):
                S.ts(K0[ft][:, :], k_[ft][:, :], PRM[:, 4, ft:ft + 1], None, ALU.mult)
                S.copy(VB16[ft][:, :], Vv[par][ft][:, :], eng="pool")
                yield
                S.act(SQ[ft][:, :], K0[ft][:, :], AF.Square)
                yield
                Ad = (A0 if d == 0 else A1)[par][ft]
                S.ts(T1[ft][:, :], Ad[:, :], PRM[:, 5, ft:ft + 1], PRM[:, 10, ft:ft + 1], ALU.mult, ALU.add)
                S.tt(AR[par][ft][:, 128:256], r_[ft][:, :], Gm[ft][:, :], ALU.mult, eng="pool")
                yield
                pss = RA.next()
                S.mm(pss[:, 0:128], blk[:, :], SQ[ft][:, :])
                S.act(Kk[ft][:, :], pss[:, 0:128], AF.Sqrt, bias=eps12[:, 0:1], scale=1.0)
                S.tt(Kd[ft][:, :], k_[ft][:, :], T1[ft][:, :], ALU.mult, eng="pool")
                yield
                S.op("dve", lambda e, ft=ft: e.reciprocal(out=Kk[ft].t[:, :], in_=Kk[ft].t[:, :]), [Kk[ft].d], [Kk[ft].d])
                S.tt(BK[par][ft][:, 128:256], Kd[ft][:, :], Gi[ft][:, :], ALU.mult, eng="pool")
                yield
                S.tt(Kk[ft][:, :], Kk[ft][:, :], K0[ft][:, :], ALU.mult)
                S.tt(KH[ft][:, :], Kd[ft][:, :], Ge[ft][:, :], ALU.mult, eng="pool")
                yield
                S.stt(AR[par][ft][:, 0:128], Kk[ft][:, :], -1.0, Gp[ft][:, :], ALU.mult, ALU.mult)
                S.tt(Bb[ft][:, :], Kk[ft][:, :], Ad[:, :], ALU.mult)
                yield
                S.tt(BK[par][ft][:, 0:128], Bb[ft][:, :], Gi[ft][:, :], ALU.mult, eng="pool")
                S.tt(BH[ft][:, :], Bb[ft][:, :], Ge[ft][:, :], ALU.mult)
                yield

            yield from rr([kk_chain(ft) for ft in range(4)])
            for ft in range(4):
                PTB = PTBR.next()
                S.tr(PTB[:, 0:128], AR[par][ft][:, 0:128], identb[:, :])
                S.tr(PTB[:, 128:256], BH[ft][:, :], identb[:, :])
                S.tr(PTB[:, 256:384], KH[ft][:, :], identb[:, :])
                S.tr(PTB[:, 384:512], VB16[ft][:, :], identb[:, :])
                S.copy(AT[par][ft][:, :], PTB[:, 0:128], eng="dve")
                for h in range(2):
                    S.copy(DI[par][ft][:, h, 128:192], PTB[:, 128 + h * 64:192 + h * 64], eng="dve")
                for h in range(2):
                    S.copy(GQa[par][ft][:, h, 128:192], PTB[:, 256 + h * 64:320 + h * 64], eng="dve")
                S.copy(VT[par][ft][:, :], PTB[:, 384:512], eng="dve")
                yield

        def mats(d, ci, par, sti):
            c0 = ci * 128
            sh = SH[par]
            r_ = sh[0:4]; k_ = sh[4:8]
            for ft in range(4):
                for h in range(2):
                    hb = h * 64
                    pA = RA.next()
                    S.mm(pA[:, 0:256], BK[par][ft][hb:hb + 64, 0:128], AR[par][ft][hb:hb + 64, :])
                    S.tt(PW[0][ft][:, h, 0, :], pA[:, 0:128], MS2[:, h, :], ALU.mult)
                    S.tt(DI[par][ft][:, h, 0:128], pA[:, 128:256], MI2[:, h, :], ALU.mult)
                    pB = RA.next()
                    S.mm(pB[:, 0:128], BK[par][ft][hb:hb + 64, 128:256], AR[par][ft][hb:hb + 64, 128:256])
                    S.tt(GQa[par][ft][:, h, 0:128], pB[:, 0:128], MI2[:, h, :], ALU.mult)
                    pC = RA.next()
                    S.mm(pC[:, 0:256], AR[par][ft][hb:hb + 64, 0:128], BK[par][ft][hb:hb + 64, :])
                    S.tt(PW[0][ft][:, h, 1, :], pC[:, 0:128], MT2[:, h, :], ALU.mult)
                    S.tt(MTAK[ft][:, h, :], pC[:, 128:256], MT2[:, h, :], ALU.mult)
                yield
            Dc = [DI[par][ft] for ft in range(4)]
            for j in range(7):
                pwc = PW[j % 3]
                pwn = PW[(j + 1) % 3]
                for ft in range(4):
                    if j < 6:
                        pP = RA.next()
                        for h in range(2):
                            S.mm(pP[:, h * 256:h * 256 + 128], pwc[ft][:, h, 1, :], pwc[ft][:, h, 0, :])
                            S.mm(pP[:, h * 256 + 128:h * 256 + 256], pwc[ft][:, h, 0, :], pwc[ft][:, h, 1, :])
                        S.copy(RV(pwn[ft][:, :, :, :], "p h t c -> p (h t c)"), pP[:, :], eng="act")
                    pD = RA.next()
                    for h in range(2):
                        S.mm(pD[:, h * 192:(h + 1) * 192], identb[:, :], Dc[ft][:, h, :], start=True, stop=False)
                        S.mm(pD[:, h * 192:(h + 1) * 192], pwc[ft][:, h, 1, :], Dc[ft][:, h, :], start=False, stop=True)
                    Dn = DX[j % 2][ft]
                    S.copy(RV(Dn[:, :, :], "p h c -> p (h c)"), pD[:, 0:384], eng=("dve" if ft % 2 == 0 else "act"))
                    Dc[ft] = Dn
                    yield
            stc = ST[sti % 2]
            stn = ST[(sti + 1) % 2]
            for ft in range(4):
                DF = Dc[ft]
                sm = SMs[par][ft]
                pR = RA.next()
                for h in range(2):
                    hb = h * 64
                    S.mm(pR[hb:hb + 64, 0:192], AT[par][ft][:, hb:hb + 64], DF[:, h, :])
                S.tt(RH[ft][:, :], pR[:, 0:128], AR[par][ft][:, 128:256], ALU.add)
                S.stt(PM[ft][:, :], idblk[:, :], sm[:, 1:2], pR[:, 128:192], ALU.mult, ALU.add)
                pG = RA.next()
                for h in range(2):
                    S.mm(pG[:, h * 192:(h + 1) * 192], MTAK[ft][:, h, :], DF[:, h, :])
                S.tt(RV(GQ[ft][:, :, :], "p h c -> p (h c)"), pG[:, 0:384], RV(GQa[par][ft][:, :, :], "p h c -> p (h c)"), ALU.add)
                yield
            for ft in range(4):
                if d == 1:
                    S.dma(YF[ft][:, :], yfs[ft * 128:(ft + 1) * 128, c0:c0 + 128])
                for h in range(2):
                    hb = h * 64
                    pY = RA.next()
                    pS = RA.next()
                    S.mm(pY[hb:hb + 64, 0:128], stc[ft][hb:hb + 64, :], RH[ft][hb:hb + 64, :], start=True, stop=False)
                    S.mm(pY[hb:hb + 64, 0:128], VT[par][ft][:, hb:hb + 64], GQ[ft][:, h, 0:128], start=False, stop=True)
                    S.mm(pS[hb:hb + 64, 0:64], PM[ft][hb:hb + 64, :], stc[ft][hb:hb + 64, :], start=True, stop=False)
                    S.mm(pS[hb:hb + 64, 0:64], GQ[ft][:, h, 128:192], VT[par][ft][:, hb:hb + 64], start=False, stop=True)
                    S.copy(stn[ft][hb:hb + 64, :], pS[hb:hb + 64, 0:64], eng="act")
                    if d == 0:
                        S.copy(Yy[ft][hb:hb + 64, :], pY[hb:hb + 64, 0:128], eng="dve")
                    else:
                        S.tt(Yy[ft][hb:hb + 64, :], pY[hb:hb + 64, 0:128], YF[ft][hb:hb + 64, :], ALU.add)
                yield
                if d == 0:
                    S.dma(yfs[ft * 128:(ft + 1) * 128, c0:c0 + 128], Yy[ft][:, :])
            if d == 0:
                return

            def epi_chain(ft):
                fs = slice(ft * 128, (ft + 1) * 128)
                S.copy(YB[ft][:, :], Yy[ft][:, :], eng="pool")
                S.tt(T2[ft][:, :], A0[par][ft][:, :], A1[par][ft][:, :], ALU.add, eng="pool")
                yield
                pm_ = RA.next()
                S.mm(pm_[:, 0:128], blkm[:, :], YB[ft][:, :])
                S.tt(Dl[ft][:, :], Yy[ft][:, :], pm_[:, 0:128], ALU.subtract)
                S.ts(T2[ft][:, :], T2[ft][:, :], PRM[:, 5, ft:ft + 1], PRM[:, 11, ft:ft + 1], ALU.mult, ALU.add)
                yield
                S.act(SQ[ft][:, :], Dl[ft][:, :], AF.Square)
                S.tt(T2[ft][:, :], T2[ft][:, :], k_[ft][:, :], ALU.mult, eng="pool")
                yield
                pv_ = RA.next()
                S.mm(pv_[:, 0:128], blkm[:, :], SQ[ft][:, :])
                S.act(Yn[ft][:, :], pv_[:, 0:128], AF.Sqrt, bias=epsgn[:, 0:1], scale=1.0)
                S.stt(RKV[ft][:, :], r_[ft][:, :], PRM[:, 6, ft:ft + 1], T2[ft][:, :], ALU.mult, ALU.mult)
                yield
                S.op("dve", lambda e, ft=ft: e.reciprocal(out=Yn[ft].t[:, :], in_=Yn[ft].t[:, :]), [Yn[ft].d], [Yn[ft].d])
                pb_ = RA.next()
                S.mm(pb_[:, 0:128], blk[:, :], RKV[ft][:, :])
                S.tt(T2[ft][:, :], pb_[:, 0:128], Vv[par][ft][:, :], ALU.mult)
                yield
                S.tt(Yn[ft][:, :], Yn[ft][:, :], Dl[ft][:, :], ALU.mult)
                yield
                S.ts(Yn[ft][:, :], Yn[ft][:, :], PRM[:, 7, ft:ft + 1], PRM[:, 8, ft:ft + 1], ALU.mult, ALU.add)
                yield
                S.tt(Yn[ft][:, :], Yn[ft][:, :], T2[ft][:, :], ALU.add)
                pg_ = RA.next()
                S.mm(pg_[:, 0:128], gup[:, fs], SGt[:, :])
                S.tt(OA[ft][:, :], Yn[ft][:, :], pg_[:, 0:128], ALU.mult)
                S.dma(mix[ft * 128:(ft + 1) * 128, c0:c0 + 128], OA[ft][:, :])
                yield

            S.act(SGt[:, :], sh[13][0:96, :], AF.Sigmoid)
            yield from rr([epi_chain(ft) for ft in range(4)])

        def run_both(a, b):
            alive = [g for g in (a, b) if g is not None]
            while alive:
                for g in list(alive):
                    try:
                        next(g)
                    except StopIteration:
                        alive.remove(g)

        for d in range(2):
            iLs, iLi, iLt = (0, 1, 2) if d == 0 else (2, 3, 0)
            for h in range(2):
                S.copy(MS2[:, h, :], masks[:, iLs, :])
                S.copy(MI2[:, h, :], masks[:, iLi, :])
                S.copy(MT2[:, h, :], masks[:, iLt, :])
            for ft in range(4):
                S.memset(ST[0][ft][:, :], 0.0)
            order = list(range(NCH)) if d == 0 else list(range(NCH - 1, -1, -1))
            gm = None
            for n, ci in enumerate(order):
                run_both(prep(d, ci, n % 2), gm)
                gm = mats(d, ci, n % 2, n)
            run_both(gm, None)
            S.barrier()
        C.close()

    PH = {"proj": phase_proj, "setup": phase_setup, "attn": phase_attn, "wout": phase_wout, "ffn": phase_ffn,
          "rwkv": phase_rwkv}
    if plan is None:
        plan = [("setup", 0, 0)]
        for s in range(NSEQ):
            for l in range(DEPTH):
                plan += [("proj", l, s), ("rwkv", l, s), ("attn", l, s), ("wout", l, s), ("ffn", l, s)]
    for (name, l, s) in plan:
        try:
            PH[name](l, s)
        except StopBuild:
            print('STOPPED')
    S.barrier()
    return nc


def _t5_bucket_np(rel):
    nb = 16
    max_exact = 8
    ret = np.where(rel > 0, nb, 0)
    n = np.abs(rel)
    nf = np.maximum(n, 1).astype(np.float32)
    large = max_exact + (np.log(nf / np.float32(max_exact)) / np.float32(math.log(128 / max_exact))
                         * np.float32(nb - max_exact)).astype(np.int32)
    large = np.minimum(large, nb - 1)
    return ret + np.where(n < max_exact, n, large)


def _consts():
    ident = np.eye(128, dtype=np.float32)
    a = np.arange(128)[:, None]
    b = np.arange(128)[None, :]
    masks = np.stack([(a < b), (a <= b), (a > b), (a >= b)], 1).astype(np.float32)
    blk = np.zeros((128, 128), np.float32)
    blk[:64, :64] = 1
    blk[64:, 64:] = 1
    delta = 639 - np.arange(NB_LEN + 1)
    bk = _t5_bucket_np(delta.astype(np.int32))
    oh = np.zeros((32, NB_LEN + 1), np.float32)
    oh[bk, np.arange(NB_LEN + 1)] = 1
    return dict(c_ident=ident, c_masks=np.ascontiguousarray(masks), c_blk=blk, c_onehot=oh)


_NC_CACHE = {}


def kernel(**inputs):
    xp = np.asarray(inputs["x_prompt"], np.float32)
    xs = np.asarray(inputs["x_sample"], np.float32)
    T = xp.shape[1]
    allx = [xp[i] for i in range(xp.shape[0])] + [xs[i] for i in range(xs.shape[0])]
    n = len(allx)
    NC = 8
    slots = [(c, 8 + (c % 4)) for c in range(NC)]
    key = (T, 2)
    if key not in _NC_CACHE:
        _NC_CACHE[key] = build(T, 2)
    nc = _NC_CACHE[key]
    base = {k: np.ascontiguousarray(np.asarray(v, np.float32)) for k, v in inputs.items()
            if k not in ("x_prompt", "x_sample")}
    base["r_k"] = base["r_k"].reshape(2, AW)
    base.update(_consts())
    in_maps = []
    for c in range(NC):
        m = dict(base)
        m["x"] = np.ascontiguousarray(np.stack([allx[slots[c][0]], allx[slots[c][1]]], 0))
        in_maps.append(m)
    res = run_bass_kernel_spmd(nc, in_maps, core_ids=list(range(NC)))
    outs = [None] * n
    for c in range(NC):
        y = res.results[c]["y"]
        outs[slots[c][0]] = y[0]
        if c < 4:
            outs[slots[c][1]] = y[1]
    yp = np.stack(outs[:xp.shape[0]], 0).astype(np.float32)
    ys = np.stack(outs[xp.shape[0]:], 0).astype(np.float32)
    return (yp, ys)
```

```python
import contextlib
import os
import math
import numpy as np
import concourse.bass as bass
import concourse.mybir as mybir
from concourse.bass_utils import run_bass_kernel_spmd

F32 = mybir.dt.float32
BF16 = mybir.dt.bfloat16
AF = mybir.ActivationFunctionType
ALU = mybir.AluOpType

D = 1024
DEPTH = 2
AW = 512
A_COLS = 1760
P_COLS = 3296
D_FF = 2816
LN_EPS = 1e-5
GN_EPS = 64e-5
ALPHA = (2 * DEPTH) ** 0.25
NB_LEN = 1279
CEXP = math.exp(-0.5)


class StopBuild(Exception):
    pass


def _stage(n):
    if int(os.environ.get('RWKV_STOP', '99')) == n:
        raise StopBuild()


class Dep:
    __slots__ = ("w", "r")

    def __init__(self):
        self.w = {}
        self.r = {}


class V:
    __slots__ = ("ap", "d")

    def __init__(self, ap, d):
        self.ap = ap
        self.d = d


class Tile:
    def __init__(self, t, d=None):
        self.t = t
        self.d = d if d is not None else Dep()

    def __getitem__(self, idx):
        return V(self.t[idx], self.d)


class Sched:
    ENG = ("pe", "act", "dve", "pool", "sp")

    def __init__(self, nc, n_dma=28, n_sw=10):
        self.nc = nc
        self.eng = {"pe": nc.tensor, "act": nc.scalar, "dve": nc.vector, "pool": nc.gpsimd, "sp": nc.sync}
        self.sem = {e: nc.semaphore("s_" + e).__enter__() for e in self.ENG}
        self.cnt = {e: 0 for e in self.ENG}
        self.dsem = [nc.semaphore("d%d" % i).__enter__() for i in range(n_dma)]
        self.dcnt = [0] * n_dma
        self.dnext = 0
        self.n_sw = n_sw
        self.swnext = 0
        self.waited = {e: {} for e in self.ENG}
        self.uid = 0

    def _s(self, k):
        return self.sem[k] if isinstance(k, str) else self.dsem[k]

    def op(self, eng, fn, reads=(), writes=(), dma=False):
        need = {}
        for d in reads:
            if d is None:
                continue
            for k, v in d.w.items():
                if need.get(k, 0) < v:
                    need[k] = v
        for d in writes:
            if d is None:
                continue
            for k, v in d.w.items():
                if k == eng and not dma:
                    continue
                if need.get(k, 0) < v:
                    need[k] = v
            for k, v in d.r.items():
                if k == eng and not dma:
                    continue
                if need.get(k, 0) < v:
                    need[k] = v
        if dma:
            if eng == "pool":
                j = self.swnext
                self.swnext = (j + 1) % self.n_sw
            else:
                j = self.n_sw + self.dnext
                self.dnext = (self.dnext + 1) % (len(self.dsem) - self.n_sw)
            if self.dcnt[j] > 0 and need.get(j, 0) < self.dcnt[j]:
                need[j] = self.dcnt[j]
            self.dcnt[j] += 16
            tk = (j, self.dcnt[j])
        else:
            self.cnt[eng] += 1
            tk = (eng, self.cnt[eng])
        E = self.eng[eng]
        wd = self.waited[eng]
        for k, v in need.items():
            if k == "pe" and eng == "pe" and not dma:
                continue
            if wd.get(k, 0) >= v:
                continue
            wd[k] = v
            E.wait_ge(self._s(k), v)
        ins = fn(E)
        ins.then_inc(self._s(tk[0]), 16 if dma else 1)
        for d in reads:
            if d is not None:
                d.r[tk[0]] = tk[1]
        for d in writes:
            if d is not None:
                d.w[tk[0]] = tk[1]
                d.r = {}
        return tk

    def barrier(self):
        for e in self.ENG:
            E = self.eng[e]
            wd = self.waited[e]
            for k in self.ENG:
                if k != e and self.cnt[k] > wd.get(k, 0):
                    wd[k] = self.cnt[k]
                    E.wait_ge(self.sem[k], self.cnt[k])
            for j in range(len(self.dsem)):
                if self.dcnt[j] > wd.get(j, 0):
                    wd[j] = self.dcnt[j]
                    E.wait_ge(self.dsem[j], self.dcnt[j])

    def dma(self, out, in_, q="sp", **kw):
        return self.op(q, lambda e: e.dma_start(out=out.ap, in_=in_.ap, **kw), [in_.d], [out.d], dma=True)

    def mm(self, out, lhsT, rhs, start=True, stop=True, extra_reads=()):
        return self.op("pe", lambda e: e.matmul(out.ap, lhsT=lhsT.ap, rhs=rhs.ap, start=start, stop=stop,
                                                skip_group_check=True),
                       [lhsT.d, rhs.d] + list(extra_reads), [out.d])

    def tr(self, out, in_, ident):
        return self.op("pe", lambda e: e.transpose(out.ap, in_.ap, ident.ap), [in_.d, ident.d], [out.d])

    def act(self, out, in_, func, bias=None, scale=None, accum=None, eng="act"):
        kw = {}
        rd = [in_.d]
        wr = [out.d]
        if bias is not None:
            if isinstance(bias, V):
                kw["bias"] = bias.ap
                rd.append(bias.d)
            else:
                kw["bias"] = bias
        if scale is not None:
            if isinstance(scale, V):
                kw["scale"] = scale.ap
                rd.append(scale.d)
            else:
                kw["scale"] = scale
        if accum is not None:
            kw["accum_out"] = accum.ap
            wr.append(accum.d)
        return self.op("act", lambda e: e.activation(out=out.ap, in_=in_.ap, func=func, **kw), rd, wr)

    def tt(self, out, a, b, op, eng="dve"):
        return self.op(eng, lambda e: e.tensor_tensor(out=out.ap, in0=a.ap, in1=b.ap, op=op), [a.d, b.d], [out.d])

    def ts(self, out, a, s1, s2, op0, op1=None, eng="dve"):
        rd = [a.d]
        x1, x2 = s1, s2
        if isinstance(s1, V):
            rd.append(s1.d)
            x1 = s1.ap
        if isinstance(s2, V):
            rd.append(s2.d)
            x2 = s2.ap
        if op1 is None:
            return self.op(eng, lambda e: e.tensor_scalar(out=out.ap, in0=a.ap, scalar1=x1, scalar2=None, op0=op0),
                           rd, [out.d])
        return self.op(eng, lambda e: e.tensor_scalar(out=out.ap, in0=a.ap, scalar1=x1, scalar2=x2, op0=op0, op1=op1),
                       rd, [out.d])

    def stt(self, out, a, s, b, op0, op1):
        rd = [a.d, b.d]
        x = s
        if isinstance(s, V):
            rd.append(s.d)
            x = s.ap
        return self.op("dve", lambda e: e.scalar_tensor_tensor(out=out.ap, in0=a.ap, scalar=x, in1=b.ap, op0=op0, op1=op1),
                       rd, [out.d])

    def copy(self, out, in_, eng="dve"):
        if eng == "act":
            return self.act(out, in_, AF.Copy)
        return self.op(eng, lambda e: e.tensor_copy(out=out.ap, in_=in_.ap), [in_.d], [out.d])

    def memset(self, out, val, eng="dve"):
        return self.op(eng, lambda e: e.memset(out.ap, val), [], [out.d])


class Ctx:
    def __init__(self, nc, S):
        self.nc = nc
        self.S = S
        self.es = contextlib.ExitStack()
        self.n = 0

    def sb(self, shape, dt, name="t"):
        self.S.uid += 1
        return Tile(self.es.enter_context(self.nc.sbuf_tensor("%s_%d" % (name, self.S.uid), list(shape), dt)))

    def ps(self, shape, dt=F32, name="p"):
        self.S.uid += 1
        return Tile(self.es.enter_context(self.nc.psum_tensor("%s_%d" % (name, self.S.uid), list(shape), dt)))

    def close(self):
        self.S.barrier()
        self.es.close()


class Ring:
    def __init__(self, tiles):
        self.tiles = tiles
        self.i = 0

    def next(self):
        t = self.tiles[self.i % len(self.tiles)]
        self.i += 1
        return t


def bcast_rows(ap1d, n, parts=128):
    return bass.AP(tensor=ap1d.tensor, offset=ap1d.offset, ap=[[0, parts], [1, n]])


def build(T, NSEQ, dbg=False, plan=None):
    nc = bass.Bass("TRN2", target_bir_lowering=False)
    S = Sched(nc)
    NT5 = T // 512
    NCH = T // 128

    def din(name, shape, dt=F32):
        return nc.dram_tensor(name, list(shape), dt, kind="ExternalInput").ap()

    def dscr(name, shape, dt=F32):
        kind = "ExternalOutput" if dbg else "Internal"
        return Tile(nc.dram_tensor(name, list(shape), dt, kind=kind).ap())

    x_in = din("x", [NSEQ, T, D])
    y_out = Tile(nc.dram_tensor("y", [NSEQ, T, D], F32, kind="ExternalOutput").ap())
    P = {}
    for name, shape in [("ln_in_g", [D]), ("ln_in_b", [D]), ("w_in", [2, D, P_COLS]), ("shift_mu", [2, 2, A_COLS]),
                        ("w0", [2, 2, AW]), ("decay_up", [2, 2, 32, AW]), ("a0", [2, 2, AW]), ("iclr_up", [2, 2, 32, AW]),
                        ("g_up", [2, 96, AW]), ("k_k", [2, AW]), ("k_a", [2, AW]), ("r_k", [2, AW]), ("gn_g", [2, AW]),
                        ("gn_b", [2, AW]), ("vres_down", [1, AW, 32]), ("vres_up", [1, 32, AW]), ("vres_0", [1, AW]),
                        ("lam", [2, 4, 64]), ("subln_g", [2, 128]), ("rel_bias", [32, 4]), ("w_out", [2, D, D]),
                        ("ln1_g", [2, D]), ("ln1_b", [2, D]), ("ffn_up", [2, D, 2 * D_FF]), ("conv_w", [2, 3, 2 * D_FF]),
                        ("conv_b", [2, 2 * D_FF]), ("ffn_down", [2, D_FF, D]), ("ln2_g", [2, D]), ("ln2_b", [2, D]),
                        ("c_ident", [128, 128]), ("c_masks", [128, 4, 128]), ("c_blk", [128, 128]),
                        ("c_onehot", [32, NB_LEN + 1])]:
        P[name] = din(name, shape)

    xres = dscr("xres", [T, D])
    pa = dscr("pa", [1792, T])
    qk = dscr("qk", [1024, T], BF16)
    vb = dscr("vb", [T, 512], BF16)
    mix = dscr("mix", [1024, T], BF16)
    x1s = dscr("x1s", [T, D])
    yfs = dscr("yfs", [512, T])
    vfs = dscr("vfs", [512, T])
    gsc = dscr("gsc", [4, NB_LEN + 1])
    hsc = dscr("hsc", [4, 128, 1536])
    upbf = dscr("upbf", [2, 44, 128, 8 * 128], BF16)

    G = Ctx(nc, S)
    ident = G.sb([128, 128], F32, "ident")
    identb = G.sb([128, 128], BF16, "identb")
    masks = G.sb([128, 4, 128], F32, "masks")
    blk = G.sb([128, 128], BF16, "blk")
    epsln = G.sb([128, 1], F32, "epsln")
    S.dma(ident[:, :], V(P["c_ident"], None))
    S.dma(identb[:, :], V(P["c_ident"], None), q="pool")
    S.dma(masks[:, :, :], V(P["c_masks"], None))
    S.dma(blk[:, :], V(P["c_blk"], None), q="pool")
    S.memset(epsln[:, :], LN_EPS)

    def layer_norm_rows(C, xt, g_bc, b_bc, out):
        st = C["st"].next()
        for c in range(2):
            S.op("dve", lambda e, c=c: e.bn_stats(out=st.t[:, c * 6:(c + 1) * 6], in_=xt.ap[:, c * 512:(c + 1) * 512]),
                 [xt.d], [st.d])
        S.op("dve", lambda e: e.bn_aggr(out=st.t[:, 12:14], in_=st.t[:, 0:12]), [st.d], [st.d])
        S.act(st[:, 14:15], st[:, 13:14], AF.Sqrt, bias=epsln[:, 0:1], scale=1.0)
        S.op("dve", lambda e: e.reciprocal(out=st.t[:, 15:16], in_=st.t[:, 14:15]), [st.d], [st.d])
        S.ts(out, xt, st[:, 12:13], st[:, 15:16], ALU.subtract, ALU.mult)
        S.tt(out, out, g_bc, ALU.mult)
        S.tt(out, out, b_bc, ALU.add)

    def load_w_bf16(dst, src_ap, KT, ncols, col0=0):
        v = src_ap.rearrange("(k p) c -> p k c", p=128)
        for k in range(KT):
            S.dma(dst[:, k, :], V(v[:, k, col0:col0 + ncols], None), q="pool")

    def phase_proj(l, s):
        C = Ctx(nc, S)
        w = C.sb([128, 8, P_COLS], BF16, "win")
        load_w_bf16(w, P["w_in"][l], 8, P_COLS)
        g_bc = C.sb([128, D], F32, "gbc")
        b_bc = C.sb([128, D], F32, "bbc")
        if l == 0:
            S.dma(g_bc[:, :], V(bcast_rows(P["ln_in_g"], D), None))
            S.dma(b_bc[:, :], V(bcast_rows(P["ln_in_b"], D), None))
        CC = {"st": Ring([C.sb([128, 16], F32, "st") for _ in range(4)])}
        xtk = Ring([C.sb([128, D], F32, "xtk") for _ in range(6)])
        xT = Ring([C.sb([128, 8, 512], BF16, "xT") for _ in range(2)])
        pst = Ring([C.ps([128, 512], F32, "pst") for _ in range(3)])
        psm = Ring([C.ps([128, 512], F32, "psm") for _ in range(4)])
        stA = Ring([C.sb([128, 512], F32, "stA") for _ in range(4)])
        stB = Ring([C.sb([128, 512], BF16, "stB") for _ in range(4)])
        ev = 0
        for t5 in range(NT5):
            t0 = t5 * 512
            xt_t = xT.next()
            blocks = []
            for j in range(4):
                xb = xtk.next()
                r0 = t0 + j * 128
                if l == 0:
                    S.dma(xb[:, :], V(x_in[s, r0:r0 + 128, :], None))
                    layer_norm_rows(CC, xb[:, :], g_bc[:, :], b_bc[:, :], xb[:, :])
                    S.dma(xres[r0:r0 + 128, :], xb[:, :], q="sp")
                else:
                    S.dma(xb[:, :], xres[r0:r0 + 128, :])
                blocks.append(xb)
            for k in range(8):
                p = pst.next()
                for j in range(4):
                    S.tr(p[:, j * 128:(j + 1) * 128], blocks[j][:, k * 128:(k + 1) * 128], ident[:, :])
                if k % 2 == 0:
                    S.copy(xt_t[:, k, :], p[:, :], eng="act")
                else:
                    S.copy(xt_t[:, k, :], p[:, :], eng="dve")
            for c in range(22):
                if c < 14:
                    c0 = c * 128
                    m = min(128, A_COLS - c0)
                else:
                    c0 = A_COLS + (c - 14) * 128
                    m = 128
                p = psm.next()
                for k in range(8):
                    S.mm(p[0:m, :], w[:, k, c0:c0 + m], xt_t[:, k, :], start=(k == 0), stop=(k == 7))
                if c < 14:
                    o = stA.next()
                    S.copy(o[0:m, :], p[0:m, :], eng=("act" if ev % 2 == 0 else "dve"))
                    S.dma(pa[c0:c0 + m, t0:t0 + 512], o[0:m, :])
                else:
                    o = stB.next()
                    S.copy(o[:, :], p[:, :], eng=("act" if ev % 2 == 0 else "dve"))
                    S.dma(qk[(c - 14) * 128:(c - 13) * 128, t0:t0 + 512], o[:, :])
                ev += 1
            for j in range(4):
                p = psm.next()
                for k in range(8):
                    S.mm(p[:, :], xt_t[:, k, j * 128:(j + 1) * 128], w[:, k, A_COLS + 1024:A_COLS + 1536],
                         start=(k == 0), stop=(k == 7))
                o = stB.next()
                S.copy(o[:, :], p[:, :], eng=("act" if ev % 2 == 0 else "dve"))
                ev += 1
                S.dma(vb[t0 + j * 128:t0 + (j + 1) * 128, :], o[:, :])
        C.close()

    def load_cols(C, dst, src1d, n, pst, stg):
        S.dma(stg[0:n, :], V(src1d.rearrange("(c p) -> c p", p=128), None))
        S.tr(pst[:, 0:n], stg[0:n, :], ident[0:n, 0:n])
        S.copy(dst, pst[:, 0:n])

    def phase_setup(l_, s_):
        C = Ctx(nc, S)
        for l in range(DEPTH):
            v = P["ffn_up"][l].rearrange("(k p) c -> p k c", p=128)
            for c in range(44):
                stg = None
                S.dma(V(upbf.t[l, c].rearrange("p (k c) -> p k c", k=8), upbf.d), V(v[:, :, c * 128:(c + 1) * 128], None),
                      q="pool")
        rb = C.sb([32, 4], F32, "rb")
        oh = C.sb([32, NB_LEN + 1], F32, "oh")
        gsb = C.sb([4, NB_LEN + 1], F32, "gsb")
        S.dma(rb[:, :], V(P["rel_bias"], None))
        S.dma(oh[:, :], V(P["c_onehot"], None))
        pp = C.ps([128, 512], F32, "pp")
        for c0 in (0, 512, 1024):
            n = min(512, NB_LEN + 1 - c0)
            S.mm(pp[0:4, 0:n], rb[:, :], oh[:, c0:c0 + n])
            S.copy(gsb[:, c0:c0 + n], pp[0:4, 0:n])
        S.dma(gsc[:, :], gsb[:, :])
        for h in range(4):
            dst = bass.AP(tensor=hsc.t.tensor, offset=hsc.t.offset + h * 128 * 1536, ap=[[1537, 128], [1, NB_LEN + 1]])
            src = bass.AP(tensor=gsc.t.tensor, offset=gsc.t.offset + h * (NB_LEN + 1), ap=[[0, 128], [1, NB_LEN + 1]])
            S.dma(V(dst, hsc.d), V(src, gsc.d))
        C.close()

    def phase_attn(l, s):
        C = Ctx(nc, S)
        lam_init = 0.8 - 0.6 * math.exp(-0.3 * l)
        lt = C.sb([1, 4, 64], F32, "lt")
        l2 = C.sb([1, 8], F32, "l2")
        ones1 = C.sb([1, 128], F32, "ones1")
        nlam = C.sb([128, 1], F32, "nlam")
        subg = C.sb([128, 128], F32, "subg")
        S.dma(lt[:, :, :], V(P["lam"][l:l + 1], None))
        S.memset(ones1[:, :], 1.0)
        S.tt(lt[:, 0, :], lt[:, 0, :], lt[:, 1, :], ALU.mult)
        S.tt(lt[:, 2, :], lt[:, 2, :], lt[:, 3, :], ALU.mult)
        S.op("dve", lambda e: e.reduce_sum(out=l2.t[:, 0:1], in_=lt.t[:, 0, :], axis=mybir.AxisListType.X), [lt.d], [l2.d])
        S.op("dve", lambda e: e.reduce_sum(out=l2.t[:, 1:2], in_=lt.t[:, 2, :], axis=mybir.AxisListType.X), [lt.d], [l2.d])
        S.act(l2[:, 2:4], l2[:, 0:2], AF.Exp)
        S.tt(l2[:, 4:5], l2[:, 3:4], l2[:, 2:3], ALU.subtract)
        S.ts(l2[:, 5:6], l2[:, 4:5], -lam_init, None, ALU.add)
        accA = [C.ps([128, 512], F32, "accA") for _ in range(2)]
        accB = [C.ps([128, 512], F32, "accB") for _ in range(2)]
        pl = accB[0]
        S.mm(pl[:, 0:1], ones1[:, :], l2[:, 5:6])
        S.copy(nlam[:, :], pl[:, 0:1])
        S.dma(subg[:, :], V(bcast_rows(P["subln_g"][l], 128), None))
        S.ts(subg[:, :], subg[:, :], 1.0 - lam_init, None, ALU.mult)

        KT = Ring([C.sb([128, T], BF16, "KT") for _ in range(2)])
        QT = Ring([[C.sb([128, T], BF16, "QT") for _ in range(2)] for _ in range(2)])
        VA = Ring([C.sb([128, NCH, 130], BF16, "VA") for _ in range(2)])
        BT = Ring([C.sb([128, 6, 512], F32, "BT") for _ in range(2)])
        CB = Ring([C.sb([128, 2], F32, "CB") for _ in range(2)])
        stR = Ring([C.ps([128, 512], F32, "st") for _ in range(3)])
        ptr = C.ps([128, 512], BF16, "ptr")
        ER = Ring([C.sb([128, 512], BF16, "E") for _ in range(4)])
        TMP = Ring([C.sb([128, 512], F32, "tmp") for _ in range(3)])
        ON = [C.sb([128, 4, 128], F32, "on") for _ in range(2)]
        SM = Ring([C.sb([128, 8], F32, "sm") for _ in range(8)])
        OB = Ring([C.sb([128, 128], BF16, "ob") for _ in range(3)])
        OF = Ring([C.sb([128, 128], F32, "of") for _ in range(3)])
        OST = Ring([C.sb([128, 512], BF16, "ost") for _ in range(2)])
        for h in range(4):
            kt_ = KT.next(); qt_ = QT.next(); va = VA.next(); bt = BT.next(); cb = CB.next()
            S.dma(kt_[:, :], qk[512 + h * 128:512 + (h + 1) * 128, :])
            S.dma(qt_[0][0:64, :], qk[h * 128:h * 128 + 64, :])
            S.memset(qt_[0][64:128, :], 0.0, eng="pool")
            S.dma(qt_[1][64:128, :], qk[h * 128 + 64:(h + 1) * 128, :])
            S.memset(qt_[1][0:64, :], 0.0, eng="pool")
            S.dma(va[:, :, 0:128], V(vb.t[:, h * 128:(h + 1) * 128].rearrange("(n p) e -> p n e", p=128), vb.d))
            S.memset(va[:, :, 128:130], 1.0, eng="pool")
            for rho in range(6):
                c_ = 639 - (rho - 1) * 128
                S.dma(bt[:, rho, :], V(hsc.t[h, :, c_:c_ + 512], hsc.d))
            S.dma(cb[:, 0:1], V(bass.AP(tensor=gsc.t.tensor, offset=gsc.t.offset + h * (NB_LEN + 1) + 1278, ap=[[0, 128], [1, 1]]), gsc.d))
            S.dma(cb[:, 1:2], V(bass.AP(tensor=gsc.t.tensor, offset=gsc.t.offset + h * (NB_LEN + 1), ap=[[0, 128], [1, 1]]), gsc.d))
            def acc_of(c, qb):
                return (accA[c], qb * 130) if qb < 3 else (accB[c], 0)

            tiles = [(q5, c, kt) for q5 in range(NT5) for c in range(2) for kt in range(NCH)]
            LOOK = 2
            Es = {}
            for idx in range(len(tiles) + LOOK):
                if idx < len(tiles):
                    q5, c, kt = tiles[idx]
                    pb = c * 64
                    st = stR.next()
                    S.mm(st[:, :], kt_[:, kt * 128:(kt + 1) * 128], qt_[c][:, q5 * 512:(q5 + 1) * 512])
                    rho = kt - 4 * q5 + 1
                    E = ER.next()
                    if 0 <= rho <= 5:
                        tmp = TMP.next()
                        S.stt(tmp[:, :], st[:, :], 0.125, bt[:, rho, :], ALU.mult, ALU.add)
                        S.act(E[:, :], tmp[:, :], AF.Exp)
                    else:
                        S.act(E[:, :], st[:, :], AF.Exp, bias=(cb[:, 0:1] if rho < 0 else cb[:, 1:2]), scale=0.125)
                    Es[idx] = E
                j = idx - LOOK
                if j < 0:
                    continue
                q5, c, kt = tiles[j]
                E = Es.pop(j)
                for qb in range(4):
                    a, o = acc_of(c, qb)
                    S.mm(a[:, o:o + 130], E[:, qb * 128:(qb + 1) * 128], va[:, kt, 0:130],
                         start=(kt == 0 and qb in (0, 3)), stop=(kt == NCH - 1))
                if kt != NCH - 1:
                    continue
                for qb in range(4):
                    a, o = acc_of(c, qb)
                    sm = SM.next()
                    S.op("dve", lambda e, sm=sm, a=a, o=o: e.reciprocal(out=sm.t[:, 0:1], in_=a.t[:, o + 128:o + 129]),
                         [a.d], [sm.d])
                    S.ts(ON[c][:, qb, :], a[:, o:o + 128], sm[:, 0:1], None, ALU.mult)
                if c == 0:
                    continue
                ost = OST.next()
                for qb in range(4):
                    of = OF.next(); ob = OB.next(); sm = SM.next()
                    S.stt(of[:, :], ON[1][:, qb, :], nlam[:, 0:1], ON[0][:, qb, :], ALU.mult, ALU.add)
                    S.act(ob[:, :], of[:, :], AF.Square, accum=sm[:, 0:1])
                    S.act(sm[:, 1:2], sm[:, 0:1], AF.Sqrt, bias=epsln[:, 0:1], scale=1.0 / 128.0)
                    S.op("dve", lambda e, sm=sm: e.reciprocal(out=sm.t[:, 2:3], in_=sm.t[:, 1:2]), [sm.d], [sm.d])
                    S.stt(ob[:, :], of[:, :], sm[:, 2:3], subg[:, :], ALU.mult, ALU.mult)
                    S.tr(ptr[:, qb * 128:(qb + 1) * 128], ob[:, :], identb[:, :])
                S.copy(ost[:, :], ptr[:, :])
                S.dma(mix[512 + h * 128:512 + (h + 1) * 128, q5 * 512:(q5 + 1) * 512], ost[:, :])
        C.close()

    def phase_wout(l, s):
        C = Ctx(nc, S)
        w = C.sb([128, 8, D], BF16, "wout")
        load_w_bf16(w, P["w_out"][l], 8, D)
        g_bc = C.sb([128, D], F32, "gbc"); b_bc = C.sb([128, D], F32, "bbc")
        S.dma(g_bc[:, :], V(bcast_rows(P["ln1_g"][l], D), None))
        S.dma(b_bc[:, :], V(bcast_rows(P["ln1_b"][l], D), None))
        CC = {"st": Ring([C.sb([128, 16], F32, "st") for _ in range(4)])}
        MT = Ring([C.sb([128, 8, 512], BF16, "mixT") for _ in range(2)])
        XB = Ring([C.sb([128, D], F32, "xb") for _ in range(4)])
        HB = Ring([C.sb([128, D], F32, "hb") for _ in range(4)])
        PS = Ring([C.ps([128, 1024], F32, "pso") for _ in range(3)])
        for t5 in range(NT5):
            t0 = t5 * 512
            mt = MT.next()
            S.dma(mt[:, :, :], V(mix.t[:, t0:t0 + 512].rearrange("(k p) t -> p k t", p=128), mix.d))
            for j in range(4):
                r0 = t0 + j * 128
                xb = XB.next(); hb = HB.next(); p = PS.next()
                S.dma(xb[:, :], xres[r0:r0 + 128, :])
                for half in range(2):
                    for k in range(8):
                        S.mm(p[:, half * 512:(half + 1) * 512], mt[:, k, j * 128:(j + 1) * 128],
                             w[:, k, half * 512:(half + 1) * 512], start=(k == 0), stop=(k == 7))
                S.stt(hb[:, :], xb[:, :], ALPHA, p[:, :], ALU.mult, ALU.add)
                layer_norm_rows(CC, hb[:, :], g_bc[:, :], b_bc[:, :], hb[:, :])
                S.dma(x1s[r0:r0 + 128, :], hb[:, :])
        C.close()

    def phase_ffn(l, s):
        C = Ctx(nc, S)
        wd = C.sb([128, 22, D], BF16, "wdown")
        load_w_bf16(wd, P["ffn_down"][l], 22, D)
        g_bc = C.sb([128, D], F32, "gbc"); b_bc = C.sb([128, D], F32, "bbc")
        S.dma(g_bc[:, :], V(bcast_rows(P["ln2_g"][l], D), None))
        S.dma(b_bc[:, :], V(bcast_rows(P["ln2_b"][l], D), None))
        cw = C.sb([128, 4, 44], F32, "cw")
        stg = C.sb([64, 128], F32, "stg")
        pst = Ring([C.ps([128, 512], F32, "pst") for _ in range(2)])
        for j in range(3):
            load_cols(C, cw[:, j, :], P["conv_w"][l, j], 44, pst.next(), stg)
        load_cols(C, cw[:, 3, :], P["conv_b"][l], 44, pst.next(), stg)
        CC = {"st": Ring([C.sb([128, 16], F32, "st") for _ in range(4)])}
        XB = Ring([C.sb([128, D], F32, "xb") for _ in range(8)])
        XH = Ring([C.sb([2, D], F32, "xh") for _ in range(2)])
        XT = Ring([C.sb([128, 8, 512], BF16, "xT") for _ in range(2)])
        XHT = Ring([C.sb([128, 8, 2], BF16, "xhT") for _ in range(2)])
        WC = Ring([C.sb([128, 8 * 128], BF16, "wc") for _ in range(4)])
        PH = Ring([C.ps([128, 512], F32, "ph") for _ in range(2)])
        PHHS = Ring([(C.ps([128, 512], F32, "phh"), 0) for i in range(2)])
        PD = Ring([C.ps([128, 1024], F32, "pd") for _ in range(1)])
        HR = Ring([C.sb([128, 514], F32, "hr") for _ in range(6)])
        OO = Ring([C.sb([128, 512], F32, "oo") for _ in range(4)])
        GA = C.sb([128, 22, 512], BF16, "ga")
        GT = C.sb([128, 22, 512], BF16, "gt")
        HB = Ring([C.sb([128, D], F32, "hb") for _ in range(3)])
        for t5 in range(NT5):
            t0 = t5 * 512
            xt = XT.next(); xh = XH.next(); xht = XHT.next()
            blocks = []
            for j in range(4):
                xb = XB.next()
                S.dma(xb[:, :], x1s[t0 + j * 128:t0 + (j + 1) * 128, :])
                blocks.append(xb)
            S.memset(xh[:, :], 0.0, eng="pool")
            if t0 > 0:
                S.dma(xh[0:1, :], x1s[t0 - 1:t0, :])
            if t0 + 512 < T:
                S.dma(xh[1:2, :], x1s[t0 + 512:t0 + 513, :])
            for k in range(8):
                p = pst.next()
                for j in range(4):
                    S.tr(p[:, j * 128:(j + 1) * 128], blocks[j][:, k * 128:(k + 1) * 128], ident[:, :])
                S.copy(xt[:, k, :], p[:, :], eng=("act" if k % 2 == 0 else "dve"))
            p = pst.next()
            for k in range(8):
                S.tr(p[:, 2 * k:2 * k + 2], xh[0:2, k * 128:(k + 1) * 128], ident[0:2, 0:2])
            S.copy(xht[:, :, :], V(p.t[:, 0:16].rearrange("p (k t) -> p k t", t=2), p.d))
            def part_a(c):
                wc = WC.next()
                S.dma(wc[:, :], upbf[l, c])
                ph = PH.next()
                for k in range(8):
                    S.mm(ph[:, :], wc[:, k * 128:(k + 1) * 128], xt[:, k, :], start=(k == 0), stop=(k == 7))
                PHH, ho = PHHS.next()
                for k in range(8):
                    S.mm(PHH[:, ho:ho + 2], wc[:, k * 128:(k + 1) * 128], xht[:, k, :], start=(k == 0), stop=(k == 7))
                hr = HR.next()
                S.copy(hr[:, 1:513], ph[:, :], eng="act")
                S.copy(hr[:, 0:1], PHH[:, ho:ho + 1], eng="act")
                S.copy(hr[:, 513:514], PHH[:, ho + 1:ho + 2], eng="act")
                return hr

            def part_b1(c, hr):
                oo = OO.next()
                if c < 22:
                    S.ts(oo[:, :], hr[:, 1:513], cw[:, 1, c:c + 1], cw[:, 3, c:c + 1], ALU.mult, ALU.add, eng="pool")
                else:
                    S.act(oo[:, :], hr[:, 1:513], AF.Identity, bias=cw[:, 3, c:c + 1], scale=cw[:, 1, c:c + 1])
                S.stt(oo[:, :], hr[:, 0:512], cw[:, 0, c:c + 1], oo[:, :], ALU.mult, ALU.add)
                return oo

            def part_b2(c, hr, oo):
                S.stt(oo[:, :], hr[:, 2:514], cw[:, 2, c:c + 1], oo[:, :], ALU.mult, ALU.add)
                if c < 22:
                    S.act(GA[:, c, :], oo[:, :], AF.Gelu)
                else:
                    S.tt(GT[:, c - 22, :], oo[:, :], GA[:, c - 22, :], ALU.mult, eng="pool")

            hrs = {}
            oos = {}
            for c in range(47):
                if c < 44:
                    hrs[c] = part_a(c)
                if 2 <= c < 46:
                    oos[c - 2] = part_b1(c - 2, hrs[c - 2])
                if c >= 3:
                    part_b2(c - 3, hrs.pop(c - 3), oos.pop(c - 3))
            for j in range(4):
                r0 = t0 + j * 128
                p = PD.next(); hb = HB.next()
                for half in range(2):
                    for k in range(22):
                        S.mm(p[:, half * 512:(half + 1) * 512], GT[:, k, j * 128:(j + 1) * 128],
                             wd[:, k, half * 512:(half + 1) * 512], start=(k == 0), stop=(k == 21))
                S.stt(hb[:, :], blocks[j][:, :], ALPHA, p[:, :], ALU.mult, ALU.add)
                layer_norm_rows(CC, hb[:, :], g_bc[:, :], b_bc[:, :], hb[:, :])
                if l == DEPTH - 1:
                    S.dma(V(y_out.t[s, r0:r0 + 128, :], y_out.d), hb[:, :])
                else:
                    S.dma(xres[r0:r0 + 128, :], hb[:, :])
        C.close()

    def RV(v, pattern, **kw):
        return V(v.ap.rearrange(pattern, **kw), v.d)

    def phase_rwkv(l, s):
        C = Ctx(nc, S)
        RA = Ring([C.ps([128, 512], F32, "ra") for _ in range(6)])
        pstc = RA.tiles[0]
        stg = C.sb([64, 128], F32, "stg")
        mu = C.sb([128, 3, 14], F32, "mu")
        for j in range(2):
            S.memset(stg[0:14, :], 0.0)
            S.dma(stg[0:13, :], V(P["shift_mu"][l, j, 0:1664].rearrange("(c p) -> c p", p=128), None))
            S.dma(stg[13:14, 0:96], V(P["shift_mu"][l, j, 1664:1760].rearrange("(c p) -> c p", p=96), None))
            S.tr(pstc[:, 0:14], stg[0:14, :], ident[0:14, 0:14])
            S.copy(mu[:, 1 + j, :], pstc[:, 0:14])
        S.tt(mu[:, 0, :], mu[:, 1, :], mu[:, 2, :], ALU.add)
        S.ts(mu[:, 0, :], mu[:, 0, :], -1.0, 1.0, ALU.mult, ALU.add)
        PRM = C.sb([128, 12, 4], F32, "prm")
        srcs = [P["w0"][l, 0], P["w0"][l, 1], P["a0"][l, 0], P["a0"][l, 1], P["k_k"][l], P["k_a"][l], P["r_k"][l],
                P["gn_g"][l], P["gn_b"][l], P["vres_0"][0]]
        for i, sap in enumerate(srcs):
            load_cols(C, PRM[:, i, :], sap, 4, pstc, stg)
        S.ts(PRM[:, 10, :], PRM[:, 5, :], -1.0, 1.0, ALU.mult, ALU.add)
        S.ts(PRM[:, 11, :], PRM[:, 5, :], -2.0, 2.0, ALU.mult, ALU.add)
        dup = C.sb([64, AW], BF16, "dup")
        iup = [C.sb([128, AW], BF16, "iup") for _ in range(2)]
        gup = C.sb([96, AW], BF16, "gup")
        vdn = C.sb([128, 4, 32], BF16, "vdn")
        vup = C.sb([32, AW], BF16, "vup")
        S.dma(dup[:, :], V(P["decay_up"][l].rearrange("d r c -> (d r) c"), None), q="pool")
        for dd in range(2):
            S.memset(iup[dd][64:128, :], 0.0)
            S.dma(iup[dd][64 + 32 * dd:96 + 32 * dd, :], V(P["iclr_up"][l, dd], None), q="pool")
        S.dma(gup[:, :], V(P["g_up"][l], None), q="pool")
        S.dma(vdn[:, :, :], V(P["vres_down"][0].rearrange("(f p) r -> p f r", p=128), None), q="pool")
        S.dma(vup[:, :], V(P["vres_up"][0], None), q="pool")
        blkm = C.sb([128, 128], BF16, "blkm")
        S.ts(blkm[:, :], blk[:, :], 1.0 / 64.0, None, ALU.mult)
        idblk = C.sb([128, 64], F32, "idblk")
        S.tt(idblk[:, :], ident[:, 0:64], ident[:, 64:128], ALU.add)
        ones = C.sb([128, 128], F32, "ones")
        S.memset(ones[:, :], 1.0)
        eps12 = C.sb([128, 1], F32, "eps12")
        S.memset(eps12[:, :], 1e-12)
        epsgn = C.sb([128, 1], F32, "epsgn")
        S.memset(epsgn[:, :], GN_EPS)
        MS2 = C.sb([128, 2, 128], F32, "ms2")
        MI2 = C.sb([128, 2, 128], F32, "mi2")
        MT2 = C.sb([128, 2, 128], F32, "mt2")

        _stage(1)
        XIN = Ring([C.sb([128, 14, 130], F32, "xin") for _ in range(2)])
        SH = [[C.sb([128, 128], F32, "sh") for _ in range(14)] for _ in range(2)]
        PTBR = Ring([C.ps([128, 1024], BF16, "ptb") for _ in range(2)])
        TWt = C.sb([64, 128], BF16, "tw")
        ADt = C.sb([128, 128], BF16, "ad")
        SGt = C.sb([96, 128], BF16, "sg")
        VD = C.sb([32, 128], BF16, "vd")

        def fset(n, dt, shape=(128, 128), name="f"):
            return [[C.sb(list(shape), dt, name) for _ in range(4)] for _ in range(n)]
        (Ee, Cu, La, Gm, Gi, Gp, Ge, K0, Kk, Bb, Kd, T1, Yy, Dl, Yn, T2) = fset(16, F32)
        A0 = [fset(1, F32)[0] for _ in range(2)]
        A1 = [fset(1, F32)[0] for _ in range(2)]
        Vv = [fset(1, F32)[0] for _ in range(2)]
        SMs = [[C.sb([128, 8], F32, "sms") for _ in range(4)] for _ in range(2)]
        (SQ, BH, KH, VB16, RH, YB, RKV, OA) = fset(8, BF16)
        AT = [fset(1, BF16)[0] for _ in range(2)]
        VT = [fset(1, BF16)[0] for _ in range(2)]
        AR = [[C.sb([128, 256], BF16, "ar") for _ in range(4)] for _ in range(2)]
        BK = [[C.sb([128, 256], BF16, "bk") for _ in range(4)] for _ in range(2)]
        DI = [[C.sb([128, 2, 192], BF16, "di") for _ in range(4)] for _ in range(2)]
        DX = [[C.sb([128, 2, 192], BF16, "dx") for _ in range(4)] for _ in range(2)]
        GQa = [[C.sb([128, 2, 192], BF16, "gqa") for _ in range(4)] for _ in range(2)]
        GQ = [C.sb([128, 2, 192], BF16, "gq") for _ in range(4)]
        MTAK = [C.sb([128, 2, 128], BF16, "mtak") for _ in range(4)]
        PW = [[C.sb([128, 2, 2, 128], BF16, "pw") for _ in range(4)] for _ in range(3)]
        PM = [C.sb([128, 64], BF16, "pm") for _ in range(4)]
        ST = [[C.sb([128, 64], BF16, "st") for _ in range(4)] for _ in range(2)]
        VF = [C.sb([128, 128], F32, "vf") for _ in range(4)]
        YF = [C.sb([128, 128], F32, "yf") for _ in range(4)]

        def rr(gens):
            gens = list(gens)
            while gens:
                for g in list(gens):
                    try:
                        next(g)
                    except StopIteration:
                        gens.remove(g)
                yield

        def prep(d, ci, par):
            c0 = ci * 128
            xin = XIN.next()
            sh = SH[par]
            lo = max(c0 - 1, 0)
            hi = min(c0 + 129, T)
            if c0 == 0:
                S.memset(xin[:, :, 0:1], 0.0, eng="pool")
            if c0 + 129 > T:
                S.memset(xin[:, :, 129:130], 0.0, eng="pool")
            o0 = lo - (c0 - 1)
            S.dma(xin[:, :, o0:o0 + (hi - lo)], V(pa.t[:, lo:hi].rearrange("(r p) t -> p r t", p=128), pa.d))
            for rt in range(14):
                S.act(sh[rt][:, :], xin[:, rt, 1:129], AF.Copy, scale=mu[:, 0, rt:rt + 1])
                if rt % 4 == 3:
                    yield
            for rt in range(14):
                S.stt(sh[rt][:, :], xin[:, rt, 0:128], mu[:, 1, rt:rt + 1], sh[rt][:, :], ALU.mult, ALU.add)
                if rt % 4 == 3:
                    yield
            for rt in range(14):
                S.stt(sh[rt][:, :], xin[:, rt, 2:130], mu[:, 2, rt:rt + 1], sh[rt][:, :], ALU.mult, ALU.add)
                if rt % 4 == 3:
                    yield
            r_ = sh[0:4]; k_ = sh[4:8]; v_ = sh[8:12]
            S.act(TWt[32 * d:32 * d + 32, :], sh[12][32 * d:32 * d + 32, :], AF.Tanh)
            S.copy(ADt[64:128, :], sh[12][64:128, :], eng="pool")
            def decay_chain(ft):
                fs = slice(ft * 128, (ft + 1) * 128)
                pz = RA.next()
                S.mm(pz[:, 0:128], dup[32 * d:32 * d + 32, fs], TWt[32 * d:32 * d + 32, :])
                S.act(Ee[ft][:, :], pz[:, 0:128], AF.Sigmoid, bias=PRM[:, d, ft:ft + 1], scale=1.0)
                yield
                for dd in ((0,) if d == 0 else (0, 1)):
                    pq = RA.next()
                    S.mm(pq[:, 0:128], iup[dd][64:128, fs], ADt[64:128, :])
                    S.act((A0 if dd == 0 else A1)[par][ft][:, :], pq[:, 0:128], AF.Sigmoid, bias=PRM[:, 2 + dd, ft:ft + 1], scale=1.0)
                S.op("dve", lambda e, ft=ft: e.tensor_tensor_scan(out=Cu[ft].t[:, :], data0=ones.t[:, :], data1=Ee[ft].t[:, :],
                                                                   initial=0.0, op0=ALU.mult, op1=ALU.add),
                     [ones.d, Ee[ft].d], [Cu[ft].d])
                yield
                sm = SMs[par][ft]
                if d == 0:
                    lam_t = Cu[ft]
                else:
                    S.stt(La[ft][:, :], Cu[ft][:, :], -1.0, Ee[ft][:, :], ALU.mult, ALU.add)
                    yield
                    S.ts(La[ft][:, :], La[ft][:, :], Cu[ft][:, 127:128], None, ALU.add)
                    lam_t = La[ft]
                S.ts(sm[:, 0:1], Cu[ft][:, 127:128], -CEXP, None, ALU.mult)
                yield
                S.act(sm[:, 1:2], Cu[ft][:, 127:128], AF.Exp, scale=-CEXP)
                S.act(Gm[ft][:, :], lam_t[:, :], AF.Exp, scale=-CEXP)
                S.tt(Gp[ft][:, :], lam_t[:, :], Ee[ft][:, :], ALU.subtract, eng="pool")
                yield
                S.act(Gi[ft][:, :], lam_t[:, :], AF.Exp, scale=CEXP)
                S.act(Ge[ft][:, :], lam_t[:, :], AF.Exp, bias=sm[:, 0:1], scale=CEXP)
                yield
                S.act(Gp[ft][:, :], Gp[ft][:, :], AF.Exp, scale=-CEXP)
                yield

            yield from rr([decay_chain(ft) for ft in range(4)])
            for ft in range(4):
                S.copy(Vv[par][ft][:, :], v_[ft][:, :], eng="pool")
            if l == 1:
                pvd = RA.next()
                for ft in range(4):
                    S.copy(VB16[ft][:, :], Vv[par][ft][:, :], eng="pool")
                    S.mm(pvd[0:32, 0:128], vdn[:, ft, :], VB16[ft][:, :], start=(ft == 0), stop=(ft == 3))
                S.copy(VD[:, :], pvd[0:32, 0:128], eng="act")
                yield
                def vres_chain(ft):
                    fs = slice(ft * 128, (ft + 1) * 128)
                    S.dma(VF[ft][:, :], vfs[ft * 128:(ft + 1) * 128, c0:c0 + 128])
                    pg = RA.next()
                    S.mm(pg[:, 0:128], vup[:, fs], VD[:, :])
                    S.act(T1[ft][:, :], pg[:, 0:128], AF.Sigmoid, bias=PRM[:, 9, ft:ft + 1], scale=1.0)
                    S.tt(Dl[ft][:, :], VF[ft][:, :], Vv[par][ft][:, :], ALU.subtract, eng="pool")
                    yield
                    S.tt(Dl[ft][:, :], Dl[ft][:, :], T1[ft][:, :], ALU.mult, eng="pool")
                    yield
                    S.tt(Vv[par][ft][:, :], Vv[par][ft][:, :], Dl[ft][:, :], ALU.add, eng="pool")
                    yield

                yield from rr([vres_chain(ft) for ft in range(4)])
            elif d == 0:
                for ft in range(4):
                    S.dma(vfs[ft * 128:(ft + 1) * 128, c0:c0 + 128], Vv[par][ft][:, :])
            def kk_chain(ft):
                S.ts(K0[ft][:, :], k_[ft][:, :], PRM[:, 4, ft:ft + 1], None, ALU.mult)
                S.copy(VB16[ft][:, :], Vv[par][ft][:, :], eng="pool")
                yield
                S.act(SQ[ft][:, :], K0[ft][:, :], AF.Square)
                yield
                Ad = (A0 if d == 0 else A1)[par][ft]
                S.ts(T1[ft][:, :], Ad[:, :], PRM[:, 5, ft:ft + 1], PRM[:, 10, ft:ft + 1], ALU.mult, ALU.add)
                S.tt(AR[par][ft][:, 128:256], r_[ft][:, :], Gm[ft][:, :], ALU.mult, eng="pool")
                yield
                pss = RA.next()
                S.mm(pss[:, 0:128], blk[:, :], SQ[ft][:, :])
                S.act(Kk[ft][:, :], pss[:, 0:128], AF.Sqrt, bias=eps12[:, 0:1], scale=1.0)
                S.tt(Kd[ft][:, :], k_[ft][:, :], T1[ft][:, :], ALU.mult, eng="pool")
                yield
                S.op("dve", lambda e, ft=ft: e.reciprocal(out=Kk[ft].t[:, :], in_=Kk[ft].t[:, :]), [Kk[ft].d], [Kk[ft].d])
                S.tt(BK[par][ft][:, 128:256], Kd[ft][:, :], Gi[ft][:, :], ALU.mult, eng="pool")
                yield
                S.tt(Kk[ft][:, :], Kk[ft][:, :], K0[ft][:, :], ALU.mult)
                S.tt(KH[ft][:, :], Kd[ft][:, :], Ge[ft][:, :], ALU.mult, eng="pool")
                yield
                S.stt(AR[par][ft][:, 0:128], Kk[ft][:, :], -1.0, Gp[ft][:, :], ALU.mult, ALU.mult)
                S.tt(Bb[ft][:, :], Kk[ft][:, :], Ad[:, :], ALU.mult)
                yield
                S.tt(BK[par][ft][:, 0:128], Bb[ft][:, :], Gi[ft][:, :], ALU.mult, eng="pool")
                S.tt(BH[ft][:, :], Bb[ft][:, :], Ge[ft][:, :], ALU.mult)
                yield

            yield from rr([kk_chain(ft) for ft in range(4)])
            for ft in range(4):
                PTB = PTBR.next()
                S.tr(PTB[:, 0:128], AR[par][ft][:, 0:128], identb[:, :])
                S.tr(PTB[:, 128:256], BH[ft][:, :], identb[:, :])
                S.tr(PTB[:, 256:384], KH[ft][:, :], identb[:, :])
                S.tr(PTB[:, 384:512], VB16[ft][:, :], identb[:, :])
                S.copy(AT[par][ft][:, :], PTB[:, 0:128], eng="dve")
                for h in range(2):
                    S.copy(DI[par][ft][:, h, 128:192], PTB[:, 128 + h * 64:192 + h * 64], eng="dve")
                for h in range(2):
                    S.copy(GQa[par][ft][:, h, 128:192], PTB[:, 256 + h * 64:320 + h * 64], eng="dve")
                S.copy(VT[par][ft][:, :], PTB[:, 384:512], eng="dve")
                yield

        def mats(d, ci, par, sti):
            c0 = ci * 128
            sh = SH[par]
            r_ = sh[0:4]; k_ = sh[4:8]
            for ft in range(4):
                for h in range(2):
                    hb = h * 64
                    pA = RA.next()
                    S.mm(pA[:, 0:256], BK[par][ft][hb:hb + 64, 0:128], AR[par][ft][hb:hb + 64, :])
                    S.tt(PW[0][ft][:, h, 0, :], pA[:, 0:128], MS2[:, h, :], ALU.mult)
                    S.tt(DI[par][ft][:, h, 0:128], pA[:, 128:256], MI2[:, h, :], ALU.mult)
                    pB = RA.next()
                    S.mm(pB[:, 0:128], BK[par][ft][hb:hb + 64, 128:256], AR[par][ft][hb:hb + 64, 128:256])
                    S.tt(GQa[par][ft][:, h, 0:128], pB[:, 0:128], MI2[:, h, :], ALU.mult)
                    pC = RA.next()
                    S.mm(pC[:, 0:256], AR[par][ft][hb:hb + 64, 0:128], BK[par][ft][hb:hb + 64, :])
                    S.tt(PW[0][ft][:, h, 1, :], pC[:, 0:128], MT2[:, h, :], ALU.mult)
                    S.tt(MTAK[ft][:, h, :], pC[:, 128:256], MT2[:, h, :], ALU.mult)
                yield
            Dc = [DI[par][ft] for ft in range(4)]
            for j in range(7):
                pwc = PW[j % 3]
                pwn = PW[(j + 1) % 3]
                for ft in range(4):
                    if j < 6:
                        pP = RA.next()
                        for h in range(2):
                            S.mm(pP[:, h * 256:h * 256 + 128], pwc[ft][:, h, 1, :], pwc[ft][:, h, 0, :])
                            S.mm(pP[:, h * 256 + 128:h * 256 + 256], pwc[ft][:, h, 0, :], pwc[ft][:, h, 1, :])
                        S.copy(RV(pwn[ft][:, :, :, :], "p h t c -> p (h t c)"), pP[:, :], eng="act")
                    pD = RA.next()
                    for h in range(2):
                        S.mm(pD[:, h * 192:(h + 1) * 192], identb[:, :], Dc[ft][:, h, :], start=True, stop=False)
                        S.mm(pD[:, h * 192:(h + 1) * 192], pwc[ft][:, h, 1, :], Dc[ft][:, h, :], start=False, stop=True)
                    Dn = DX[j % 2][ft]
                    S.copy(RV(Dn[:, :, :], "p h c -> p (h c)"), pD[:, 0:384], eng=("dve" if ft % 2 == 0 else "act"))
                    Dc[ft] = Dn
                    yield
            stc = ST[sti % 2]
            stn = ST[(sti + 1) % 2]
            for ft in range(4):
                DF = Dc[ft]
                sm = SMs[par][ft]
                pR = RA.next()
                for h in range(2):
                    hb = h * 64
                    S.mm(pR[hb:hb + 64, 0:192], AT[par][ft][:, hb:hb + 64], DF[:, h, :])
                S.tt(RH[ft][:, :], pR[:, 0:128], AR[par][ft][:, 128:256], ALU.add)
                S.stt(PM[ft][:, :], idblk[:, :], sm[:, 1:2], pR[:, 128:192], ALU.mult, ALU.add)
                pG = RA.next()
                for h in range(2):
                    S.mm(pG[:, h * 192:(h + 1) * 192], MTAK[ft][:, h, :], DF[:, h, :])
                S.tt(RV(GQ[ft][:, :, :], "p h c -> p (h c)"), pG[:, 0:384], RV(GQa[par][ft][:, :, :], "p h c -> p (h c)"), ALU.add)
                yield
            for ft in range(4):
                if d == 1:
                    S.dma(YF[ft][:, :], yfs[ft * 128:(ft + 1) * 128, c0:c0 + 128])
                for h in range(2):
                    hb = h * 64
                    pY = RA.next()
                    pS = RA.next()
                    S.mm(pY[hb:hb + 64, 0:128], stc[ft][hb:hb + 64, :], RH[ft][hb:hb + 64, :], start=True, stop=False)
                    S.mm(pY[hb:hb + 64, 0:128], VT[par][ft][:, hb:hb + 64], GQ[ft][:, h, 0:128], start=False, stop=True)
                    S.mm(pS[hb:hb + 64, 0:64], PM[ft][hb:hb + 64, :], stc[ft][hb:hb + 64, :], start=True, stop=False)
                    S.mm(pS[hb:hb + 64, 0:64], GQ[ft][:, h, 128:192], VT[par][ft][:, hb:hb + 64], start=False, stop=True)
                    S.copy(stn[ft][hb:hb + 64, :], pS[hb:hb + 64, 0:64], eng="act")
                    if d == 0:
                        S.copy(Yy[ft][hb:hb + 64, :], pY[hb:hb + 64, 0:128], eng="dve")
                    else:
                        S.tt(Yy[ft][hb:hb + 64, :], pY[hb:hb + 64, 0:128], YF[ft][hb:hb + 64, :], ALU.add)
                yield
                if d == 0:
                    S.dma(yfs[ft * 128:(ft + 1) * 128, c0:c0 + 128], Yy[ft][:, :])
            if d == 0:
                return

            def epi_chain(ft):
                fs = slice(ft * 128, (ft + 1) * 128)
                S.copy(YB[ft][:, :], Yy[ft][:, :], eng="pool")
                S.tt(T2[ft][:, :], A0[par][ft][:, :], A1[par][ft][:, :], ALU.add, eng="pool")
                yield
                pm_ = RA.next()
                S.mm(pm_[:, 0:128], blkm[:, :], YB[ft][:, :])
                S.tt(Dl[ft][:, :], Yy[ft][:, :], pm_[:, 0:128], ALU.subtract)
                S.ts(T2[ft][:, :], T2[ft][:, :], PRM[:, 5, ft:ft + 1], PRM[:, 11, ft:ft + 1], ALU.mult, ALU.add)
                yield
                S.act(SQ[ft][:, :], Dl[ft][:, :], AF.Square)
                S.tt(T2[ft][:, :], T2[ft][:, :], k_[ft][:, :], ALU.mult, eng="pool")
                yield
                pv_ = RA.next()
                S.mm(pv_[:, 0:128], blkm[:, :], SQ[ft][:, :])
                S.act(Yn[ft][:, :], pv_[:, 0:128], AF.Sqrt, bias=epsgn[:, 0:1], scale=1.0)
                S.stt(RKV[ft][:, :], r_[ft][:, :], PRM[:, 6, ft:ft + 1], T2[ft][:, :], ALU.mult, ALU.mult)
                yield
                S.op("dve", lambda e, ft=ft: e.reciprocal(out=Yn[ft].t[:, :], in_=Yn[ft].t[:, :]), [Yn[ft].d], [Yn[ft].d])
                pb_ = RA.next()
                S.mm(pb_[:, 0:128], blk[:, :], RKV[ft][:, :])
                S.tt(T2[ft][:, :], pb_[:, 0:128], Vv[par][ft][:, :], ALU.mult)
                yield
                S.tt(Yn[ft][:, :], Yn[ft][:, :], Dl[ft][:, :], ALU.mult)
                yield
                S.ts(Yn[ft][:, :], Yn[ft][:, :], PRM[:, 7, ft:ft + 1], PRM[:, 8, ft:ft + 1], ALU.mult, ALU.add)
                yield
                S.tt(Yn[ft][:, :], Yn[ft][:, :], T2[ft][:, :], ALU.add)
                pg_ = RA.next()
                S.mm(pg_[:, 0:128], gup[:, fs], SGt[:, :])
                S.tt(OA[ft][:, :], Yn[ft][:, :], pg_[:, 0:128], ALU.mult)
                S.dma(mix[ft * 128:(ft + 1) * 128, c0:c0 + 128], OA[ft][:, :])
                yield

            S.act(SGt[:, :], sh[13][0:96, :], AF.Sigmoid)
            yield from rr([epi_chain(ft) for ft in range(4)])

        def run_both(a, b):
            alive = [g for g in (a, b) if g is not None]
            while alive:
                for g in list(alive):
                    try:
                        next(g)
                    except StopIteration:
                        alive.remove(g)

        for d in range(2):
            iLs, iLi, iLt = (0, 1, 2) if d == 0 else (2, 3, 0)
            for h in range(2):
                S.copy(MS2[:, h, :], masks[:, iLs, :])
                S.copy(MI2[:, h, :], masks[:, iLi, :])
                S.copy(MT2[:, h, :], masks[:, iLt, :])
            for ft in range(4):
                S.memset(ST[0][ft][:, :], 0.0)
            order = list(range(NCH)) if d == 0 else list(range(NCH - 1, -1, -1))
            gm = None
            for n, ci in enumerate(order):
                run_both(prep(d, ci, n % 2), gm)
                gm = mats(d, ci, n % 2, n)
            run_both(gm, None)
            S.barrier()
        C.close()

    PH = {"proj": phase_proj, "setup": phase_setup, "attn": phase_attn, "wout": phase_wout, "ffn": phase_ffn,
          "rwkv": phase_rwkv}
    if plan is None:
        plan = [("setup", 0, 0)]
        for s in range(NSEQ):
            for l in range(DEPTH):
                plan += [("proj", l, s), ("rwkv", l, s), ("attn", l, s), ("wout", l, s), ("ffn", l, s)]
    for (name, l, s) in plan:
        try:
            PH[name](l, s)
        except StopBuild:
            print('STOPPED')
    S.barrier()
    return nc


def _t5_bucket_np(rel):
    nb = 16
    max_exact = 8
    ret = np.where(rel > 0, nb, 0)
    n = np.abs(rel)
    nf = np.maximum(n, 1).astype(np.float32)
    large = max_exact + (np.log(nf / np.float32(max_exact)) / np.float32(math.log(128 / max_exact))
                         * np.float32(nb - max_exact)).astype(np.int32)
    large = np.minimum(large, nb - 1)
    return ret + np.where(n < max_exact, n, large)


def _consts():
    ident = np.eye(128, dtype=np.float32)
    a = np.arange(128)[:, None]
    b = np.arange(128)[None, :]
    masks = np.stack([(a < b), (a <= b), (a > b), (a >= b)], 1).astype(np.float32)
    blk = np.zeros((128, 128), np.float32)
    blk[:64, :64] = 1
    blk[64:, 64:] = 1
    delta = 639 - np.arange(NB_LEN + 1)
    bk = _t5_bucket_np(delta.astype(np.int32))
    oh = np.zeros((32, NB_LEN + 1), np.float32)
    oh[bk, np.arange(NB_LEN + 1)] = 1
    return dict(c_ident=ident, c_masks=np.ascontiguousarray(masks), c_blk=blk, c_onehot=oh)


_NC_CACHE = {}


def kernel(**inputs):
    xp = np.asarray(inputs["x_prompt"], np.float32)
    xs = np.asarray(inputs["x_sample"], np.float32)
    T = xp.shape[1]
    allx = [xp[i] for i in range(xp.shape[0])] + [xs[i] for i in range(xs.shape[0])]
    n = len(allx)
    NC = 8
    slots = [(c, 8 + (c % 4)) for c in range(NC)]
    key = (T, 2)
    if key not in _NC_CACHE:
        _NC_CACHE[key] = build(T, 2)
    nc = _NC_CACHE[key]
    base = {k: np.ascontiguousarray(np.asarray(v, np.float32)) for k, v in inputs.items()
            if k not in ("x_prompt", "x_sample")}
    base["r_k"] = base["r_k"].reshape(2, AW)
    base.update(_consts())
    in_maps = []
    for c in range(NC):
        m = dict(base)
        m["x"] = np.ascontiguousarray(np.stack([allx[slots[c][0]], allx[slots[c][1]]], 0))
        in_maps.append(m)
    res = run_bass_kernel_spmd(nc, in_maps, core_ids=list(range(NC)))
    outs = [None] * n
    for c in range(NC):
        y = res.results[c]["y"]
        outs[slots[c][0]] = y[0]
        if c < 4:
            outs[slots[c][1]] = y[1]
    yp = np.stack(outs[:xp.shape[0]], 0).astype(np.float32)
    ys = np.stack(outs[xp.shape[0]:], 0).astype(np.float32)
    return (yp, ys)
```

```python
import contextlib
import os
import math
import numpy as np
import concourse.bass as bass
import concourse.mybir as mybir
from concourse.bass_utils import run_bass_kernel_spmd

F32 = mybir.dt.float32
BF16 = mybir.dt.bfloat16
AF = mybir.ActivationFunctionType
ALU = mybir.AluOpType

D = 1024
DEPTH = 2
AW = 512
A_COLS = 1760
P_COLS = 3296
D_FF = 2816
LN_EPS = 1e-5
GN_EPS = 64e-5
ALPHA = (2 * DEPTH) ** 0.25
NB_LEN = 1279
CEXP = math.exp(-0.5)


class StopBuild(Exception):
    pass


def _stage(n):
    if int(os.environ.get('RWKV_STOP', '99')) == n:
        raise StopBuild()


class Dep:
    __slots__ = ("w", "r")

    def __init__(self):
        self.w = {}
        self.r = {}


class V:
    __slots__ = ("ap", "d")

    def __init__(self, ap, d):
        self.ap = ap
        self.d = d


class Tile:
    def __init__(self, t, d=None):
        self.t = t
        self.d = d if d is not None else Dep()

    def __getitem__(self, idx):
        return V(self.t[idx], self.d)


class Sched:
    ENG = ("pe", "act", "dve", "pool", "sp")

    def __init__(self, nc, n_dma=28, n_sw=10):
        self.nc = nc
        self.eng = {"pe": nc.tensor, "act": nc.scalar, "dve": nc.vector, "pool": nc.gpsimd, "sp": nc.sync}
        self.sem = {e: nc.semaphore("s_" + e).__enter__() for e in self.ENG}
        self.cnt = {e: 0 for e in self.ENG}
        self.dsem = [nc.semaphore("d%d" % i).__enter__() for i in range(n_dma)]
        self.dcnt = [0] * n_dma
        self.dnext = 0
        self.n_sw = n_sw
        self.swnext = 0
        self.waited = {e: {} for e in self.ENG}
        self.uid = 0

    def _s(self, k):
        return self.sem[k] if isinstance(k, str) else self.dsem[k]

    def op(self, eng, fn, reads=(), writes=(), dma=False):
        need = {}
        for d in reads:
            if d is None:
                continue
            for k, v in d.w.items():
                if need.get(k, 0) < v:
                    need[k] = v
        for d in writes:
            if d is None:
                continue
            for k, v in d.w.items():
                if k == eng and not dma:
                    continue
                if need.get(k, 0) < v:
                    need[k] = v
            for k, v in d.r.items():
                if k == eng and not dma:
                    continue
                if need.get(k, 0) < v:
                    need[k] = v
        if dma:
            if eng == "pool":
                j = self.swnext
                self.swnext = (j + 1) % self.n_sw
            else:
                j = self.n_sw + self.dnext
                self.dnext = (self.dnext + 1) % (len(self.dsem) - self.n_sw)
            if self.dcnt[j] > 0 and need.get(j, 0) < self.dcnt[j]:
                need[j] = self.dcnt[j]
            self.dcnt[j] += 16
            tk = (j, self.dcnt[j])
        else:
            self.cnt[eng] += 1
            tk = (eng, self.cnt[eng])
        E = self.eng[eng]
        wd = self.waited[eng]
        for k, v in need.items():
            if k == "pe" and eng == "pe" and not dma:
                continue
            if wd.get(k, 0) >= v:
                continue
            wd[k] = v
            E.wait_ge(self._s(k), v)
        ins = fn(E)
        ins.then_inc(self._s(tk[0]), 16 if dma else 1)
        for d in reads:
            if d is not None:
                d.r[tk[0]] = tk[1]
        for d in writes:
            if d is not None:
                d.w[tk[0]] = tk[1]
                d.r = {}
        return tk

    def barrier(self):
        for e in self.ENG:
            E = self.eng[e]
            wd = self.waited[e]
            for k in self.ENG:
                if k != e and self.cnt[k] > wd.get(k, 0):
                    wd[k] = self.cnt[k]
                    E.wait_ge(self.sem[k], self.cnt[k])
            for j in range(len(self.dsem)):
                if self.dcnt[j] > wd.get(j, 0):
                    wd[j] = self.dcnt[j]
                    E.wait_ge(self.dsem[j], self.dcnt[j])

    def dma(self, out, in_, q="sp", **kw):
        return self.op(q, lambda e: e.dma_start(out=out.ap, in_=in_.ap, **kw), [in_.d], [out.d], dma=True)

    def mm(self, out, lhsT, rhs, start=True, stop=True, extra_reads=()):
        return self.op("pe", lambda e: e.matmul(out.ap, lhsT=lhsT.ap, rhs=rhs.ap, start=start, stop=stop,
                                                skip_group_check=True),
                       [lhsT.d, rhs.d] + list(extra_reads), [out.d])

    def tr(self, out, in_, ident):
        return self.op("pe", lambda e: e.transpose(out.ap, in_.ap, ident.ap), [in_.d, ident.d], [out.d])

    def act(self, out, in_, func, bias=None, scale=None, accum=None, eng="act"):
        kw = {}
        rd = [in_.d]
        wr = [out.d]
        if bias is not None:
            if isinstance(bias, V):
                kw["bias"] = bias.ap
                rd.append(bias.d)
            else:
                kw["bias"] = bias
        if scale is not None:
            if isinstance(scale, V):
                kw["scale"] = scale.ap
                rd.append(scale.d)
            else:
                kw["scale"] = scale
        if accum is not None:
            kw["accum_out"] = accum.ap
            wr.append(accum.d)
        return self.op("act", lambda e: e.activation(out=out.ap, in_=in_.ap, func=func, **kw), rd, wr)

    def tt(self, out, a, b, op, eng="dve"):
        return self.op(eng, lambda e: e.tensor_tensor(out=out.ap, in0=a.ap, in1=b.ap, op=op), [a.d, b.d], [out.d])

    def ts(self, out, a, s1, s2, op0, op1=None, eng="dve"):
        rd = [a.d]
        x1, x2 = s1, s2
        if isinstance(s1, V):
            rd.append(s1.d)
            x1 = s1.ap
        if isinstance(s2, V):
            rd.append(s2.d)
            x2 = s2.ap
        if op1 is None:
            return self.op(eng, lambda e: e.tensor_scalar(out=out.ap, in0=a.ap, scalar1=x1, scalar2=None, op0=op0),
                           rd, [out.d])
        return self.op(eng, lambda e: e.tensor_scalar(out=out.ap, in0=a.ap, scalar1=x1, scalar2=x2, op0=op0, op1=op1),
                       rd, [out.d])

    def stt(self, out, a, s, b, op0, op1):
        rd = [a.d, b.d]
        x = s
        if isinstance(s, V):
            rd.append(s.d)
            x = s.ap
        return self.op("dve", lambda e: e.scalar_tensor_tensor(out=out.ap, in0=a.ap, scalar=x, in1=b.ap, op0=op0, op1=op1),
                       rd, [out.d])

    def copy(self, out, in_, eng="dve"):
        if eng == "act":
            return self.act(out, in_, AF.Copy)
        return self.op(eng, lambda e: e.tensor_copy(out=out.ap, in_=in_.ap), [in_.d], [out.d])

    def memset(self, out, val, eng="dve"):
        return self.op(eng, lambda e: e.memset(out.ap, val), [], [out.d])


class Ctx:
    def __init__(self, nc, S):
        self.nc = nc
        self.S = S
        self.es = contextlib.ExitStack()
        self.n = 0

    def sb(self, shape, dt, name="t"):
        self.S.uid += 1
        return Tile(self.es.enter_context(self.nc.sbuf_tensor("%s_%d" % (name, self.S.uid), list(shape), dt)))

    def ps(self, shape, dt=F32, name="p"):
        self.S.uid += 1
        return Tile(self.es.enter_context(self.nc.psum_tensor("%s_%d" % (name, self.S.uid), list(shape), dt)))

    def close(self):
        self.S.barrier()
        self.es.close()


class Ring:
    def __init__(self, tiles):
        self.tiles = tiles
        self.i = 0

    def next(self):
        t = self.tiles[self.i % len(self.tiles)]
        self.i += 1
        return t


def bcast_rows(ap1d, n, parts=128):
    return bass.AP(tensor=ap1d.tensor, offset=ap1d.offset, ap=[[0, parts], [1, n]])


def build(T, NSEQ, dbg=False, plan=None):
    nc = bass.Bass("TRN2", target_bir_lowering=False)
    S = Sched(nc)
    NT5 = T // 512
    NCH = T // 128

    def din(name, shape, dt=F32):
        return nc.dram_tensor(name, list(shape), dt, kind="ExternalInput").ap()

    def dscr(name, shape, dt=F32):
        kind = "ExternalOutput" if dbg else "Internal"
        return Tile(nc.dram_tensor(name, list(shape), dt, kind=kind).ap())

    x_in = din("x", [NSEQ, T, D])
    y_out = Tile(nc.dram_tensor("y", [NSEQ, T, D], F32, kind="ExternalOutput").ap())
    P = {}
    for name, shape in [("ln_in_g", [D]), ("ln_in_b", [D]), ("w_in", [2, D, P_COLS]), ("shift_mu", [2, 2, A_COLS]),
                        ("w0", [2, 2, AW]), ("decay_up", [2, 2, 32, AW]), ("a0", [2, 2, AW]), ("iclr_up", [2, 2, 32, AW]),
                        ("g_up", [2, 96, AW]), ("k_k", [2, AW]), ("k_a", [2, AW]), ("r_k", [2, AW]), ("gn_g", [2, AW]),
                        ("gn_b", [2, AW]), ("vres_down", [1, AW, 32]), ("vres_up", [1, 32, AW]), ("vres_0", [1, AW]),
                        ("lam", [2, 4, 64]), ("subln_g", [2, 128]), ("rel_bias", [32, 4]), ("w_out", [2, D, D]),
                        ("ln1_g", [2, D]), ("ln1_b", [2, D]), ("ffn_up", [2, D, 2 * D_FF]), ("conv_w", [2, 3, 2 * D_FF]),
                        ("conv_b", [2, 2 * D_FF]), ("ffn_down", [2, D_FF, D]), ("ln2_g", [2, D]), ("ln2_b", [2, D]),
                        ("c_ident", [128, 128]), ("c_masks", [128, 4, 128]), ("c_blk", [128, 128]),
                        ("c_onehot", [32, NB_LEN + 1])]:
        P[name] = din(name, shape)

    xres = dscr("xres", [T, D])
    pa = dscr("pa", [1792, T])
    qk = dscr("qk", [1024, T], BF16)
    vb = dscr("vb", [T, 512], BF16)
    mix = dscr("mix", [1024, T], BF16)
    x1s = dscr("x1s", [T, D])
    yfs = dscr("yfs", [512, T])
    vfs = dscr("vfs", [512, T])
    gsc = dscr("gsc", [4, NB_LEN + 1])
    hsc = dscr("hsc", [4, 128, 1536])
    upbf = dscr("upbf", [2, 44, 128, 8 * 128], BF16)

    G = Ctx(nc, S)
    ident = G.sb([128, 128], F32, "ident")
    identb = G.sb([128, 128], BF16, "identb")
    masks = G.sb([128, 4, 128], F32, "masks")
    blk = G.sb([128, 128], BF16, "blk")
    epsln = G.sb([128, 1], F32, "epsln")
    S.dma(ident[:, :], V(P["c_ident"], None))
    S.dma(identb[:, :], V(P["c_ident"], None), q="pool")
    S.dma(masks[:, :, :], V(P["c_masks"], None))
    S.dma(blk[:, :], V(P["c_blk"], None), q="pool")
    S.memset(epsln[:, :], LN_EPS)

    def layer_norm_rows(C, xt, g_bc, b_bc, out):
        st = C["st"].next()
        for c in range(2):
            S.op("dve", lambda e, c=c: e.bn_stats(out=st.t[:, c * 6:(c + 1) * 6], in_=xt.ap[:, c * 512:(c + 1) * 512]),
                 [xt.d], [st.d])
        S.op("dve", lambda e: e.bn_aggr(out=st.t[:, 12:14], in_=st.t[:, 0:12]), [st.d], [st.d])
        S.act(st[:, 14:15], st[:, 13:14], AF.Sqrt, bias=epsln[:, 0:1], scale=1.0)
        S.op("dve", lambda e: e.reciprocal(out=st.t[:, 15:16], in_=st.t[:, 14:15]), [st.d], [st.d])
        S.ts(out, xt, st[:, 12:13], st[:, 15:16], ALU.subtract, ALU.mult)
        S.tt(out, out, g_bc, ALU.mult)
        S.tt(out, out, b_bc, ALU.add)

    def load_w_bf16(dst, src_ap, KT, ncols, col0=0):
        v = src_ap.rearrange("(k p) c -> p k c", p=128)
        for k in range(KT):
            S.dma(dst[:, k, :], V(v[:, k, col0:col0 + ncols], None), q="pool")

    def phase_proj(l, s):
        C = Ctx(nc, S)
        w = C.sb([128, 8, P_COLS], BF16, "win")
        load_w_bf16(w, P["w_in"][l], 8, P_COLS)
        g_bc = C.sb([128, D], F32, "gbc")
        b_bc = C.sb([128, D], F32, "bbc")
        if l == 0:
            S.dma(g_bc[:, :], V(bcast_rows(P["ln_in_g"], D), None))
            S.dma(b_bc[:, :], V(bcast_rows(P["ln_in_b"], D), None))
        CC = {"st": Ring([C.sb([128, 16], F32, "st") for _ in range(4)])}
        xtk = Ring([C.sb([128, D], F32, "xtk") for _ in range(6)])
        xT = Ring([C.sb([128, 8, 512], BF16, "xT") for _ in range(2)])
        pst = Ring([C.ps([128, 512], F32, "pst") for _ in range(3)])
        psm = Ring([C.ps([128, 512], F32, "psm") for _ in range(4)])
        stA = Ring([C.sb([128, 512], F32, "stA") for _ in range(4)])
        stB = Ring([C.sb([128, 512], BF16, "stB") for _ in range(4)])
        ev = 0
        for t5 in range(NT5):
            t0 = t5 * 512
            xt_t = xT.next()
            blocks = []
            for j in range(4):
                xb = xtk.next()
                r0 = t0 + j * 128
                if l == 0:
                    S.dma(xb[:, :], V(x_in[s, r0:r0 + 128, :], None))
                    layer_norm_rows(CC, xb[:, :], g_bc[:, :], b_bc[:, :], xb[:, :])
                    S.dma(xres[r0:r0 + 128, :], xb[:, :], q="sp")
                else:
                    S.dma(xb[:, :], xres[r0:r0 + 128, :])
                blocks.append(xb)
            for k in range(8):
                p = pst.next()
                for j in range(4):
                    S.tr(p[:, j * 128:(j + 1) * 128], blocks[j][:, k * 128:(k + 1) * 128], ident[:, :])
                if k % 2 == 0:
                    S.copy(xt_t[:, k, :], p[:, :], eng="act")
                else:
                    S.copy(xt_t[:, k, :], p[:, :], eng="dve")
            for c in range(22):
                if c < 14:
                    c0 = c * 128
                    m = min(128, A_COLS - c0)
                else:
                    c0 = A_COLS + (c - 14) * 128
                    m = 128
                p = psm.next()
                for k in range(8):
                    S.mm(p[0:m, :], w[:, k, c0:c0 + m], xt_t[:, k, :], start=(k == 0), stop=(k == 7))
                if c < 14:
                    o = stA.next()
                    S.copy(o[0:m, :], p[0:m, :], eng=("act" if ev % 2 == 0 else "dve"))
                    S.dma(pa[c0:c0 + m, t0:t0 + 512], o[0:m, :])
                else:
                    o = stB.next()
                    S.copy(o[:, :], p[:, :], eng=("act" if ev % 2 == 0 else "dve"))
                    S.dma(qk[(c - 14) * 128:(c - 13) * 128, t0:t0 + 512], o[:, :])
                ev += 1
            for j in range(4):
                p = psm.next()
                for k in range(8):
                    S.mm(p[:, :], xt_t[:, k, j * 128:(j + 1) * 128], w[:, k, A_COLS + 1024:A_COLS + 1536],
                         start=(k == 0), stop=(k == 7))
                o = stB.next()
                S.copy(o[:, :], p[:, :], eng=("act" if ev % 2 == 0 else "dve"))
                ev += 1
                S.dma(vb[t0 + j * 128:t0 + (j + 1) * 128, :], o[:, :])
        C.close()

    def load_cols(C, dst, src1d, n, pst, stg):
        S.dma(stg[0:n, :], V(src1d.rearrange("(c p) -> c p", p=128), None))
        S.tr(pst[:, 0:n], stg[0:n, :], ident[0:n, 0:n])
        S.copy(dst, pst[:, 0:n])

    def phase_setup(l_, s_):
        C = Ctx(nc, S)
        for l in range(DEPTH):
            v = P["ffn_up"][l].rearrange("(k p) c -> p k c", p=128)
            for c in range(44):
                stg = None
                S.dma(V(upbf.t[l, c].rearrange("p (k c) -> p k c", k=8), upbf.d), V(v[:, :, c * 128:(c + 1) * 128], None),
                      q="pool")
        rb = C.sb([32, 4], F32, "rb")
        oh = C.sb([32, NB_LEN + 1], F32, "oh")
        gsb = C.sb([4, NB_LEN + 1], F32, "gsb")
        S.dma(rb[:, :], V(P["rel_bias"], None))
        S.dma(oh[:, :], V(P["c_onehot"], None))
        pp = C.ps([128, 512], F32, "pp")
        for c0 in (0, 512, 1024):
            n = min(512, NB_LEN + 1 - c0)
            S.mm(pp[0:4, 0:n], rb[:, :], oh[:, c0:c0 + n])
            S.copy(gsb[:, c0:c0 + n], pp[0:4, 0:n])
        S.dma(gsc[:, :], gsb[:, :])
        for h in range(4):
            dst = bass.AP(tensor=hsc.t.tensor, offset=hsc.t.offset + h * 128 * 1536, ap=[[1537, 128], [1, NB_LEN + 1]])
            src = bass.AP(tensor=gsc.t.tensor, offset=gsc.t.offset + h * (NB_LEN + 1), ap=[[0, 128], [1, NB_LEN + 1]])
            S.dma(V(dst, hsc.d), V(src, gsc.d))
        C.close()

    def phase_attn(l, s):
        C = Ctx(nc, S)
        lam_init = 0.8 - 0.6 * math.exp(-0.3 * l)
        lt = C.sb([1, 4, 64], F32, "lt")
        l2 = C.sb([1, 8], F32, "l2")
        ones1 = C.sb([1, 128], F32, "ones1")
        nlam = C.sb([128, 1], F32, "nlam")
        subg = C.sb([128, 128], F32, "subg")
        S.dma(lt[:, :, :], V(P["lam"][l:l + 1], None))
        S.memset(ones1[:, :], 1.0)
        S.tt(lt[:, 0, :], lt[:, 0, :], lt[:, 1, :], ALU.mult)
        S.tt(lt[:, 2, :], lt[:, 2, :], lt[:, 3, :], ALU.mult)
        S.op("dve", lambda e: e.reduce_sum(out=l2.t[:, 0:1], in_=lt.t[:, 0, :], axis=mybir.AxisListType.X), [lt.d], [l2.d])
        S.op("dve", lambda e: e.reduce_sum(out=l2.t[:, 1:2], in_=lt.t[:, 2, :], axis=mybir.AxisListType.X), [lt.d], [l2.d])
        S.act(l2[:, 2:4], l2[:, 0:2], AF.Exp)
        S.tt(l2[:, 4:5], l2[:, 3:4], l2[:, 2:3], ALU.subtract)
        S.ts(l2[:, 5:6], l2[:, 4:5], -lam_init, None, ALU.add)
        accA = [C.ps([128, 512], F32, "accA") for _ in range(2)]
        accB = [C.ps([128, 512], F32, "accB") for _ in range(2)]
        pl = accB[0]
        S.mm(pl[:, 0:1], ones1[:, :], l2[:, 5:6])
        S.copy(nlam[:, :], pl[:, 0:1])
        S.dma(subg[:, :], V(bcast_rows(P["subln_g"][l], 128), None))
        S.ts(subg[:, :], subg[:, :], 1.0 - lam_init, None, ALU.mult)

        KT = Ring([C.sb([128, T], BF16, "KT") for _ in range(2)])
        QT = Ring([[C.sb([128, T], BF16, "QT") for _ in range(2)] for _ in range(2)])
        VA = Ring([C.sb([128, NCH, 130], BF16, "VA") for _ in range(2)])
        BT = Ring([C.sb([128, 6, 512], F32, "BT") for _ in range(2)])
        CB = Ring([C.sb([128, 2], F32, "CB") for _ in range(2)])
        stR = Ring([C.ps([128, 512], F32, "st") for _ in range(3)])
        ptr = C.ps([128, 512], BF16, "ptr")
        ER = Ring([C.sb([128, 512], BF16, "E") for _ in range(4)])
        TMP = Ring([C.sb([128, 512], F32, "tmp") for _ in range(3)])
        ON = [C.sb([128, 4, 128], F32, "on") for _ in range(2)]
        SM = Ring([C.sb([128, 8], F32, "sm") for _ in range(8)])
        OB = Ring([C.sb([128, 128], BF16, "ob") for _ in range(3)])
        OF = Ring([C.sb([128, 128], F32, "of") for _ in range(3)])
        OST = Ring([C.sb([128, 512], BF16, "ost") for _ in range(2)])
        for h in range(4):
            kt_ = KT.next(); qt_ = QT.next(); va = VA.next(); bt = BT.next(); cb = CB.next()
            S.dma(kt_[:, :], qk[512 + h * 128:512 + (h + 1) * 128, :])
            S.dma(qt_[0][0:64, :], qk[h * 128:h * 128 + 64, :])
            S.memset(qt_[0][64:128, :], 0.0, eng="pool")
            S.dma(qt_[1][64:128, :], qk[h * 128 + 64:(h + 1) * 128, :])
            S.memset(qt_[1][0:64, :], 0.0, eng="pool")
            S.dma(va[:, :, 0:128], V(vb.t[:, h * 128:(h + 1) * 128].rearrange("(n p) e -> p n e", p=128), vb.d))
            S.memset(va[:, :, 128:130], 1.0, eng="pool")
            for rho in range(6):
                c_ = 639 - (rho - 1) * 128
                S.dma(bt[:, rho, :], V(hsc.t[h, :, c_:c_ + 512], hsc.d))
            S.dma(cb[:, 0:1], V(bass.AP(tensor=gsc.t.tensor, offset=gsc.t.offset + h * (NB_LEN + 1) + 1278, ap=[[0, 128], [1, 1]]), gsc.d))
            S.dma(cb[:, 1:2], V(bass.AP(tensor=gsc.t.tensor, offset=gsc.t.offset + h * (NB_LEN + 1), ap=[[0, 128], [1, 1]]), gsc.d))
            def acc_of(c, qb):
                return (accA[c], qb * 130) if qb < 3 else (accB[c], 0)

            tiles = [(q5, c, kt) for q5 in range(NT5) for c in range(2) for kt in range(NCH)]
            LOOK = 2
            Es = {}
            for idx in range(len(tiles) + LOOK):
                if idx < len(tiles):
                    q5, c, kt = tiles[idx]
                    pb = c * 64
                    st = stR.next()
                    S.mm(st[:, :], kt_[:, kt * 128:(kt + 1) * 128], qt_[c][:, q5 * 512:(q5 + 1) * 512])
                    rho = kt - 4 * q5 + 1
                    E = ER.next()
                    if 0 <= rho <= 5:
                        tmp = TMP.next()
                        S.stt(tmp[:, :], st[:, :], 0.125, bt[:, rho, :], ALU.mult, ALU.add)
                        S.act(E[:, :], tmp[:, :], AF.Exp)
                    else:
                        S.act(E[:, :], st[:, :], AF.Exp, bias=(cb[:, 0:1] if rho < 0 else cb[:, 1:2]), scale=0.125)
                    Es[idx] = E
                j = idx - LOOK
                if j < 0:
                    continue
                q5, c, kt = tiles[j]
                E = Es.pop(j)
                for qb in range(4):
                    a, o = acc_of(c, qb)
                    S.mm(a[:, o:o + 130], E[:, qb * 128:(qb + 1) * 128], va[:, kt, 0:130],
                         start=(kt == 0 and qb in (0, 3)), stop=(kt == NCH - 1))
                if kt != NCH - 1:
                    continue
                for qb in range(4):
                    a, o = acc_of(c, qb)
                    sm = SM.next()
                    S.op("dve", lambda e, sm=sm, a=a, o=o: e.reciprocal(out=sm.t[:, 0:1], in_=a.t[:, o + 128:o + 129]),
                         [a.d], [sm.d])
                    S.ts(ON[c][:, qb, :], a[:, o:o + 128], sm[:, 0:1], None, ALU.mult)
                if c == 0:
                    continue
                ost = OST.next()
                for qb in range(4):
                    of = OF.next(); ob = OB.next(); sm = SM.next()
                    S.stt(of[:, :], ON[1][:, qb, :], nlam[:, 0:1], ON[0][:, qb, :], ALU.mult, ALU.add)
                    S.act(ob[:, :], of[:, :], AF.Square, accum=sm[:, 0:1])
                    S.act(sm[:, 1:2], sm[:, 0:1], AF.Sqrt, bias=epsln[:, 0:1], scale=1.0 / 128.0)
                    S.op("dve", lambda e, sm=sm: e.reciprocal(out=sm.t[:, 2:3], in_=sm.t[:, 1:2]), [sm.d], [sm.d])
                    S.stt(ob[:, :], of[:, :], sm[:, 2:3], subg[:, :], ALU.mult, ALU.mult)
                    S.tr(ptr[:, qb * 128:(qb + 1) * 128], ob[:, :], identb[:, :])
                S.copy(ost[:, :], ptr[:, :])
                S.dma(mix[512 + h * 128:512 + (h + 1) * 128, q5 * 512:(q5 + 1) * 512], ost[:, :])
        C.close()

    def phase_wout(l, s):
        C = Ctx(nc, S)
        w = C.sb([128, 8, D], BF16, "wout")
        load_w_bf16(w, P["w_out"][l], 8, D)
        g_bc = C.sb([128, D], F32, "gbc"); b_bc = C.sb([128, D], F32, "bbc")
        S.dma(g_bc[:, :], V(bcast_rows(P["ln1_g"][l], D), None))
        S.dma(b_bc[:, :], V(bcast_rows(P["ln1_b"][l], D), None))
        CC = {"st": Ring([C.sb([128, 16], F32, "st") for _ in range(4)])}
        MT = Ring([C.sb([128, 8, 512], BF16, "mixT") for _ in range(2)])
        XB = Ring([C.sb([128, D], F32, "xb") for _ in range(4)])
        HB = Ring([C.sb([128, D], F32, "hb") for _ in range(4)])
        PS = Ring([C.ps([128, 1024], F32, "pso") for _ in range(3)])
        for t5 in range(NT5):
            t0 = t5 * 512
            mt = MT.next()
            S.dma(mt[:, :, :], V(mix.t[:, t0:t0 + 512].rearrange("(k p) t -> p k t", p=128), mix.d))
            for j in range(4):
                r0 = t0 + j * 128
                xb = XB.next(); hb = HB.next(); p = PS.next()
                S.dma(xb[:, :], xres[r0:r0 + 128, :])
                for half in range(2):
                    for k in range(8):
                        S.mm(p[:, half * 512:(half + 1) * 512], mt[:, k, j * 128:(j + 1) * 128],
                             w[:, k, half * 512:(half + 1) * 512], start=(k == 0), stop=(k == 7))
                S.stt(hb[:, :], xb[:, :], ALPHA, p[:, :], ALU.mult, ALU.add)
                layer_norm_rows(CC, hb[:, :], g_bc[:, :], b_bc[:, :], hb[:, :])
                S.dma(x1s[r0:r0 + 128, :], hb[:, :])
        C.close()

    def phase_ffn(l, s):
        C = Ctx(nc, S)
        wd = C.sb([128, 22, D], BF16, "wdown")
        load_w_bf16(wd, P["ffn_down"][l], 22, D)
        g_bc = C.sb([128, D], F32, "gbc"); b_bc = C.sb([128, D], F32, "bbc")
        S.dma(g_bc[:, :], V(bcast_rows(P["ln2_g"][l], D), None))
        S.dma(b_bc[:, :], V(bcast_rows(P["ln2_b"][l], D), None))
        cw = C.sb([128, 4, 44], F32, "cw")
        stg = C.sb([64, 128], F32, "stg")
        pst = Ring([C.ps([128, 512], F32, "pst") for _ in range(2)])
        for j in range(3):
            load_cols(C, cw[:, j, :], P["conv_w"][l, j], 44, pst.next(), stg)
        load_cols(C, cw[:, 3, :], P["conv_b"][l], 44, pst.next(), stg)
        CC = {"st": Ring([C.sb([128, 16], F32, "st") for _ in range(4)])}
        XB = Ring([C.sb([128, D], F32, "xb") for _ in range(8)])
        XH = Ring([C.sb([2, D], F32, "xh") for _ in range(2)])
        XT = Ring([C.sb([128, 8, 512], BF16, "xT") for _ in range(2)])
        XHT = Ring([C.sb([128, 8, 2], BF16, "xhT") for _ in range(2)])
        WC = Ring([C.sb([128, 8 * 128], BF16, "wc") for _ in range(4)])
        PH = Ring([C.ps([128, 512], F32, "ph") for _ in range(2)])
        PHHS = Ring([(C.ps([128, 512], F32, "phh"), 0) for i in range(2)])
        PD = Ring([C.ps([128, 1024], F32, "pd") for _ in range(1)])
        HR = Ring([C.sb([128, 514], F32, "hr") for _ in range(6)])
        OO = Ring([C.sb([128, 512], F32, "oo") for _ in range(4)])
        GA = C.sb([128, 22, 512], BF16, "ga")
        GT = C.sb([128, 22, 512], BF16, "gt")
        HB = Ring([C.sb([128, D], F32, "hb") for _ in range(3)])
        def prep_tile(t5):
            t0 = t5 * 512
            xt = XT.next(); xh = XH.next(); xht = XHT.next()
            blocks = []
            for j in range(4):
                xb = XB.next()
                S.dma(xb[:, :], x1s[t0 + j * 128:t0 + (j + 1) * 128, :])
                blocks.append(xb)
            S.memset(xh[:, :], 0.0, eng="pool")
            if t0 > 0:
                S.dma(xh[0:1, :], x1s[t0 - 1:t0, :])
            if t0 + 512 < T:
                S.dma(xh[1:2, :], x1s[t0 + 512:t0 + 513, :])
            for k in range(8):
                p = pst.next()
                for j in range(4):
                    S.tr(p[:, j * 128:(j + 1) * 128], blocks[j][:, k * 128:(k + 1) * 128], ident[:, :])
                S.copy(xt[:, k, :], p[:, :], eng=("act" if k % 2 == 0 else "dve"))
            p = pst.next()
            for k in range(8):
                S.tr(p[:, 2 * k:2 * k + 2], xh[0:2, k * 128:(k + 1) * 128], ident[0:2, 0:2])
            S.copy(xht[:, :, :], V(p.t[:, 0:16].rearrange("p (k t) -> p k t", t=2), p.d))
            return xt, xht, blocks

        nxt = prep_tile(0)
        for t5 in range(NT5):
            t0 = t5 * 512
            xt, xht, blocks = nxt
            def part_a(c):
                wc = WC.next()
                S.dma(wc[:, :], upbf[l, c])
                ph = PH.next()
                for k in range(8):
                    S.mm(ph[:, :], wc[:, k * 128:(k + 1) * 128], xt[:, k, :], start=(k == 0), stop=(k == 7))
                PHH, ho = PHHS.next()
                for k in range(8):
                    S.mm(PHH[:, ho:ho + 2], wc[:, k * 128:(k + 1) * 128], xht[:, k, :], start=(k == 0), stop=(k == 7))
                hr = HR.next()
                S.copy(hr[:, 1:513], ph[:, :], eng="act")
                S.copy(hr[:, 0:1], PHH[:, ho:ho + 1], eng="act")
                S.copy(hr[:, 513:514], PHH[:, ho + 1:ho + 2], eng="act")
                return hr

            def part_b1(c, hr):
                oo = OO.next()
                if c < 22:
                    S.ts(oo[:, :], hr[:, 1:513], cw[:, 1, c:c + 1], cw[:, 3, c:c + 1], ALU.mult, ALU.add, eng="pool")
                else:
                    S.act(oo[:, :], hr[:, 1:513], AF.Identity, bias=cw[:, 3, c:c + 1], scale=cw[:, 1, c:c + 1])
                S.stt(oo[:, :], hr[:, 0:512], cw[:, 0, c:c + 1], oo[:, :], ALU.mult, ALU.add)
                return oo

            def part_b2(c, hr, oo):
                S.stt(oo[:, :], hr[:, 2:514], cw[:, 2, c:c + 1], oo[:, :], ALU.mult, ALU.add)
                if c < 22:
                    S.act(GA[:, c, :], oo[:, :], AF.Gelu)
                else:
                    S.tt(GT[:, c - 22, :], oo[:, :], GA[:, c - 22, :], ALU.mult, eng="pool")

            hrs = {}
            oos = {}
            for c in range(47):
                if c < 44:
                    hrs[c] = part_a(c)
                if 2 <= c < 46:
                    oos[c - 2] = part_b1(c - 2, hrs[c - 2])
                if c >= 3:
                    part_b2(c - 3, hrs.pop(c - 3), oos.pop(c - 3))
            if t5 + 1 < NT5:
                nxt = prep_tile(t5 + 1)
            for j in range(4):
                r0 = t0 + j * 128
                p = PD.next(); hb = HB.next()
                for half in range(2):
                    for k in range(22):
                        S.mm(p[:, half * 512:(half + 1) * 512], GT[:, k, j * 128:(j + 1) * 128],
                             wd[:, k, half * 512:(half + 1) * 512], start=(k == 0), stop=(k == 21))
                S.stt(hb[:, :], blocks[j][:, :], ALPHA, p[:, :], ALU.mult, ALU.add)
                layer_norm_rows(CC, hb[:, :], g_bc[:, :], b_bc[:, :], hb[:, :])
                if l == DEPTH - 1:
                    S.dma(V(y_out.t[s, r0:r0 + 128, :], y_out.d), hb[:, :])
                else:
                    S.dma(xres[r0:r0 + 128, :], hb[:, :])
        C.close()

    def RV(v, pattern, **kw):
        return V(v.ap.rearrange(pattern, **kw), v.d)

    def phase_rwkv(l, s):
        C = Ctx(nc, S)
        RA = Ring([C.ps([128, 512], F32, "ra") for _ in range(6)])
        pstc = RA.tiles[0]
        stg = C.sb([64, 128], F32, "stg")
        mu = C.sb([128, 3, 14], F32, "mu")
        for j in range(2):
            S.memset(stg[0:14, :], 0.0)
            S.dma(stg[0:13, :], V(P["shift_mu"][l, j, 0:1664].rearrange("(c p) -> c p", p=128), None))
            S.dma(stg[13:14, 0:96], V(P["shift_mu"][l, j, 1664:1760].rearrange("(c p) -> c p", p=96), None))
            S.tr(pstc[:, 0:14], stg[0:14, :], ident[0:14, 0:14])
            S.copy(mu[:, 1 + j, :], pstc[:, 0:14])
        S.tt(mu[:, 0, :], mu[:, 1, :], mu[:, 2, :], ALU.add)
        S.ts(mu[:, 0, :], mu[:, 0, :], -1.0, 1.0, ALU.mult, ALU.add)
        PRM = C.sb([128, 12, 4], F32, "prm")
        srcs = [P["w0"][l, 0], P["w0"][l, 1], P["a0"][l, 0], P["a0"][l, 1], P["k_k"][l], P["k_a"][l], P["r_k"][l],
                P["gn_g"][l], P["gn_b"][l], P["vres_0"][0]]
        for i, sap in enumerate(srcs):
            load_cols(C, PRM[:, i, :], sap, 4, pstc, stg)
        S.ts(PRM[:, 10, :], PRM[:, 5, :], -1.0, 1.0, ALU.mult, ALU.add)
        S.ts(PRM[:, 11, :], PRM[:, 5, :], -2.0, 2.0, ALU.mult, ALU.add)
        dup = C.sb([64, AW], BF16, "dup")
        iup = [C.sb([128, AW], BF16, "iup") for _ in range(2)]
        gup = C.sb([96, AW], BF16, "gup")
        vdn = C.sb([128, 4, 32], BF16, "vdn")
        vup = C.sb([32, AW], BF16, "vup")
        S.dma(dup[:, :], V(P["decay_up"][l].rearrange("d r c -> (d r) c"), None), q="pool")
        for dd in range(2):
            S.memset(iup[dd][64:128, :], 0.0)
            S.dma(iup[dd][64 + 32 * dd:96 + 32 * dd, :], V(P["iclr_up"][l, dd], None), q="pool")
        S.dma(gup[:, :], V(P["g_up"][l], None), q="pool")
        S.dma(vdn[:, :, :], V(P["vres_down"][0].rearrange("(f p) r -> p f r", p=128), None), q="pool")
        S.dma(vup[:, :], V(P["vres_up"][0], None), q="pool")
        blkm = C.sb([128, 128], BF16, "blkm")
        S.ts(blkm[:, :], blk[:, :], 1.0 / 64.0, None, ALU.mult)
        idblk = C.sb([128, 64], F32, "idblk")
        S.tt(idblk[:, :], ident[:, 0:64], ident[:, 64:128], ALU.add)
        ones = C.sb([128, 128], F32, "ones")
        S.memset(ones[:, :], 1.0)
        eps12 = C.sb([128, 1], F32, "eps12")
        S.memset(eps12[:, :], 1e-12)
        epsgn = C.sb([128, 1], F32, "epsgn")
        S.memset(epsgn[:, :], GN_EPS)
        MS2 = C.sb([128, 2, 128], F32, "ms2")
        MI2 = C.sb([128, 2, 128], F32, "mi2")
        MT2 = C.sb([128, 2, 128], F32, "mt2")

        _stage(1)
        XIN = Ring([C.sb([128, 14, 130], F32, "xin") for _ in range(2)])
        SH = [[C.sb([128, 128], F32, "sh") for _ in range(14)] for _ in range(2)]
        PTBR = Ring([C.ps([128, 1024], BF16, "ptb") for _ in range(2)])
        TWt = C.sb([64, 128], BF16, "tw")
        ADt = C.sb([128, 128], BF16, "ad")
        SGt = C.sb([96, 128], BF16, "sg")
        VD = C.sb([32, 128], BF16, "vd")

        def fset(n, dt, shape=(128, 128), name="f"):
            return [[C.sb(list(shape), dt, name) for _ in range(4)] for _ in range(n)]
        (Ee, Cu, La, Gm, Gi, Gp, Ge, K0, Kk, Bb, Kd, T1, Yy, Dl, Yn, T2) = fset(16, F32)
        A0 = [fset(1, F32)[0] for _ in range(2)]
        A1 = [fset(1, F32)[0] for _ in range(2)]
        Vv = [fset(1, F32)[0] for _ in range(2)]
        SMs = [[C.sb([128, 8], F32, "sms") for _ in range(4)] for _ in range(2)]
        (SQ, BH, KH, VB16, RH, YB, RKV, OA) = fset(8, BF16)
        AT = [fset(1, BF16)[0] for _ in range(2)]
        VT = [fset(1, BF16)[0] for _ in range(2)]
        AR = [[C.sb([128, 256], BF16, "ar") for _ in range(4)] for _ in range(2)]
        BK = [[C.sb([128, 256], BF16, "bk") for _ in range(4)] for _ in range(2)]
        DI = [[C.sb([128, 2, 192], BF16, "di") for _ in range(4)] for _ in range(2)]
        DX = [[C.sb([128, 2, 192], BF16, "dx") for _ in range(4)] for _ in range(2)]
        GQa = [[C.sb([128, 2, 192], BF16, "gqa") for _ in range(4)] for _ in range(2)]
        GQ = [C.sb([128, 2, 192], BF16, "gq") for _ in range(4)]
        MTAK = [C.sb([128, 2, 128], BF16, "mtak") for _ in range(4)]
        PW = [[C.sb([128, 2, 2, 128], BF16, "pw") for _ in range(4)] for _ in range(3)]
        PM = [C.sb([128, 64], BF16, "pm") for _ in range(4)]
        ST = [[C.sb([128, 64], BF16, "st") for _ in range(4)] for _ in range(2)]
        VF = [C.sb([128, 128], F32, "vf") for _ in range(4)]
        YF = [C.sb([128, 128], F32, "yf") for _ in range(4)]

        def rr(gens):
            gens = list(gens)
            while gens:
                for g in list(gens):
                    try:
                        next(g)
                    except StopIteration:
                        gens.remove(g)
                yield

        def prep(d, ci, par):
            c0 = ci * 128
            xin = XIN.next()
            sh = SH[par]
            lo = max(c0 - 1, 0)
            hi = min(c0 + 129, T)
            if c0 == 0:
                S.memset(xin[:, :, 0:1], 0.0, eng="pool")
            if c0 + 129 > T:
                S.memset(xin[:, :, 129:130], 0.0, eng="pool")
            o0 = lo - (c0 - 1)
            S.dma(xin[:, :, o0:o0 + (hi - lo)], V(pa.t[:, lo:hi].rearrange("(r p) t -> p r t", p=128), pa.d))
            for rt in range(14):
                S.act(sh[rt][:, :], xin[:, rt, 1:129], AF.Copy, scale=mu[:, 0, rt:rt + 1])
                if rt % 4 == 3:
                    yield
            for rt in range(14):
                S.stt(sh[rt][:, :], xin[:, rt, 0:128], mu[:, 1, rt:rt + 1], sh[rt][:, :], ALU.mult, ALU.add)
                if rt % 4 == 3:
                    yield
            for rt in range(14):
                S.stt(sh[rt][:, :], xin[:, rt, 2:130], mu[:, 2, rt:rt + 1], sh[rt][:, :], ALU.mult, ALU.add)
                if rt % 4 == 3:
                    yield
            r_ = sh[0:4]; k_ = sh[4:8]; v_ = sh[8:12]
            S.act(TWt[32 * d:32 * d + 32, :], sh[12][32 * d:32 * d + 32, :], AF.Tanh)
            S.copy(ADt[64:128, :], sh[12][64:128, :], eng="pool")
            def decay_chain(ft):
                fs = slice(ft * 128, (ft + 1) * 128)
                pz = RA.next()
                S.mm(pz[:, 0:128], dup[32 * d:32 * d + 32, fs], TWt[32 * d:32 * d + 32, :])
                S.act(Ee[ft][:, :], pz[:, 0:128], AF.Sigmoid, bias=PRM[:, d, ft:ft + 1], scale=1.0)
                yield
                for dd in ((0,) if d == 0 else (0, 1)):
                    pq = RA.next()
                    S.mm(pq[:, 0:128], iup[dd][64:128, fs], ADt[64:128, :])
                    S.act((A0 if dd == 0 else A1)[par][ft][:, :], pq[:, 0:128], AF.Sigmoid, bias=PRM[:, 2 + dd, ft:ft + 1], scale=1.0)
                S.op("dve", lambda e, ft=ft: e.tensor_tensor_scan(out=Cu[ft].t[:, :], data0=ones.t[:, :], data1=Ee[ft].t[:, :],
                                                                   initial=0.0, op0=ALU.mult, op1=ALU.add),
                     [ones.d, Ee[ft].d], [Cu[ft].d])
                yield
                sm = SMs[par][ft]
                if d == 0:
                    lam_t = Cu[ft]
                else:
                    S.stt(La[ft][:, :], Cu[ft][:, :], -1.0, Ee[ft][:, :], ALU.mult, ALU.add)
                    yield
                    S.ts(La[ft][:, :], La[ft][:, :], Cu[ft][:, 127:128], None, ALU.add)
                    lam_t = La[ft]
                S.ts(sm[:, 0:1], Cu[ft][:, 127:128], -CEXP, None, ALU.mult)
                yield
                S.act(sm[:, 1:2], Cu[ft][:, 127:128], AF.Exp, scale=-CEXP)
                S.act(Gm[ft][:, :], lam_t[:, :], AF.Exp, scale=-CEXP)
                S.tt(Gp[ft][:, :], lam_t[:, :], Ee[ft][:, :], ALU.subtract, eng="pool")
                yield
                S.act(Gi[ft][:, :], lam_t[:, :], AF.Exp, scale=CEXP)
                S.act(Ge[ft][:, :], lam_t[:, :], AF.Exp, bias=sm[:, 0:1], scale=CEXP)
                yield
                S.act(Gp[ft][:, :], Gp[ft][:, :], AF.Exp, scale=-CEXP)
                yield

            yield from rr([decay_chain(ft) for ft in range(4)])
            for ft in range(4):
                S.copy(Vv[par][ft][:, :], v_[ft][:, :], eng="pool")
            if l == 1:
                pvd = RA.next()
                for ft in range(4):
                    S.copy(VB16[ft][:, :], Vv[par][ft][:, :], eng="pool")
                    S.mm(pvd[0:32, 0:128], vdn[:, ft, :], VB16[ft][:, :], start=(ft == 0), stop=(ft == 3))
                S.copy(VD[:, :], pvd[0:32, 0:128], eng="act")
                yield
                def vres_chain(ft):
                    fs = slice(ft * 128, (ft + 1) * 128)
                    S.dma(VF[ft][:, :], vfs[ft * 128:(ft + 1) * 128, c0:c0 + 128])
                    pg = RA.next()
                    S.mm(pg[:, 0:128], vup[:, fs], VD[:, :])
                    S.act(T1[ft][:, :], pg[:, 0:128], AF.Sigmoid, bias=PRM[:, 9, ft:ft + 1], scale=1.0)
                    S.tt(Dl[ft][:, :], VF[ft][:, :], Vv[par][ft][:, :], ALU.subtract, eng="pool")
                    yield
                    S.tt(Dl[ft][:, :], Dl[ft][:, :], T1[ft][:, :], ALU.mult, eng="pool")
                    yield
                    S.tt(Vv[par][ft][:, :], Vv[par][ft][:, :], Dl[ft][:, :], ALU.add, eng="pool")
                    yield

                yield from rr([vres_chain(ft) for ft in range(4)])
            elif d == 0:
                for ft in range(4):
                    S.dma(vfs[ft * 128:(ft + 1) * 128, c0:c0 + 128], Vv[par][ft][:, :])
            def kk_chain(ft):
                S.ts(K0[ft][:, :], k_[ft][:, :], PRM[:, 4, ft:ft + 1], None, ALU.mult)
                S.copy(VB16[ft][:, :], Vv[par][ft][:, :], eng="pool")
                yield
                S.act(SQ[ft][:, :], K0[ft][:, :], AF.Square)
                yield
                Ad = (A0 if d == 0 else A1)[par][ft]
                S.ts(T1[ft][:, :], Ad[:, :], PRM[:, 5, ft:ft + 1], PRM[:, 10, ft:ft + 1], ALU.mult, ALU.add)
                S.tt(AR[par][ft][:, 128:256], r_[ft][:, :], Gm[ft][:, :], ALU.mult, eng="pool")
                yield
                pss = RA.next()
                S.mm(pss[:, 0:128], blk[:, :], SQ[ft][:, :])
                S.act(Kk[ft][:, :], pss[:, 0:128], AF.Sqrt, bias=eps12[:, 0:1], scale=1.0)
                S.tt(Kd[ft][:, :], k_[ft][:, :], T1[ft][:, :], ALU.mult, eng="pool")
                yield
                S.op("dve", lambda e, ft=ft: e.reciprocal(out=Kk[ft].t[:, :], in_=Kk[ft].t[:, :]), [Kk[ft].d], [Kk[ft].d])
                S.tt(BK[par][ft][:, 128:256], Kd[ft][:, :], Gi[ft][:, :], ALU.mult, eng="pool")
                yield
                S.tt(Kk[ft][:, :], Kk[ft][:, :], K0[ft][:, :], ALU.mult)
                S.tt(KH[ft][:, :], Kd[ft][:, :], Ge[ft][:, :], ALU.mult, eng="pool")
                yield
                S.stt(AR[par][ft][:, 0:128], Kk[ft][:, :], -1.0, Gp[ft][:, :], ALU.mult, ALU.mult)
                S.tt(Bb[ft][:, :], Kk[ft][:, :], Ad[:, :], ALU.mult)
                yield
                S.tt(BK[par][ft][:, 0:128], Bb[ft][:, :], Gi[ft][:, :], ALU.mult, eng="pool")
                S.tt(BH[ft][:, :], Bb[ft][:, :], Ge[ft][:, :], ALU.mult)
                yield

            yield from rr([kk_chain(ft) for ft in range(4)])
            for ft in range(4):
                PTB = PTBR.next()
                S.tr(PTB[:, 0:128], AR[par][ft][:, 0:128], identb[:, :])
                S.tr(PTB[:, 128:256], BH[ft][:, :], identb[:, :])
                S.tr(PTB[:, 256:384], KH[ft][:, :], identb[:, :])
                S.tr(PTB[:, 384:512], VB16[ft][:, :], identb[:, :])
                S.copy(AT[par][ft][:, :], PTB[:, 0:128], eng="dve")
                for h in range(2):
                    S.copy(DI[par][ft][:, h, 128:192], PTB[:, 128 + h * 64:192 + h * 64], eng="dve")
                for h in range(2):
                    S.copy(GQa[par][ft][:, h, 128:192], PTB[:, 256 + h * 64:320 + h * 64], eng="dve")
                S.copy(VT[par][ft][:, :], PTB[:, 384:512], eng="dve")
                yield

        def mats(d, ci, par, sti):
            c0 = ci * 128
            sh = SH[par]
            r_ = sh[0:4]; k_ = sh[4:8]
            for ft in range(4):
                for h in range(2):
                    hb = h * 64
                    pA = RA.next()
                    S.mm(pA[:, 0:256], BK[par][ft][hb:hb + 64, 0:128], AR[par][ft][hb:hb + 64, :])
                    S.tt(PW[0][ft][:, h, 0, :], pA[:, 0:128], MS2[:, h, :], ALU.mult)
                    S.tt(DI[par][ft][:, h, 0:128], pA[:, 128:256], MI2[:, h, :], ALU.mult)
                    pB = RA.next()
                    S.mm(pB[:, 0:128], BK[par][ft][hb:hb + 64, 128:256], AR[par][ft][hb:hb + 64, 128:256])
                    S.tt(GQa[par][ft][:, h, 0:128], pB[:, 0:128], MI2[:, h, :], ALU.mult)
                    pC = RA.next()
                    S.mm(pC[:, 0:256], AR[par][ft][hb:hb + 64, 0:128], BK[par][ft][hb:hb + 64, :])
                    S.tt(PW[0][ft][:, h, 1, :], pC[:, 0:128], MT2[:, h, :], ALU.mult)
                    S.tt(MTAK[ft][:, h, :], pC[:, 128:256], MT2[:, h, :], ALU.mult)
                yield
            Dc = [DI[par][ft] for ft in range(4)]
            for j in range(7):
                pwc = PW[j % 3]
                pwn = PW[(j + 1) % 3]
                for ft in range(4):
                    if j < 6:
                        pP = RA.next()
                        for h in range(2):
                            S.mm(pP[:, h * 256:h * 256 + 128], pwc[ft][:, h, 1, :], pwc[ft][:, h, 0, :])
                            S.mm(pP[:, h * 256 + 128:h * 256 + 256], pwc[ft][:, h, 0, :], pwc[ft][:, h, 1, :])
                        S.copy(RV(pwn[ft][:, :, :, :], "p h t c -> p (h t c)"), pP[:, :], eng="act")
                    pD = RA.next()
                    for h in range(2):
                        S.mm(pD[:, h * 192:(h + 1) * 192], identb[:, :], Dc[ft][:, h, :], start=True, stop=False)
                        S.mm(pD[:, h * 192:(h + 1) * 192], pwc[ft][:, h, 1, :], Dc[ft][:, h, :], start=False, stop=True)
                    Dn = DX[j % 2][ft]
                    S.copy(RV(Dn[:, :, :], "p h c -> p (h c)"), pD[:, 0:384], eng=("dve" if ft % 2 == 0 else "act"))
                    Dc[ft] = Dn
                    yield
            stc = ST[sti % 2]
            stn = ST[(sti + 1) % 2]
            for ft in range(4):
                DF = Dc[ft]
                sm = SMs[par][ft]
                pR = RA.next()
                for h in range(2):
                    hb = h * 64
                    S.mm(pR[hb:hb + 64, 0:192], AT[par][ft][:, hb:hb + 64], DF[:, h, :])
                S.tt(RH[ft][:, :], pR[:, 0:128], AR[par][ft][:, 128:256], ALU.add)
                S.stt(PM[ft][:, :], idblk[:, :], sm[:, 1:2], pR[:, 128:192], ALU.mult, ALU.add)
                pG = RA.next()
                for h in range(2):
                    S.mm(pG[:, h * 192:(h + 1) * 192], MTAK[ft][:, h, :], DF[:, h, :])
                S.tt(RV(GQ[ft][:, :, :], "p h c -> p (h c)"), pG[:, 0:384], RV(GQa[par][ft][:, :, :], "p h c -> p (h c)"), ALU.add)
                yield
            for ft in range(4):
                if d == 1:
                    S.dma(YF[ft][:, :], yfs[ft * 128:(ft + 1) * 128, c0:c0 + 128])
                for h in range(2):
                    hb = h * 64
                    pY = RA.next()
                    pS = RA.next()
                    S.mm(pY[hb:hb + 64, 0:128], stc[ft][hb:hb + 64, :], RH[ft][hb:hb + 64, :], start=True, stop=False)
                    S.mm(pY[hb:hb + 64, 0:128], VT[par][ft][:, hb:hb + 64], GQ[ft][:, h, 0:128], start=False, stop=True)
                    S.mm(pS[hb:hb + 64, 0:64], PM[ft][hb:hb + 64, :], stc[ft][hb:hb + 64, :], start=True, stop=False)
                    S.mm(pS[hb:hb + 64, 0:64], GQ[ft][:, h, 128:192], VT[par][ft][:, hb:hb + 64], start=False, stop=True)
                    S.copy(stn[ft][hb:hb + 64, :], pS[hb:hb + 64, 0:64], eng="act")
                    if d == 0:
                        S.copy(Yy[ft][hb:hb + 64, :], pY[hb:hb + 64, 0:128], eng="dve")
                    else:
                        S.tt(Yy[ft][hb:hb + 64, :], pY[hb:hb + 64, 0:128], YF[ft][hb:hb + 64, :], ALU.add)
                yield
                if d == 0:
                    S.dma(yfs[ft * 128:(ft + 1) * 128, c0:c0 + 128], Yy[ft][:, :])
            if d == 0:
                return

            def epi_chain(ft):
                fs = slice(ft * 128, (ft + 1) * 128)
                S.copy(YB[ft][:, :], Yy[ft][:, :], eng="pool")
                S.tt(T2[ft][:, :], A0[par][ft][:, :], A1[par][ft][:, :], ALU.add, eng="pool")
                yield
                pm_ = RA.next()
                S.mm(pm_[:, 0:128], blkm[:, :], YB[ft][:, :])
                S.tt(Dl[ft][:, :], Yy[ft][:, :], pm_[:, 0:128], ALU.subtract)
                S.ts(T2[ft][:, :], T2[ft][:, :], PRM[:, 5, ft:ft + 1], PRM[:, 11, ft:ft + 1], ALU.mult, ALU.add)
                yield
                S.act(SQ[ft][:, :], Dl[ft][:, :], AF.Square)
                S.tt(T2[ft][:, :], T2[ft][:, :], k_[ft][:, :], ALU.mult, eng="pool")
                yield
                pv_ = RA.next()
                S.mm(pv_[:, 0:128], blkm[:, :], SQ[ft][:, :])
                S.act(Yn[ft][:, :], pv_[:, 0:128], AF.Sqrt, bias=epsgn[:, 0:1], scale=1.0)
                S.stt(RKV[ft][:, :], r_[ft][:, :], PRM[:, 6, ft:ft + 1], T2[ft][:, :], ALU.mult, ALU.mult)
                yield
                S.op("dve", lambda e, ft=ft: e.reciprocal(out=Yn[ft].t[:, :], in_=Yn[ft].t[:, :]), [Yn[ft].d], [Yn[ft].d])
                pb_ = RA.next()
                S.mm(pb_[:, 0:128], blk[:, :], RKV[ft][:, :])
                S.tt(T2[ft][:, :], pb_[:, 0:128], Vv[par][ft][:, :], ALU.mult)
                yield
                S.tt(Yn[ft][:, :], Yn[ft][:, :], Dl[ft][:, :], ALU.mult)
                yield
                S.ts(Yn[ft][:, :], Yn[ft][:, :], PRM[:, 7, ft:ft + 1], PRM[:, 8, ft:ft + 1], ALU.mult, ALU.add)
                yield
                S.tt(Yn[ft][:, :], Yn[ft][:, :], T2[ft][:, :], ALU.add)
                pg_ = RA.next()
                S.mm(pg_[:, 0:128], gup[:, fs], SGt[:, :])
                S.tt(OA[ft][:, :], Yn[ft][:, :], pg_[:, 0:128], ALU.mult)
                S.dma(mix[ft * 128:(ft + 1) * 128, c0:c0 + 128], OA[ft][:, :])
                yield

            S.act(SGt[:, :], sh[13][0:96, :], AF.Sigmoid)
            yield from rr([epi_chain(ft) for ft in range(4)])

        def run_both(a, b):
            alive = [g for g in (a, b) if g is not None]
            while alive:
                for g in list(alive):
                    try:
                        next(g)
                    except StopIteration:
                        alive.remove(g)

        for d in range(2):
            iLs, iLi, iLt = (0, 1, 2) if d == 0 else (2, 3, 0)
            for h in range(2):
                S.copy(MS2[:, h, :], masks[:, iLs, :])
                S.copy(MI2[:, h, :], masks[:, iLi, :])
                S.copy(MT2[:, h, :], masks[:, iLt, :])
            for ft in range(4):
                S.memset(ST[0][ft][:, :], 0.0)
            order = list(range(NCH)) if d == 0 else list(range(NCH - 1, -1, -1))
            gm = None
            for n, ci in enumerate(order):
                run_both(prep(d, ci, n % 2), gm)
                gm = mats(d, ci, n % 2, n)
            run_both(gm, None)
            S.barrier()
        C.close()

    PH = {"proj": phase_proj, "setup": phase_setup, "attn": phase_attn, "wout": phase_wout, "ffn": phase_ffn,
          "rwkv": phase_rwkv}
    if plan is None:
        plan = [("setup", 0, 0)]
        for s in range(NSEQ):
            for l in range(DEPTH):
                plan += [("proj", l, s), ("rwkv", l, s), ("attn", l, s), ("wout", l, s), ("ffn", l, s)]
    for (name, l, s) in plan:
        try:
            PH[name](l, s)
        except StopBuild:
            print('STOPPED')
    S.barrier()
    return nc


def _t5_bucket_np(rel):
    nb = 16
    max_exact = 8
    ret = np.where(rel > 0, nb, 0)
    n = np.abs(rel)
    nf = np.maximum(n, 1).astype(np.float32)
    large = max_exact + (np.log(nf / np.float32(max_exact)) / np.float32(math.log(128 / max_exact))
                         * np.float32(nb - max_exact)).astype(np.int32)
    large = np.minimum(large, nb - 1)
    return ret + np.where(n < max_exact, n, large)


def _consts():
    ident = np.eye(128, dtype=np.float32)
    a = np.arange(128)[:, None]
    b = np.arange(128)[None, :]
    masks = np.stack([(a < b), (a <= b), (a > b), (a >= b)], 1).astype(np.float32)
    blk = np.zeros((128, 128), np.float32)
    blk[:64, :64] = 1
    blk[64:, 64:] = 1
    delta = 639 - np.arange(NB_LEN + 1)
    bk = _t5_bucket_np(delta.astype(np.int32))
    oh = np.zeros((32, NB_LEN + 1), np.float32)
    oh[bk, np.arange(NB_LEN + 1)] = 1
    return dict(c_ident=ident, c_masks=np.ascontiguousarray(masks), c_blk=blk, c_onehot=oh)


_NC_CACHE = {}


def kernel(**inputs):
    xp = np.asarray(inputs["x_prompt"], np.float32)
    xs = np.asarray(inputs["x_sample"], np.float32)
    T = xp.shape[1]
    allx = [xp[i] for i in range(xp.shape[0])] + [xs[i] for i in range(xs.shape[0])]
    n = len(allx)
    NC = 8
    slots = [(c, 8 + (c % 4)) for c in range(NC)]
    key = (T, 2)
    if key not in _NC_CACHE:
        _NC_CACHE[key] = build(T, 2)
    nc = _NC_CACHE[key]
    base = {k: np.ascontiguousarray(np.asarray(v, np.float32)) for k, v in inputs.items()
            if k not in ("x_prompt", "x_sample")}
    base["r_k"] = base["r_k"].reshape(2, AW)
    base.update(_consts())
    in_maps = []
    for c in range(NC):
        m = dict(base)
        m["x"] = np.ascontiguousarray(np.stack([allx[slots[c][0]], allx[slots[c][1]]], 0))
        in_maps.append(m)
    res = run_bass_kernel_spmd(nc, in_maps, core_ids=list(range(NC)))
    outs = [None] * n
    for c in range(NC):
        y = res.results[c]["y"]
        outs[slots[c][0]] = y[0]
        if c < 4:
            outs[slots[c][1]] = y[1]
    yp = np.stack(outs[:xp.shape[0]], 0).astype(np.float32)
    ys = np.stack(outs[xp.shape[0]:], 0).astype(np.float32)
    return (yp, ys)
```
